# Optimizing a Trainium2 kernel written in Bass

```python
import math
import jax
import jax.numpy as jnp
from jax import lax
import numpy as np

D_MODEL = 2048
BATCH = 2
SEQ = 4096
DEPTH = 4
DEC_BATCH = 8
DEC_SEQ = 8
PAST_LEN = 16384
PAGE_SIZE = 128

ATT_HEADS = 8
ATT_HEAD_DIM = 128
ATT_WIDTH = ATT_HEADS * ATT_HEAD_DIM
DILATED_PATTERNS = ((128, 1), (512, 4), (2048, 16))
ATT_WINDOW_MAX = 2048
ATT_BLOCK = 128
ATT_SCALE = ATT_HEAD_DIM ** -0.5
N_REL_BUCKETS = 32
REL_MAX_DIST = 2048
SSM_WIDTH = 512
SSM_GROUP = 16
SSM_GROUPS = SSM_WIDTH // SSM_GROUP
SSM_STATE = 64
SGU_WIDTH = 512
SGU_HEADS = 4
SGU_HEAD_DIM = SGU_WIDTH // SGU_HEADS
SGU_CHUNK = 128
MIX_WIDTH = ATT_WIDTH + SSM_WIDTH + SGU_WIDTH
IN_COLS = 3 * ATT_WIDTH + SSM_WIDTH + 2 * SGU_WIDTH
FFN_HIDDEN = -((-8 * D_MODEL) // (3 * 256)) * 256
EPS = 1e-6
NEG_INF = -1e30

kernel_name = 'hybrid_dilated_s5_gmlp_decoder_step'


def rmsnorm(x, g):
    xf = x.astype(jnp.float32)
    y = xf * lax.rsqrt(jnp.mean(xf * xf, axis=-1, keepdims=True) + EPS)
    return (y * g.astype(jnp.float32)).astype(x.dtype)


def layernorm_gain(x, g):
    xf = x.astype(jnp.float32)
    xc = xf - jnp.mean(xf, axis=-1, keepdims=True)
    y = xc * lax.rsqrt(jnp.mean(xc * xc, axis=-1, keepdims=True) + EPS)
    return (y * g.astype(jnp.float32)).astype(x.dtype)


def t5_bucket(dist):
    max_exact = N_REL_BUCKETS // 2
    df = jnp.maximum(dist, 1).astype(jnp.float32)
    large = max_exact + (jnp.log(df / max_exact) / math.log(REL_MAX_DIST / max_exact)
                         * (N_REL_BUCKETS - max_exact)).astype(jnp.int32)
    large = jnp.minimum(large, N_REL_BUCKETS - 1)
    return jnp.where(dist < max_exact, dist, large)


def strided_bias(rel_bias, dilation, n_steps):
    dist = jnp.arange(n_steps + 1, dtype=jnp.int32) * dilation
    return rel_bias[t5_bucket(dist)].astype(jnp.float32)


def mix_by_denominators(outs, lses):
    wts = jax.nn.softmax(jnp.stack(lses, axis=0), axis=0)
    return sum(wts[i][..., None] * outs[i].astype(jnp.float32) for i in range(len(outs)))


def dilated_attn_prompt(q, k, v, rel_bias):
    N, L, H, E = q.shape
    QB = ATT_BLOCK
    outs, lses = [], []
    for window, d in DILATED_PATTERNS:
        nk = window // d
        Ls = L // d
        Lp = -(-Ls // QB) * QB
        nb = Lp // QB

        def sub(t):
            t = t.reshape(N, Ls, d, H, E)
            return jnp.pad(t, ((0, 0), (0, Lp - Ls), (0, 0), (0, 0), (0, 0)))

        def band(t):
            t = jnp.pad(t, ((0, 0), (QB, 0), (0, 0), (0, 0), (0, 0))).reshape(N, nb + 1, QB, d, H, E)
            return jnp.concatenate([t[:, :-1], t[:, 1:]], axis=2)

        qb = sub(q).reshape(N, nb, QB, d, H, E)
        kb, vb = band(sub(k)), band(sub(v))
        s = jnp.einsum('nbqrhe,nbkrhe->nbrhqk', qb, kb).astype(jnp.float32) * ATT_SCALE
        qi = jnp.arange(QB)[:, None]
        kj = jnp.arange(2 * QB)[None, :]
        delta = QB + qi - kj
        band_ok = (delta >= 0) & (delta <= nk)
        blk = jnp.arange(nb)[:, None, None]
        valid = band_ok[None] & ((blk > 0) | (kj[None] >= QB))
        bias = strided_bias(rel_bias, d, nk)[jnp.clip(delta, 0, nk)]
        s = s + jnp.transpose(bias, (2, 0, 1))[None, None, None]
        s = jnp.where(valid[None, :, None, None], s, NEG_INF)
        lse = jax.nn.logsumexp(s, axis=-1)
        p = jnp.exp(s - lse[..., None])
        o = jnp.einsum('nbrhqk,nbkrhe->nbqrhe', p.astype(vb.dtype), vb)
        o = o.reshape(N, Lp, d, H, E)[:, :Ls].reshape(N, L, H, E)
        lse = jnp.transpose(lse, (0, 1, 4, 2, 3)).reshape(N, Lp, d, H)[:, :Ls].reshape(N, L, H)
        outs.append(o)
        lses.append(lse)
    return mix_by_denominators(outs, lses)


def dilated_attn_sample(q, k_all, v_all, rel_bias):
    N, T, H, E = q.shape
    W = k_all.shape[1] - T
    outs, lses = [], []
    for window, d in DILATED_PATTERNS:
        nk = window // d
        j = jnp.arange(nk + 1)
        idx = W + jnp.arange(T)[:, None] - j[None, :] * d
        valid = idx >= 0
        idxc = jnp.maximum(idx, 0)
        kg = k_all[:, idxc]
        vg = v_all[:, idxc]
        s = jnp.einsum('nthe,ntjhe->nthj', q, kg).astype(jnp.float32) * ATT_SCALE
        s = s + strided_bias(rel_bias, d, nk).T[None, None]
        s = jnp.where(valid[None, :, None, :], s, NEG_INF)
        lse = jax.nn.logsumexp(s, axis=-1)
        p = jnp.exp(s - lse[..., None])
        outs.append(jnp.einsum('nthj,ntjhe->nthe', p.astype(vg.dtype), vg))
        lses.append(lse)
    return mix_by_denominators(outs, lses)


def s5_scan(u, h0_re, h0_im, lam_re, lam_im, log_step, b_re, b_im, c_re, c_im, d_skip):
    f32 = jnp.float32
    u = u.astype(f32)
    lam_re = lam_re.astype(f32)
    lam_im = lam_im.astype(f32)
    step = jnp.exp(log_step.astype(f32))[:, None]
    mag = jnp.exp(lam_re * step)
    a_re = mag * jnp.cos(lam_im * step)
    a_im = mag * jnp.sin(lam_im * step)
    den = lam_re * lam_re + lam_im * lam_im
    coef_re = ((a_re - 1.0) * lam_re + a_im * lam_im) / den
    coef_im = (a_im * lam_re - (a_re - 1.0) * lam_im) / den
    b_re = b_re.astype(f32)
    b_im = b_im.astype(f32)
    bb_re = coef_re[..., None] * b_re - coef_im[..., None] * b_im
    bb_im = coef_re[..., None] * b_im + coef_im[..., None] * b_re
    x_re = jnp.einsum('nlgc,gpc->nlgp', u, bb_re)
    x_im = jnp.einsum('nlgc,gpc->nlgp', u, bb_im)
    h0_re = h0_re.astype(f32)
    h0_im = h0_im.astype(f32)
    x_re = x_re.at[:, 0].add(a_re * h0_re - a_im * h0_im)
    x_im = x_im.at[:, 0].add(a_re * h0_im + a_im * h0_re)
    ar = jnp.broadcast_to(a_re, x_re.shape)
    ai = jnp.broadcast_to(a_im, x_im.shape)

    def combine(e1, e2):
        a1r, a1i, b1r, b1i = e1
        a2r, a2i, b2r, b2i = e2
        return (a1r * a2r - a1i * a2i, a1r * a2i + a1i * a2r,
                a2r * b1r - a2i * b1i + b2r, a2r * b1i + a2i * b1r + b2i)

    _, _, h_re, h_im = lax.associative_scan(combine, (ar, ai, x_re, x_im), axis=1)
    y = (jnp.einsum('gcp,nlgp->nlgc', c_re.astype(f32), h_re)
         - jnp.einsum('gcp,nlgp->nlgc', c_im.astype(f32), h_im))
    y = y + d_skip.astype(f32).reshape(SSM_GROUPS, SSM_GROUP) * u
    return y, h_re[:, -1], h_im[:, -1]


def chunk_sgu(u, v, w_sp, b_sp):
    N, L = u.shape[:2]
    Lp = -(-L // SGU_CHUNK) * SGU_CHUNK
    vp = jnp.pad(v, ((0, 0), (0, Lp - L), (0, 0), (0, 0)))
    vc = vp.reshape(N, Lp // SGU_CHUNK, SGU_CHUNK, SGU_HEADS, SGU_HEAD_DIM)
    causal = jnp.tril(jnp.ones((SGU_CHUNK, SGU_CHUNK), w_sp.dtype))
    mixed = jnp.einsum('hts,ncshe->ncthe', w_sp * causal, vc) + b_sp.T[None, None, :, :, None]
    mixed = mixed.reshape(N, Lp, SGU_HEADS, SGU_HEAD_DIM)[:, :L]
    return u * mixed


def trunk_layer(x, p, rel_bias, cache_k=None, cache_v=None, h_re=None, h_im=None):
    N, L, _ = x.shape
    h = rmsnorm(x, p['g_pre_mix'])
    z = h @ p['w_in']
    A = ATT_WIDTH
    q = z[..., :A].reshape(N, L, ATT_HEADS, ATT_HEAD_DIM)
    k = z[..., A:2 * A].reshape(N, L, ATT_HEADS, ATT_HEAD_DIM)
    v = z[..., 2 * A:3 * A].reshape(N, L, ATT_HEADS, ATT_HEAD_DIM)
    o0 = 3 * A
    u_ssm = z[..., o0:o0 + SSM_WIDTH]
    u_sgu = z[..., o0 + SSM_WIDTH:o0 + SSM_WIDTH + SGU_WIDTH]
    v_sgu = z[..., o0 + SSM_WIDTH + SGU_WIDTH:]
    if cache_k is None:
        o_att = dilated_attn_prompt(q, k, v, rel_bias)
        n_keep = min(ATT_WINDOW_MAX, L)
        new_k, new_v = k[:, L - n_keep:], v[:, L - n_keep:]
        h_re = jnp.zeros((N, SSM_GROUPS, SSM_STATE), jnp.float32)
        h_im = jnp.zeros((N, SSM_GROUPS, SSM_STATE), jnp.float32)
    else:
        k_all = jnp.concatenate([cache_k.astype(k.dtype), k], axis=1)
        v_all = jnp.concatenate([cache_v.astype(v.dtype), v], axis=1)
        o_att = dilated_attn_sample(q, k_all, v_all, rel_bias)
        new_k, new_v = k, v
    o_att = o_att.reshape(N, L, ATT_WIDTH).astype(x.dtype)
    y_ssm, hr, hi = s5_scan(u_ssm.reshape(N, L, SSM_GROUPS, SSM_GROUP), h_re, h_im,
                            p['ssm_lam_re'], p['ssm_lam_im'], p['ssm_log_step'],
                            p['ssm_b_re'], p['ssm_b_im'], p['ssm_c_re'], p['ssm_c_im'], p['ssm_d'])
    y_ssm = y_ssm.reshape(N, L, SSM_WIDTH)
    o_ssm = y_ssm * jax.nn.sigmoid(y_ssm @ p['ssm_w_glu'].astype(jnp.float32)
                                   + p['ssm_b_glu'].astype(jnp.float32))
    o_ssm = o_ssm.astype(x.dtype)
    gu = jax.nn.gelu(u_sgu)
    gv = layernorm_gain(jax.nn.gelu(v_sgu), p['sgu_g'])
    o_sgu = chunk_sgu(gu.reshape(N, L, SGU_HEADS, SGU_HEAD_DIM),
                      gv.reshape(N, L, SGU_HEADS, SGU_HEAD_DIM),
                      p['sgu_w'], p['sgu_b']).reshape(N, L, SGU_WIDTH)
    g = p['g_mix_out']
    mixed = jnp.concatenate([rmsnorm(o_att, g[:A]),
                             rmsnorm(o_ssm, g[A:A + SSM_WIDTH]),
                             rmsnorm(o_sgu, g[A + SSM_WIDTH:])], axis=-1)
    x = x + rmsnorm(mixed @ p['w_out'], p['g_post_mix'])
    h = rmsnorm(x, p['g_pre_ffn'])
    f = (jax.nn.silu(h @ p['w_gate']) * (h @ p['w_up'])) @ p['w_down']
    x = x + rmsnorm(f, p['g_post_ffn'])
    return x, (new_k, new_v, hr, hi, gv)


def setup_inputs(seed: int = 0) -> dict:
    key = jax.random.key(seed)
    k = jax.random.split(key, 30)
    f32 = jnp.float32

    def nrm(kk, shape, scale):
        return jax.random.normal(kk, shape, f32) * scale

    def gain(kk, shape):
        return 1.0 + 0.02 * jax.random.normal(kk, shape, f32)

    w_buf = min(ATT_WINDOW_MAX, PAST_LEN)
    G, P, CG = SSM_GROUPS, SSM_STATE, SSM_GROUP
    n_idx = jnp.arange(P, dtype=f32)[None, None, :]
    return {
        'x_prompt': nrm(k[0], (BATCH, SEQ, D_MODEL), 1.0),
        'x_sample': nrm(k[1], (DEC_BATCH, DEC_SEQ, D_MODEL), 1.0),
        'cache_attn_k': nrm(k[2], (DEPTH, DEC_BATCH, w_buf, ATT_HEADS, ATT_HEAD_DIM), 1.0),
        'cache_attn_v': nrm(k[3], (DEPTH, DEC_BATCH, w_buf, ATT_HEADS, ATT_HEAD_DIM), 1.0),
        'state_ssm_re': nrm(k[4], (DEPTH, DEC_BATCH, G, P), 0.5),
        'state_ssm_im': nrm(k[5], (DEPTH, DEC_BATCH, G, P), 0.5),
        'rel_bias': nrm(k[6], (N_REL_BUCKETS, ATT_HEADS), 0.5),
        'w_in': nrm(k[7], (DEPTH, D_MODEL, IN_COLS), D_MODEL ** -0.5),
        'w_out': nrm(k[8], (DEPTH, MIX_WIDTH, D_MODEL), MIX_WIDTH ** -0.5),
        'g_pre_mix': gain(k[9], (DEPTH, D_MODEL)),
        'g_post_mix': gain(k[10], (DEPTH, D_MODEL)),
        'g_mix_out': gain(k[11], (DEPTH, MIX_WIDTH)),
        'ssm_lam_re': -0.5 + nrm(k[12], (DEPTH, G, P), 0.01),
        'ssm_lam_im': math.pi * n_idx + nrm(k[13], (DEPTH, G, P), 0.01),
        'ssm_log_step': jax.random.uniform(k[14], (DEPTH, G), f32, math.log(1e-3), math.log(1e-1)),
        'ssm_b_re': nrm(k[15], (DEPTH, G, P, CG), (2 * CG) ** -0.5),
        'ssm_b_im': nrm(k[16], (DEPTH, G, P, CG), (2 * CG) ** -0.5),
        'ssm_c_re': nrm(k[17], (DEPTH, G, CG, P), P ** -0.5),
        'ssm_c_im': nrm(k[18], (DEPTH, G, CG, P), P ** -0.5),
        'ssm_d': nrm(k[19], (DEPTH, SSM_WIDTH), 1.0),
        'ssm_w_glu': nrm(k[20], (DEPTH, SSM_WIDTH, SSM_WIDTH), SSM_WIDTH ** -0.5),
        'ssm_b_glu': nrm(k[21], (DEPTH, SSM_WIDTH), 0.02),
        'sgu_g': gain(k[22], (DEPTH, SGU_WIDTH)),
        'sgu_w': nrm(k[23], (DEPTH, SGU_HEADS, SGU_CHUNK, SGU_CHUNK), SGU_CHUNK ** -0.5),
        'sgu_b': gain(k[24], (DEPTH, SGU_HEADS, SGU_CHUNK)),
        'g_pre_ffn': gain(k[25], (DEPTH, D_MODEL)),
        'g_post_ffn': gain(k[26], (DEPTH, D_MODEL)),
        'w_gate': nrm(k[27], (DEPTH, D_MODEL, FFN_HIDDEN), D_MODEL ** -0.5),
        'w_up': nrm(k[28], (DEPTH, D_MODEL, FFN_HIDDEN), D_MODEL ** -0.5),
        'w_down': nrm(k[29], (DEPTH, FFN_HIDDEN, D_MODEL), FFN_HIDDEN ** -0.5),
    }


def reference(x_prompt, x_sample, cache_attn_k, cache_attn_v, state_ssm_re, state_ssm_im,
              rel_bias, w_in, w_out, g_pre_mix, g_post_mix, g_mix_out,
              ssm_lam_re, ssm_lam_im, ssm_log_step, ssm_b_re, ssm_b_im, ssm_c_re, ssm_c_im,
              ssm_d, ssm_w_glu, ssm_b_glu, sgu_g, sgu_w, sgu_b,
              g_pre_ffn, g_post_ffn, w_gate, w_up, w_down):
    xp, xs = x_prompt, x_sample
    kp_l, vp_l, rp_l, ip_l = [], [], [], []
    ks_l, vs_l, rs_l, is_l, us_l = [], [], [], [], []
    for l in range(DEPTH):
        p = {'w_in': w_in[l], 'w_out': w_out[l], 'g_pre_mix': g_pre_mix[l],
             'g_post_mix': g_post_mix[l], 'g_mix_out': g_mix_out[l],
             'ssm_lam_re': ssm_lam_re[l], 'ssm_lam_im': ssm_lam_im[l],
             'ssm_log_step': ssm_log_step[l], 'ssm_b_re': ssm_b_re[l], 'ssm_b_im': ssm_b_im[l],
             'ssm_c_re': ssm_c_re[l], 'ssm_c_im': ssm_c_im[l], 'ssm_d': ssm_d[l],
             'ssm_w_glu': ssm_w_glu[l], 'ssm_b_glu': ssm_b_glu[l],
             'sgu_g': sgu_g[l], 'sgu_w': sgu_w[l], 'sgu_b': sgu_b[l],
             'g_pre_ffn': g_pre_ffn[l], 'g_post_ffn': g_post_ffn[l],
             'w_gate': w_gate[l], 'w_up': w_up[l], 'w_down': w_down[l]}
        xp, (nk, nv, hr, hi, _) = trunk_layer(xp, p, rel_bias)
        kp_l.append(nk)
        vp_l.append(nv)
        rp_l.append(hr)
        ip_l.append(hi)
        xs, (nk, nv, hr, hi, sv) = trunk_layer(xs, p, rel_bias, cache_attn_k[l], cache_attn_v[l],
                                               state_ssm_re[l], state_ssm_im[l])
        ks_l.append(nk)
        vs_l.append(nv)
        rs_l.append(hr)
        is_l.append(hi)
        us_l.append(sv)
    return (xp, xs, jnp.stack(kp_l), jnp.stack(vp_l), jnp.stack(rp_l), jnp.stack(ip_l),
            jnp.stack(ks_l), jnp.stack(vs_l), jnp.stack(rs_l), jnp.stack(is_l), jnp.stack(us_l))
```

```python
import contextlib, math
import numpy as np
import concourse.bass as bass
import concourse.mybir as mybir
from concourse.bass_utils import run_bass_kernel_spmd

F32 = mybir.dt.float32
BF16 = mybir.dt.bfloat16
I32 = mybir.dt.int32
ALU = mybir.AluOpType
AF = mybir.ActivationFunctionType
AX = mybir.AxisListType

ENGS = ['tensor', 'vector', 'scalar', 'gpsimd', 'sync']
DMA_R = 8
DEBUG_NAMES = None
import os as _os
KSTOP = int(_os.environ.get('KSTOP', '9'))
KDBG = int(_os.environ.get('KDBG', '0'))
KSUB = int(_os.environ.get('KSUB', '99'))
KVAR = int(_os.environ.get('KVAR', '0'))


class _Stop(Exception):
    pass


class Buf:
    __slots__ = ('name', 'w', 'r', 'excl')

    def __init__(self, name='', excl=False):
        self.name = name
        self.w = None
        self.r = []
        self.excl = excl


class Ins:
    __slots__ = ('eng', 'fn', 'deps', 'kind', 'dma_no', 'marked', 'cnt')

    def __init__(self, eng, fn, kind):
        self.eng = eng
        self.fn = fn
        self.kind = kind
        self.deps = []
        self.marked = False
        self.cnt = 0
        self.dma_no = -1


class Prog:
    def __init__(self, nc):
        self.nc = nc
        self.q = {e: [] for e in ENGS}
        self.ndma = {e: 0 for e in ENGS}
        self.same_engine_sync = {'vector', 'scalar', 'gpsimd'}
        self.fence_deps = {e: [] for e in ENGS}
        self.since_fence_dma = []

    def emit(self, eng, fn, reads=(), writes=(), kind='c'):
        ins = Ins(eng, fn, kind)
        if kind == 'd':
            ins.dma_no = self.ndma[eng]
            self.ndma[eng] += 1
            self.since_fence_dma.append(ins)
        deps = list(self.fence_deps[eng])
        self.fence_deps[eng] = []
        for b in reads:
            if b.w is not None:
                deps.append(b.w)
            if b.excl:
                deps.extend(x for x in b.r if x.eng != eng)
        for b in writes:
            if b.w is not None:
                deps.append(b.w)
            deps.extend(b.r)
        for d in deps:
            if d is ins:
                continue
            if d.eng == eng and d.kind == 'c' and kind == 'c' and eng not in self.same_engine_sync:
                continue
            ins.deps.append(d)
        for b in reads:
            b.r.append(ins)
        for b in writes:
            b.w = ins
            b.r = []
        self.q[eng].append(ins)
        return ins

    def fence(self):
        tails = []
        for e in ENGS:
            for ins in reversed(self.q[e]):
                if ins.kind == 'c':
                    tails.append(ins)
                    break
        tails.extend(self.since_fence_dma)
        self.since_fence_dma = []
        for e in ENGS:
            self.fence_deps[e] = self.fence_deps[e] + tails

    def dma(self, eng, out, in_, reads=(), writes=(), **kw):
        return self.emit(eng, lambda e: e.dma_start(out=out, in_=in_, **kw), reads, writes, kind='d')

    def finalize(self):
        nc = self.nc
        for e in ENGS:
            for ins in self.q[e]:
                for d in ins.deps:
                    d.marked = True
        for e in ENGS:
            c = 0
            for ins in self.q[e]:
                if ins.kind == 'c' and ins.marked:
                    c += 1
                ins.cnt = c
        with contextlib.ExitStack() as es:
            esem = {e: es.enter_context(nc.semaphore('s_' + e)) for e in ENGS}
            dsem = {e: [es.enter_context(nc.semaphore('d_%s_%d' % (e, i))) for i in range(DMA_R)]
                    for e in ENGS if self.ndma[e] > 0}
            block = es.enter_context(nc.Block())
            prog = self

            def run_engine(ename, eng):
                seen_e = {e2: 0 for e2 in ENGS}
                seen_d = {}
                for ins in prog.q[ename]:
                    need_e = {}
                    need_d = {}
                    for d in ins.deps:
                        if d.kind == 'c':
                            if d.cnt > seen_e[d.eng] and d.cnt > need_e.get(d.eng, 0):
                                need_e[d.eng] = d.cnt
                        else:
                            key = (d.eng, d.dma_no % DMA_R)
                            val = 16 * (d.dma_no // DMA_R + 1)
                            if val > seen_d.get(key, 0) and val > need_d.get(key, 0):
                                need_d[key] = val
                    if ins.kind == 'd' and ins.dma_no >= DMA_R:
                        key = (ename, ins.dma_no % DMA_R)
                        val = 16 * (ins.dma_no // DMA_R)
                        if val > seen_d.get(key, 0) and val > need_d.get(key, 0):
                            need_d[key] = val
                    for e2, v in need_e.items():
                        eng.wait_ge(esem[e2], v)
                        seen_e[e2] = v
                    for key, v in need_d.items():
                        eng.wait_ge(dsem[key[0]][key[1]], v)
                        seen_d[key] = v
                    r = ins.fn(eng)
                    if DEBUG_NAMES is not None:
                        try:
                            DEBUG_NAMES.append((r.ins.name if hasattr(r, 'ins') else getattr(r, 'name', '?'), r.concise()[:400]))
                        except Exception as ex:
                            DEBUG_NAMES.append(str(ex))
                    if ins.kind == 'c':
                        if ins.marked:
                            r.then_inc(esem[ename], 1)
                    else:
                        r.then_inc(dsem[ename][ins.dma_no % DMA_R], 16)
                if prog.ndma[ename] > 0:
                    n = prog.ndma[ename]
                    for k in range(DMA_R):
                        cntk = len(range(k, n, DMA_R))
                        if cntk > 0 and 16 * cntk > seen_d.get((ename, k), 0):
                            eng.wait_ge(dsem[ename][k], 16 * cntk)

            @block.tensor
            def _(eng):
                run_engine('tensor', eng)

            @block.vector
            def _(eng):
                run_engine('vector', eng)

            @block.scalar
            def _(eng):
                run_engine('scalar', eng)

            @block.gpsimd
            def _(eng):
                run_engine('gpsimd', eng)

            @block.sync
            def _(eng):
                run_engine('sync', eng)


D_MODEL = 2048
NKC = 16
ATT_H = 8
IN_COLS = 4608
FFN_H = 5632
NHC = 44
EPS = 1e-6
ATT_SCALE = 128 ** -0.5
TOK = 1024
TW = 1032
ETW = 17 * 128
TVL = ETW + 127
TWO_PI = 2.0 * math.pi
class KB:
    def __init__(self, DEPTH, NCH):
        self.DEPTH = DEPTH
        self.NCH = NCH
        self.L = NCH * TOK
        self.NKEEP = min(2048, self.L)
        nc = bass.Bass("TRN2", target_bir_lowering=False)
        self.nc = nc
        self.P = Prog(nc)
        self.es = contextlib.ExitStack()
        self.rr = 0
        D = DEPTH
        L = self.L

        def din(name, shape, dt=F32):
            return nc.dram_tensor(name, list(shape), dt, kind="ExternalInput").ap()

        def dout(name, shape, dt=F32):
            return nc.dram_tensor(name, list(shape), dt, kind="ExternalOutput").ap()

        self.i = dict(
            xp=din('xp', [L, 2048]), xs=din('xs', [8, 2048]),
            ck=din('ck', [D, 2048, 1024]), cv=din('cv', [D, 2048, 1024]),
            sre=din('sre', [D, 128, 16]), sim=din('sim', [D, 128, 16]),
            w_in=din('w_in', [D, 2048, IN_COLS]), w_out=din('w_out', [D, 2048, 2048]),
            w_gate=din('w_gate', [D, 2048, FFN_H]), w_up=din('w_up', [D, 2048, FFN_H]),
            w_down=din('w_down', [D, FFN_H, 2048]), wglu=din('wglu', [D, 512, 512]),
            gcols=din('gcols', [D, 128, 5 * 16]), ssmcols=din('ssmcols', [D, 128, 3 * 16]),
            dcols=din('dcols', [D, 128, 8]),
            bbd_re=din('bbd_re', [D, 16, 128, 128]), bbd_im=din('bbd_im', [D, 16, 128, 128]),
            cbd_re=din('cbd_re', [D, 16, 128, 128]), cbd_im=din('cbd_im', [D, 16, 128, 128]),
            sgug=din('sgug', [D, 128, 512]), sguwT=din('sguwT', [D, 128, 512]), sgub=din('sgub', [D, 1, 512]),
            ident=din('ident', [128, 128]), jflip=din('jflip', [128, 128]), trilT=din('trilT', [128, 512]),
            iota=din('iota', [128, 1025]), c33=din('c33', [33, TVL]), relb=din('relb', [33, 8]),
        )
        self.o = dict(
            yp=dout('yp', [L, 2048]), ys=dout('ys', [8, 2048]),
            nk=dout('nk', [D, self.NKEEP, 1024]), nv=dout('nv', [D, self.NKEEP, 1024]),
            nsre=dout('nsre', [D, 128, 16]), nsim=dout('nsim', [D, 128, 16]),
            nks=dout('nks', [D, 8, 1024]), nvs=dout('nvs', [D, 8, 1024]),
            nssre=dout('nssre', [D, 128, 16]), nssim=dout('nssim', [D, 128, 16]),
            nsgu=dout('nsgu', [D, 8, 512]),
        )
        if KDBG:
            self.o['dbg'] = dout('dbg', [128, NKC * TW])
        self.xscr = nc.dram_tensor('xscr', [2048, L], F32).ap()
        self.ktscr = nc.dram_tensor('ktscr', [1024, L], BF16).ap()
        self.vscr = nc.dram_tensor('vscr', [L, 1024], BF16).ap()
        self.etv = nc.dram_tensor('etv', [8, TVL], F32)
        self.ett = nc.dram_tensor('ett', [8, 128, ETW], BF16).ap()
        self.b_xscr = [Buf() for _ in range(NCH)]
        self.b_kt = [Buf() for _ in range(NCH)]
        self.b_v = [Buf() for _ in range(NCH)]
        self.b_out = Buf()

    def sb(self, name, shape, dt, es=None):
        self.uid = getattr(self, 'uid', 0) + 1
        return (es or self.es).enter_context(self.nc.sbuf_tensor('%s_u%d' % (name, self.uid), list(shape), dt))

    def op(self, eng, name, reads=(), writes=(), **kw):
        return self.P.emit(eng, lambda e: getattr(e, name)(**kw), reads, writes)

    def mm(self, out, lhsT, rhs, start, stop, reads=(), writes=(), skip=False):
        if skip:
            return self.P.emit('tensor', lambda e: e.matmul(out, lhsT=lhsT, rhs=rhs, start=start, stop=stop, skip_group_check=True), reads, writes)
        return self.P.emit('tensor', lambda e: e.matmul(out, lhsT=lhsT, rhs=rhs, start=start, stop=stop), reads, writes)

    def act(self, out, in_, func, reads=(), writes=(), **kw):
        return self.P.emit('scalar', lambda e: e.activation(out=out, in_=in_, func=func, **kw), reads, writes)

    def ps(self):
        k = self.rr % getattr(self, 'ps_n', 8)
        self.rr += 1
        return self.psb[k]

    def alt(self):
        self.altc = getattr(self, 'altc', 0) + 1
        return 'vector' if self.altc % 2 else 'gpsimd'

    @contextlib.contextmanager
    def phase(self):
        es = contextlib.ExitStack()
        with es:
            yield es
        self.P.fence()

    def build(self):
        nc, P = self.nc, self.P
        with self.es:
            self.psb = []
            for k in range(8):
                t = self.es.enter_context(nc.psum_tensor('ps%d' % k, [128, 512], F32))
                self.psb.append((t, Buf('ps%d' % k, excl=True)))
            self.xT = self.sb('xT', [128, NKC, TW], F32)
            self.bx = [Buf('x0'), Buf('x1'), Buf('xs')]
            self.xS = self.sb('xS', [128, NKC, 8], F32)
            self.ident = self.sb('ident', [128, 128], F32)
            self.ones_bf = self.sb('ones_bf', [128, 128], BF16)
            self.gcol = self.sb('gcol', [128, 80], F32)
            self.b_gcol = Buf('gcol')
            self.Hst = self.sb('Hst', [128, 4, 16], F32)
            self.b_H = Buf('H')
            self.setup()
            try:
                for l in range(self.DEPTH):
                    P.dma('sync', self.gcol[:], self.i['gcols'][l], writes=[self.b_gcol])
                    for j in range(self.NCH):
                        self.chunk_layer(l, j)
            except _Stop:
                pass
            P.finalize()
        return nc

    def groups(self, j):
        g = [(0, 512, 0), (512, 1024, 1)]
        if j == 0:
            g.append((1024, 1032, 2))
        return g

    def setup(self):
        nc, P = self.nc, self.P
        b = Buf()
        P.dma('sync', self.ident[:], self.i['ident'], writes=[b])
        self.op('vector', 'memset', writes=[b], ap=self.ones_bf[:], constant=1.0)
        with self.phase() as es:
            c33 = self.sb('c33', [33, TVL], F32, es)
            relb = self.sb('relb', [33, 8], F32, es)
            jf = self.sb('jf', [128, 128], F32, es)
            tv = self.sb('tv', [8, TVL], F32, es)
            hk = self.sb('hk', [128, ETW], F32, es)
            tt = self.sb('tt', [128, ETW], BF16, es)
            b1, b2, b3, b4, b5, b6, b7 = [Buf() for _ in range(7)]
            P.dma('sync', c33[:], self.i['c33'], writes=[b1])
            P.dma('sync', relb[:], self.i['relb'], writes=[b2])
            P.dma('sync', jf[:], self.i['jflip'], writes=[b3])
            c0 = 0
            while c0 < TVL:
                n = min(512, TVL - c0)
                pt, pb = self.ps()
                self.mm(pt[0:8, 0:n], relb[:, :], c33[:, c0:c0 + n], True, True, reads=[b1, b2], writes=[pb])
                self.act(tv[:, c0:c0 + n], pt[0:8, 0:n], AF.Exp, reads=[pb], writes=[b4])
                c0 += n
            P.dma('sync', self.etv.ap(), tv[:], reads=[b4], writes=[b5])
            for h in range(8):
                src = bass.AP(self.etv, h * TVL, [[1, 128], [1, ETW]])
                P.dma('sync', hk[:], src, reads=[b5], writes=[b6])
                c0 = 0
                while c0 < ETW:
                    n = min(512, ETW - c0)
                    pt, pb = self.ps()
                    self.mm(pt[:, 0:n], jf[:, :], hk[:, c0:c0 + n], True, True, reads=[b3, b6], writes=[pb])
                    self.op('vector', 'tensor_copy', reads=[pb], writes=[b7], out=tt[:, c0:c0 + n], in_=pt[:, 0:n])
                    c0 += n
                P.dma('sync', self.ett[h], tt[:], reads=[b7], writes=[Buf()])

    def rmsnorm(self, es, src_fn, dst_fn, nch, gcol_fn, groups, rbufs, wbufs, tag, in_dt=F32):
        sq = [self.sb('sq%s%d' % (tag, k), [128, 512], BF16, es) for k in range(2)]
        bsq = [Buf(), Buf()]
        rstd = self.sb('rstd' + tag, [128, 512], F32, es)
        brs = Buf()
        for (c0, c1, gi) in groups:
            n = c1 - c0
            pt, pb = self.ps()
            for ch in range(nch):
                k = ch % 2
                self.act(sq[k][:, 0:n], src_fn(ch, c0, c1), AF.Square, reads=rbufs[gi], writes=[bsq[k]])
                self.mm(pt[:, 0:n], self.ones_bf[:, :], sq[k][:, 0:n], ch == 0, ch == nch - 1, reads=[bsq[k]], writes=[pb])
            self.op('vector', 'tensor_scalar', reads=[pb], writes=[brs], out=rstd[:, 0:n], in0=pt[:, 0:n],
                    scalar1=1.0 / (nch * 128), scalar2=EPS, op0=ALU.mult, op1=ALU.add)
            self.act(rstd[:, 0:n], rstd[:, 0:n], AF.Sqrt, reads=[brs], writes=[brs])
            self.op('vector', 'reciprocal', reads=[brs], writes=[brs], out=rstd[:, 0:n], in_=rstd[:, 0:n])
            for ch in range(nch):
                self.op('vector', 'scalar_tensor_tensor', reads=[brs, self.b_gcol] + list(rbufs[gi]), writes=wbufs[gi],
                        out=dst_fn(ch, c0, c1), in0=src_fn(ch, c0, c1), scalar=gcol_fn(ch), in1=rstd[:, 0:n],
                        op0=ALU.mult, op1=ALU.mult)
    def wload(self, wt, wb, src):
        nk = wt.shape[1]
        for kc in range(nk):
            self.P.dma('gpsimd', wt[:, kc, :], src[:, kc, :], writes=[wb])

    def gelu_from_psum(self, es_bufs, pt, pb, rows, n, out_ap, wb_out):
        t1, t2, b1, b2 = es_bufs
        x = pt[0:rows, 0:n]
        self.act(t1[0:rows, 0:n], x, AF.Square, reads=[pb], writes=[b1])
        self.op('vector', 'tensor_scalar', reads=[b1], writes=[b1], out=t1[0:rows, 0:n], in0=t1[0:rows, 0:n],
                scalar1=0.044715, scalar2=1.0, op0=ALU.mult, op1=ALU.add)
        self.op('vector', 'tensor_tensor', reads=[b1, pb], writes=[b2], out=t2[0:rows, 0:n], in0=t1[0:rows, 0:n], in1=x, op=ALU.mult)
        self.act(t1[0:rows, 0:n], t2[0:rows, 0:n], AF.Sigmoid, reads=[b2], writes=[b1], scale=1.5957691216057308)
        self.op('vector', 'tensor_tensor', reads=[b1, pb], writes=wb_out, out=out_ap, in0=t1[0:rows, 0:n], in1=x, op=ALU.mult)

    def chunk_layer(self, l, j):
        nc, P = self.nc, self.P
        base = j * TOK
        groups = self.groups(j)
        samp = (j == 0)
        xT = self.xT
        last = (l == self.DEPTH - 1)
        with self.phase() as es:
            if l == 0:
                xin = [self.sb('xin%d' % k, [128, 2048], F32, es) for k in range(2)]
                bxin = [Buf(), Buf()]
                for tt in range(8):
                    k = tt % 2
                    P.dma('sync', xin[k][:], self.i['xp'][base + tt * 128: base + (tt + 1) * 128, :], writes=[bxin[k]])
                    for c4 in range(4):
                        pt, pb = self.ps()
                        for q in range(4):
                            ch = c4 * 4 + q
                            self.mm(pt[:, q * 128:(q + 1) * 128], xin[k][:, ch * 128:(ch + 1) * 128], self.ident[:, :], True, True, reads=[bxin[k]], writes=[pb])
                        self.op('vector', 'tensor_copy', reads=[pb], writes=[self.bx[tt // 4]],
                                out=xT[:, c4 * 4:(c4 + 1) * 4, tt * 128:(tt + 1) * 128],
                                in_=pt[:, :].rearrange("p (q t) -> p q t", q=4))
                if samp:
                    P.dma('sync', xin[0][0:8, :], self.i['xs'], writes=[bxin[0]])
                    for c4 in range(4):
                        pt, pb = self.ps()
                        for q in range(4):
                            ch = c4 * 4 + q
                            self.mm(pt[:, q * 8:(q + 1) * 8], xin[0][0:8, ch * 128:(ch + 1) * 128], self.ident[0:8, 0:8], True, True, reads=[bxin[0]], writes=[pb])
                        self.op('vector', 'tensor_copy', reads=[pb], writes=[self.bx[2]],
                                out=xT[:, c4 * 4:(c4 + 1) * 4, 1024:1032], in_=pt[:, 0:32].rearrange("p (q t) -> p q t", q=4))
            else:
                for ch in range(NKC):
                    P.dma('sync', xT[:, ch, 0:1024], self.xscr[ch * 128:(ch + 1) * 128, base:base + 1024],
                          reads=[self.b_xscr[j]], writes=[self.bx[0], self.bx[1]])
                if samp:
                    self.op('vector', 'tensor_copy', writes=[self.bx[2]], out=xT[:, :, 1024:1032], in_=self.xS[:, :, :])
        with self.phase() as es_mix:
            hm = self.sb('hm', [128, NKC, TW], BF16, es_mix)
            mixedT = hm
            bmix = {gi: [Buf()] for gi in range(3)}
            with self.phase() as es_u:
                uT = self.sb('uT', [128, 4, TW], BF16, es_u)
                bu = Buf()
                with self.phase() as es_g:
                    guT = self.sb('guT', [128, 4, TW], BF16, es_g)
                    gv = self.sb('gv', [128, 8, 512], BF16, es_g)
                    gvs = self.sb('gvs', [8, 512], BF16, es_g)
                    bgu, bgv = Buf(), Buf()
                    with self.phase() as es_q:
                        qT = self.sb('qT', [128, 8, TW], BF16, es_q)
                        KTs = self.sb('KTs', [128, 8, 8], BF16, es_q)
                        Vs = self.sb('Vs', [8, 1024], BF16, es_q)
                        bq, bkts, bvs = Buf(), Buf(), Buf()
                        if KSTOP >= 1:
                            self.phase_A(l, j, groups, samp, hm, qT, uT, guT, gv, gvs, KTs, Vs, bq, bu, bgu, bgv, bkts, bvs)
                        if KSTOP >= 2:
                            self.phase_att(l, j, samp, qT, KTs, Vs, mixedT, bmix, bq, bkts, bvs)
                    if KSTOP >= 3:
                        self.phase_sgu(l, j, samp, guT, gv, gvs, bgu, bgv, mixedT, bmix)
                if KSTOP >= 4:
                    self.phase_ssm(l, j, groups, samp, uT, bu, mixedT, bmix)
            if KDBG and l == 0 and j == 0:
                self.P.fence()
                self.P.dma('gpsimd', self.o['dbg'], hm[:].rearrange("p a b -> p (a b)"), writes=[self.b_out])
                self.P.fence()
            if KSTOP >= 5:
                self.phase_out(l, j, groups, mixedT, bmix)
        if KSTOP >= 6:
            self.phase_ffn(l, j, groups)
        with self.phase() as es:
            if samp:
                self.op('vector', 'tensor_copy', reads=[self.bx[2]], writes=[Buf()], out=self.xS[:, :, :], in_=xT[:, :, 1024:1032])
            if not last:
                for ch in range(NKC):
                    P.dma('sync', self.xscr[ch * 128:(ch + 1) * 128, base:base + 1024], xT[:, ch, 0:1024],
                          reads=[self.bx[0], self.bx[1]], writes=[self.b_xscr[j]])
            else:
                yo = [self.sb('yo%d' % k, [128, 2048], F32, es) for k in range(2)]
                byo = [Buf(), Buf()]
                for tt in range(8):
                    k = tt % 2
                    for c4 in range(4):
                        pt, pb = self.ps()
                        for q in range(4):
                            ch = c4 * 4 + q
                            self.mm(pt[:, q * 128:(q + 1) * 128], xT[:, ch, tt * 128:(tt + 1) * 128], self.ident[:, :], True, True,
                                    reads=[self.bx[tt // 4]], writes=[pb])
                        self.op('vector', 'tensor_copy', reads=[pb], writes=[byo[k]], out=yo[k][:, c4 * 512:(c4 + 1) * 512], in_=pt[:, :])
                    P.dma('sync', self.o['yp'][base + tt * 128: base + (tt + 1) * 128, :], yo[k][:], reads=[byo[k]], writes=[self.b_out])
                if samp:
                    for c4 in range(4):
                        pt, pb = self.ps()
                        for q in range(4):
                            ch = c4 * 4 + q
                            self.mm(pt[0:8, q * 128:(q + 1) * 128], xT[:, ch, 1024:1032], self.ident[:, :], True, True, reads=[self.bx[2]], writes=[pb])
                        self.op('vector', 'tensor_copy', reads=[pb], writes=[byo[0]], out=yo[0][0:8, c4 * 512:(c4 + 1) * 512], in_=pt[0:8, :])
                    P.dma('sync', self.o['ys'], yo[0][0:8, :], reads=[byo[0]], writes=[self.b_out])

    def phase_A(self, l, j, groups, samp, hT, qT, uT, guT, gv, gvs, KTs, Vs, bq, bu, bgu, bgv, bkts, bvs):
        nc, P = self.nc, self.P
        base = j * TOK
        xT = self.xT
        keep_lo = self.L - self.NKEEP
        with self.phase() as es:
            bh = {gi: [Buf()] for gi in range(3)}
            with self.phase() as es2:
                self.rmsnorm(es2, lambda ch, c0, c1: xT[:, ch, c0:c1], lambda ch, c0, c1: hT[:, ch, c0:c1], NKC,
                             lambda ch: self.gcol[:, ch:ch + 1], groups, {gi: [self.bx[gi]] for gi in range(3)}, bh, 'a')
            allh = [bh[g[2]][0] for g in groups]
            wbuf = [self.sb('wA%d' % k, [128, NKC, 512], BF16, es) for k in range(2)]
            bw = [Buf(), Buf()]
            kst = [self.sb('kst%d' % k, [128, 1024], BF16, es) for k in range(2)]
            bkst = [Buf(), Buf()]
            vst = [self.sb('vst%d' % k, [128, 512], BF16, es) for k in range(2)]
            bvst = [Buf(), Buf()]
            fst = [self.sb('fst%d' % k, [128, 512], F32, es) for k in range(2)]
            bfst = [Buf(), Buf()]
            t1 = self.sb('gt1', [128, 512], F32, es)
            t2 = self.sb('gt2', [128, 512], F32, es)
            gb = (t1, t2, Buf(), Buf())
            lnx = self.sb('lnx', [128, 512], F32, es)
            blnx = Buf()
            lnj = self.sb('lnj', [128, 512], F32, es)
            sm = self.sb('lnsm', [128, 8], F32, es)
            bsm = Buf()
            sgug = self.sb('sgug', [128, 512], F32, es)
            bsg = Buf()
            P.dma('sync', sgug[:], self.i['sgug'][l], writes=[bsg])
            win = self.i['w_in'][l]
            cnt = 0
            for blk in range(9):
                if blk >= KSUB:
                    break
                k = blk % 2
                self.wload(wbuf[k][:], bw[k], win[:, blk * 512:(blk + 1) * 512].rearrange("(kc p) n -> p kc n", p=128))
                fm = blk in (0, 1, 2, 3, 6, 7)
                if fm:
                    for cc in range(4):
                        head = (blk % 2) * 4 + cc
                        for (c0, c1, gi) in groups:
                            n = c1 - c0
                            pt, pb = self.ps()
                            for kc in range(NKC):
                                self.mm(pt[:, 0:n], wbuf[k][:, kc, cc * 128:(cc + 1) * 128], hT[:, kc, c0:c1], kc == 0, kc == NKC - 1,
                                        reads=[bw[k], bh[gi][0]], writes=[pb])
                            if blk in (0, 1):
                                self.act(qT[:, head, c0:c1], pt[:, 0:n], AF.Copy, reads=[pb], writes=[bq])
                            elif blk in (2, 3):
                                if gi < 2:
                                    kk = head % 2
                                    self.act(kst[kk][:, c0:c1], pt[:, 0:n], AF.Copy, reads=[pb], writes=[bkst[kk]])
                                    if gi == 1:
                                        P.dma('sync', self.ktscr[head * 128:(head + 1) * 128, base:base + 1024], kst[kk][:, :],
                                              reads=[bkst[kk]], writes=[self.b_kt[j]])
                                else:
                                    self.act(KTs[:, head, :], pt[:, 0:n], AF.Copy, reads=[pb], writes=[bkts])
                            elif blk == 6:
                                self.act(uT[:, cc, c0:c1], pt[:, 0:n], AF.Copy, reads=[pb], writes=[bu])
                            else:
                                self.gelu_from_psum(gb, pt, pb, 128, n, guT[:, cc, c0:c1], [bgu])
                tokmaj = blk in (4, 5, 8) or (blk in (2, 3))
                if tokmaj:
                    tts = list(range(8)) + ([8] if samp else [])
                    for tt in tts:
                        rows = 128 if tt < 8 else 8
                        tok0 = base + tt * 128
                        if blk in (2, 3):
                            if tt < 8 and tok0 < keep_lo:
                                continue
                        c0 = tt * 128 if tt < 8 else 1024
                        gi = (tt // 4) if tt < 8 else 2
                        pt, pb = self.ps()
                        for kc in range(NKC):
                            self.mm(pt[0:rows, :], hT[:, kc, c0:c0 + rows], wbuf[k][:, kc, :], kc == 0, kc == NKC - 1,
                                    reads=[bw[k], bh[gi][0]], writes=[pb])
                        half = blk % 2
                        if blk in (2, 3, 4, 5):
                            isk = blk in (2, 3)
                            if tt < 8:
                                if not isk and KVAR != 1:
                                    kk = cnt % 2
                                    self.op('vector', 'tensor_copy', reads=[pb], writes=[bvst[kk]], out=vst[kk][:, :], in_=pt[:, :])
                                    P.dma('sync', self.vscr[tok0:tok0 + 128, half * 512:(half + 1) * 512], vst[kk][:, :],
                                          reads=[bvst[kk]], writes=[self.b_v[j]])
                                if tok0 >= keep_lo:
                                    kk = cnt % 2
                                    self.act(fst[kk][:, :], pt[:, :], AF.Copy, reads=[pb], writes=[bfst[kk]])
                                    dst = self.o['nk' if isk else 'nv'][l, tok0 - keep_lo: tok0 - keep_lo + 128, half * 512:(half + 1) * 512]
                                    P.dma('sync', dst, fst[kk][:, :], reads=[bfst[kk]], writes=[self.b_out])
                                cnt += 1
                            else:
                                kk = cnt % 2
                                cnt += 1
                                if not isk and KVAR != 2:
                                    self.op('vector', 'tensor_copy', reads=[pb], writes=[bvs], out=Vs[:, half * 512:(half + 1) * 512], in_=pt[0:8, :])
                                self.act(fst[kk][0:8, :], pt[0:8, :], AF.Copy, reads=[pb], writes=[bfst[kk]])
                                dst = self.o['nks' if isk else 'nvs'][l, :, half * 512:(half + 1) * 512]
                                P.dma('sync', dst, fst[kk][0:8, :], reads=[bfst[kk]], writes=[self.b_out])
                        else:
                            self.gelu_from_psum(gb, pt, pb, rows, 512, lnx[0:rows, :], [blnx])
                            self.op('vector', 'tensor_reduce', reads=[blnx], writes=[bsm], out=sm[0:rows, 0:1], in_=lnx[0:rows, :], axis=AX.X, op=ALU.add)
                            self.act(lnj[0:rows, :], lnx[0:rows, :], AF.Square, reads=[blnx], writes=[gb[2]])
                            self.op('vector', 'tensor_reduce', reads=[gb[2]], writes=[bsm], out=sm[0:rows, 1:2], in_=lnj[0:rows, :], axis=AX.X, op=ALU.add)
                            self.op('vector', 'tensor_scalar', reads=[bsm], writes=[bsm], out=sm[0:rows, 2:3], in0=sm[0:rows, 0:1], scalar1=1.0 / 512, scalar2=None, op0=ALU.mult)
                            self.op('vector', 'tensor_tensor', reads=[bsm], writes=[bsm], out=sm[0:rows, 3:4], in0=sm[0:rows, 2:3], in1=sm[0:rows, 2:3], op=ALU.mult)
                            self.op('vector', 'scalar_tensor_tensor', reads=[bsm], writes=[bsm], out=sm[0:rows, 4:5], in0=sm[0:rows, 1:2], scalar=1.0 / 512, in1=sm[0:rows, 3:4], op0=ALU.mult, op1=ALU.subtract)
                            self.op('vector', 'tensor_scalar', reads=[bsm], writes=[bsm], out=sm[0:rows, 5:6], in0=sm[0:rows, 4:5], scalar1=EPS, scalar2=None, op0=ALU.add)
                            self.act(sm[0:rows, 5:6], sm[0:rows, 5:6], AF.Sqrt, reads=[bsm], writes=[bsm])
                            self.op('vector', 'reciprocal', reads=[bsm], writes=[bsm], out=sm[0:rows, 5:6], in_=sm[0:rows, 5:6])
                            self.op('vector', 'tensor_scalar', reads=[blnx, bsm], writes=[blnx], out=lnx[0:rows, :], in0=lnx[0:rows, :],
                                    scalar1=sm[0:rows, 2:3], scalar2=sm[0:rows, 5:6], op0=ALU.subtract, op1=ALU.mult)
                            if tt < 8:
                                self.op('vector', 'tensor_tensor', reads=[blnx, bsg], writes=[bgv], out=gv[:, tt, :], in0=lnx[:, :], in1=sgug[:, :], op=ALU.mult)
                            else:
                                kk = 0
                                self.op('vector', 'tensor_tensor', reads=[blnx, bsg], writes=[bfst[kk]], out=fst[kk][0:8, :], in0=lnx[0:8, :], in1=sgug[0:8, :], op=ALU.mult)
                                self.op('vector', 'tensor_copy', reads=[bfst[kk]], writes=[bgv], out=gvs[:, :], in_=fst[kk][0:8, :])
                                P.dma('sync', self.o['nsgu'][l], fst[kk][0:8, :], reads=[bfst[kk]], writes=[self.b_out])
    def phase_att(self, l, j, samp, qT, KTs, Vs, mixedT, bmix, bq, bkts, bvs):
        nc, P = self.nc, self.P
        base = j * TOK
        lo_tile = max(0, 16 - 8 * j)
        lo_tok = base - 2048 + lo_tile * 128
        ntile = 24 - lo_tile
        with self.phase() as es:
            KTh = [self.sb('KTh%d' % k, [128, 3072], BF16, es) for k in range(2)]
            Vh = [self.sb('Vh%d' % k, [128, 24, 128], BF16, es) for k in range(2)]
            Eh = [self.sb('Eh%d' % k, [128, ETW], BF16, es) for k in range(2)]
            bKT, bV, bE = [Buf(), Buf()], [Buf(), Buf()], [Buf(), Buf()]
            Pt = [self.sb('Pt%d' % k, [128, 512], BF16, es) for k in range(3)]
            Pm = [self.sb('Pm%d' % k, [128, 512], BF16, es) for k in range(3)]
            bPt, bPm = [Buf() for _ in range(3)], [Buf() for _ in range(3)]
            rz = self.sb('rz', [128, 512], F32, es)
            brz = Buf()
            if samp:
                ckt = self.sb('ckt', [128, 16, 128], F32, es)
                bck = Buf()
                KTc = self.sb('KTc', [128, 2048], BF16, es)
                bKTc = Buf()
                Vc = self.sb('Vc', [128, 16, 128], BF16, es)
                bVc = Buf()
            pc = 0
            self.ps_n = 4
            accn = 0
            for h in range(8):
                k = h % 2
                P.dma('sync', KTh[k][:, lo_tile * 128:3072], self.ktscr[h * 128:(h + 1) * 128, lo_tok:base + 1024],
                      reads=[self.b_kt[jj] for jj in range(max(0, j - 2), j + 1)], writes=[bKT[k]])
                vsrc = self.vscr[lo_tok:base + 1024, h * 128:(h + 1) * 128].rearrange("(t p) e -> p t e", p=128)
                P.dma('sync', Vh[k][:, lo_tile:24, :], vsrc,
                      reads=[self.b_v[jj] for jj in range(max(0, j - 2), j + 1)], writes=[bV[k]])
                P.dma('sync', Eh[k][:, :], self.ett[h], writes=[bE[k]])
                for qg in range(2):
                    G = 16 + 4 * qg
                    q0 = qg * 512
                    po, pbo = self.psb[4 + 2 * (accn % 2)]
                    pz, pbz = self.psb[5 + 2 * (accn % 2)]
                    accn += 1
                    kts = [kt for kt in range(G - 16, G + 4) if kt >= lo_tile]
                    kts = [G] + [kt for kt in kts if kt != G]
                    for idx, kt in enumerate(kts):
                        i0 = max(0, kt - G)
                        i1 = min(3, kt - G + 16)
                        c0, c1 = i0 * 128, (i1 + 1) * 128
                        n = c1 - c0
                        e0 = (G + i0 - kt) * 128
                        pt, pb = self.ps()
                        m = pc % 3
                        pc += 1
                        self.mm(pt[:, 0:n], KTh[k][:, kt * 128:(kt + 1) * 128], qT[:, h, q0 + c0:q0 + c1], True, True,
                                reads=[bKT[k], bq], writes=[pb])
                        self.act(Pt[m][:, 0:n], pt[:, 0:n], AF.Exp, reads=[pb], writes=[bPt[m]], scale=ATT_SCALE)
                        self.op(self.alt(), 'tensor_tensor', reads=[bPt[m], bE[k]], writes=[bPm[m]], out=Pm[m][:, 0:n], in0=Pt[m][:, 0:n],
                                in1=Eh[k][:, e0:e0 + n], op=ALU.mult)
                        st, sp = idx == 0, idx == len(kts) - 1
                        self.mm(po[:, c0:c1], Vh[k][:, kt, :], Pm[m][:, 0:n], st, sp, reads=[bV[k], bPm[m]], writes=[pbo], skip=True)
                        self.mm(pz[:, c0:c1], self.ones_bf[:, :], Pm[m][:, 0:n], st, sp, reads=[bPm[m]], writes=[pbz], skip=True)
                    self.op('vector', 'reciprocal', reads=[pbz], writes=[brz], out=rz[:, :], in_=pz[:, :])
                    self.op('vector', 'tensor_tensor', reads=[pbo, brz], writes=bmix[qg], out=mixedT[:, h, q0:q0 + 512], in0=po[:, :], in1=rz[:, :], op=ALU.mult)
                if samp:
                    c = None
                    P.dma('sync', ckt[:], self.i['ck'][l][:, h * 128:(h + 1) * 128].rearrange("(t p) e -> p t e", p=128), writes=[bck])
                    P.dma('gpsimd', Vc[:], self.i['cv'][l][:, h * 128:(h + 1) * 128].rearrange("(t p) e -> p t e", p=128), writes=[bVc])
                    for c4 in range(4):
                        pt, pb = self.ps()
                        for q in range(4):
                            ct = c4 * 4 + q
                            self.mm(pt[:, q * 128:(q + 1) * 128], ckt[:, ct, :], self.ident[:, :], True, True, reads=[bck], writes=[pb])
                        self.act(KTc[:, c4 * 512:(c4 + 1) * 512], pt[:, :], AF.Copy, reads=[pb], writes=[bKTc])
                    ps_, pbs = self.ps()
                    qs = qT[:, h, 1024:1032]
                    self.op('vector', 'memset', writes=[pbs], ap=ps_[:, 0:8], constant=0.0)
                    self.mm(ps_[0:8, 0:8], KTs[:, h, :], qs, True, True, reads=[bkts, bq], writes=[pbs], skip=True)
                    for s in range(1, 17):
                        ct = 16 - s
                        self.mm(ps_[:, s * 8:(s + 1) * 8], KTc[:, ct * 128:(ct + 1) * 128], qs, True, True, reads=[bKTc, bq], writes=[pbs], skip=True)
                    m = pc % 3
                    pc += 1
                    self.act(Pt[m][:, 0:136], ps_[:, 0:136], AF.Exp, reads=[pbs], writes=[bPt[m]], scale=ATT_SCALE)
                    self.op('vector', 'tensor_tensor', reads=[bPt[m], bE[k]], writes=[bPm[m]],
                            out=Pm[m][:, 0:136].rearrange("p (s t) -> p s t", t=8), in0=Pt[m][:, 0:136].rearrange("p (s t) -> p s t", t=8),
                            in1=Eh[k][:, :].rearrange("p (s t) -> p s t", t=128)[:, :, 0:8], op=ALU.mult)
                    po, pbo = self.psb[4 + 2 * (accn % 2)]
                    pz, pbz = self.psb[5 + 2 * (accn % 2)]
                    accn += 1
                    self.mm(po[:, 0:8], Vs[0:8, h * 128:(h + 1) * 128], Pm[m][0:8, 0:8], True, False, reads=[bvs, bPm[m]], writes=[pbo], skip=True)
                    self.mm(pz[:, 0:8], self.ones_bf[0:8, :], Pm[m][0:8, 0:8], True, False, reads=[bPm[m]], writes=[pbz], skip=True)
                    for s in range(1, 17):
                        ct = 16 - s
                        self.mm(po[:, 0:8], Vc[:, ct, :], Pm[m][:, s * 8:(s + 1) * 8], False, s == 16, reads=[bVc, bPm[m]], writes=[pbo], skip=True)
                        self.mm(pz[:, 0:8], self.ones_bf[:, :], Pm[m][:, s * 8:(s + 1) * 8], False, s == 16, reads=[bPm[m]], writes=[pbz], skip=True)
                    self.op('vector', 'reciprocal', reads=[pbz], writes=[brz], out=rz[:, 0:8], in_=pz[:, 0:8])
                    self.op('vector', 'tensor_tensor', reads=[pbo, brz], writes=bmix[2], out=mixedT[:, h, 1024:1032], in0=po[:, 0:8], in1=rz[:, 0:8], op=ALU.mult)
            self.ps_n = 8
            groups = self.groups(j)
            with self.phase() as es2:
                self.rmsnorm(es2, lambda ch, c0, c1: mixedT[:, ch, c0:c1], lambda ch, c0, c1: mixedT[:, ch, c0:c1], 8,
                             lambda ch: self.gcol[:, 32 + ch:33 + ch], groups, bmix, bmix, 'b')

    def sin_reduced(self, dst, x, tf, ti, rb, wb):
        V = lambda name, **kw: self.op('vector', name, reads=rb + wb, writes=wb, **kw)
        V('tensor_scalar', out=tf, in0=x, scalar1=1.0 / TWO_PI, scalar2=0.5, op0=ALU.mult, op1=ALU.add)
        V('tensor_copy', out=ti, in_=tf)
        V('tensor_copy', out=tf, in_=ti)
        V('scalar_tensor_tensor', out=x, in0=tf, scalar=-TWO_PI, in1=x, op0=ALU.mult, op1=ALU.add)
        V('tensor_scalar', out=tf, in0=x, scalar1=-math.pi, scalar2=TWO_PI, op0=ALU.is_lt, op1=ALU.mult)
        V('tensor_tensor', out=x, in0=x, in1=tf, op=ALU.add)
        V('tensor_scalar', out=tf, in0=x, scalar1=math.pi, scalar2=-TWO_PI, op0=ALU.is_gt, op1=ALU.mult)
        V('tensor_tensor', out=x, in0=x, in1=tf, op=ALU.add)
        self.act(dst, x, AF.Sin, reads=rb + wb, writes=wb)

    def phase_ssm(self, l, j, groups, samp, uT, bu, mixedT, bmix):
        nc, P = self.nc, self.P
        with self.phase() as es:
            es_s = contextlib.ExitStack()
            cols = self.sb('ssmcols', [128, 48], F32, es)
            dcol = self.sb('dcol', [128, 8], F32, es)
            pr = self.sb('ssmpr', [128, 16, 16], F32, es)
            Bb = self.sb('Bb', [128, 16, 2, 128], BF16, es)
            Cb = self.sb('Cb', [128, 16, 2, 128], BF16, es)
            wg = self.sb('wglu', [128, 4, 512], BF16, es)
            bbr = self.sb('bbr', [128, 16, 128], F32, es_s)
            bbi = self.sb('bbi', [128, 16, 128], F32, es_s)
            b0, bpr, bbb, bBb, bCb, bwg = [Buf() for _ in range(6)]
            P.dma('sync', cols[:], self.i['ssmcols'][l], writes=[b0])
            P.dma('sync', dcol[:], self.i['dcols'][l], writes=[b0])
            P.dma('sync', bbr[:], self.i['bbd_re'][l].rearrange("s p n -> p s n"), writes=[bbb])
            P.dma('sync', bbi[:], self.i['bbd_im'][l].rearrange("s p n -> p s n"), writes=[bbb])
            P.dma('gpsimd', Cb[:, :, 0, :], self.i['cbd_re'][l].rearrange("s p n -> p s n"), writes=[bCb])
            P.dma('gpsimd', Cb[:, :, 1, :], self.i['cbd_im'][l].rearrange("s p n -> p s n"), writes=[bCb])
            P.dma('gpsimd', wg[:], self.i['wglu'][l].rearrange("(kc p) n -> p kc n", p=128), writes=[bwg])
            lre, lim, lst = cols[:, 0:16], cols[:, 16:32], cols[:, 32:48]
            K = lambda i: pr[:, i, :]
            V = lambda name, **kw: self.op('vector', name, reads=[b0, bpr], writes=[bpr], **kw)
            self.act(K(0), lst, AF.Exp, reads=[b0], writes=[bpr])
            V('tensor_tensor', out=K(1), in0=lre, in1=K(0), op=ALU.mult)
            V('tensor_tensor', out=K(2), in0=lim, in1=K(0), op=ALU.mult)
            self.act(K(3), K(1), AF.Exp, reads=[bpr], writes=[bpr])
            kti = self.sb('kti', [128, 16], I32, es_s)
            V('tensor_scalar', out=K(11), in0=K(2), scalar1=TWO_PI, scalar2=None, op0=ALU.add)
            self.sin_reduced(K(5), K(11), K(12), kti[:, :], [b0], [bpr])
            V('tensor_scalar', out=K(11), in0=K(2), scalar1=TWO_PI + 0.5 * math.pi, scalar2=None, op0=ALU.add)
            self.sin_reduced(K(4), K(11), K(12), kti[:, :], [b0], [bpr])
            V('tensor_tensor', out=K(6), in0=K(3), in1=K(4), op=ALU.mult)
            V('tensor_scalar', out=K(6), in0=K(6), scalar1=-1.0, scalar2=None, op0=ALU.add)
            V('tensor_tensor', out=K(7), in0=K(3), in1=K(5), op=ALU.mult)
            V('tensor_tensor', out=K(8), in0=lre, in1=lre, op=ALU.mult)
            V('tensor_tensor', out=K(11), in0=lim, in1=lim, op=ALU.mult)
            V('tensor_tensor', out=K(8), in0=K(8), in1=K(11), op=ALU.add)
            V('reciprocal', out=K(8), in_=K(8))
            V('tensor_tensor', out=K(9), in0=K(6), in1=lre, op=ALU.mult)
            V('tensor_tensor', out=K(11), in0=K(7), in1=lim, op=ALU.mult)
            V('tensor_tensor', out=K(9), in0=K(9), in1=K(11), op=ALU.add)
            V('tensor_tensor', out=K(9), in0=K(9), in1=K(8), op=ALU.mult)
            V('tensor_tensor', out=K(10), in0=K(7), in1=lre, op=ALU.mult)
            V('tensor_tensor', out=K(11), in0=K(6), in1=lim, op=ALU.mult)
            V('tensor_tensor', out=K(10), in0=K(10), in1=K(11), op=ALU.subtract)
            V('tensor_tensor', out=K(10), in0=K(10), in1=K(8), op=ALU.mult)
            ones_f = self.sb('ones_f', [128, 128], F32, es_s)
            crow = self.sb('crow', [128, 16, 2, 128], F32, es_s)
            bcr = Buf()
            self.op('vector', 'memset', writes=[bcr], ap=ones_f[:], constant=1.0)
            dg = self.sb('dg', [128, 128], F32, es_s)
            bdg = Buf()
            for sc in range(16):
                for ri in range(2):
                    self.op('vector', 'tensor_scalar', reads=[bpr], writes=[bdg], out=dg[:, :], in0=self.ident[:, :],
                            scalar1=pr[:, 9 + ri, sc:sc + 1], scalar2=None, op0=ALU.mult)
                    pt, pb = self.ps()
                    self.mm(pt[:, 0:128], ones_f[:, :], dg[:, :], True, True, reads=[bdg, bcr], writes=[pb])
                    self.op('vector', 'tensor_copy', reads=[pb], writes=[bcr], out=crow[:, sc, ri, :], in_=pt[:, 0:128])
            tb = self.sb('tb', [128, 16, 128], F32, es_s)
            tb2 = self.sb('tb2', [128, 16, 128], F32, es_s)
            btb = Buf()
            G = lambda name, **kw: self.op('gpsimd', name, reads=[bbb, bcr, btb], writes=[btb], **kw)
            G('tensor_tensor', out=tb[:], in0=bbr[:], in1=crow[:, :, 0, :], op=ALU.mult)
            G('tensor_tensor', out=tb2[:], in0=bbi[:], in1=crow[:, :, 1, :], op=ALU.mult)
            self.op('gpsimd', 'tensor_tensor', reads=[btb], writes=[bBb], out=Bb[:, :, 0, :], in0=tb[:], in1=tb2[:], op=ALU.subtract)
            G('tensor_tensor', out=tb[:], in0=bbr[:], in1=crow[:, :, 1, :], op=ALU.mult)
            G('tensor_tensor', out=tb2[:], in0=bbi[:], in1=crow[:, :, 0, :], op=ALU.mult)
            self.op('gpsimd', 'tensor_tensor', reads=[btb], writes=[bBb], out=Bb[:, :, 1, :], in0=tb[:], in1=tb2[:], op=ALU.add)
            self.op('gpsimd', 'tensor_scalar', reads=[bCb], writes=[bCb], out=Cb[:, :, 1, :], in0=Cb[:, :, 1, :], scalar1=-1.0, scalar2=None, op0=ALU.mult)
            self.P.fence()
            es_s.close()
            iot = self.sb('iot', [128, 1025], F32, es)
            P.dma('sync', iot[:], self.i['iota'], writes=[Buf()])
            cs = self.sb('cs', [128, 1025], F32, es)
            tfl = self.sb('tfl', [128, 1025], F32, es)
            tin = self.sb('tin', [128, 1025], I32, es)
            sn = self.sb('sn', [128, 1025], F32, es)
            wr = self.sb('wr', [128, 1024], F32, es)
            wi = self.sb('wi', [128, 1024], F32, es)
            ta = self.sb('ta', [128, 1024], F32, es)
            tc = self.sb('tc', [128, 1024], F32, es)
            hr = self.sb('hr', [128, 1024], BF16, es)
            hi = self.sb('hi', [128, 1024], BF16, es)
            ini = self.sb('ini', [128, 8], F32, es)
            yT = self.sb('ysT', [128, 4, TW], BF16, es)
            yf = self.sb('ysf', [128, 4, TW], F32, es)
            bcs, bsn, bwr, bwi, bta, btc, bhr, bhi, bini, byT = [Buf() for _ in range(10)]
            self.P.fence()
            if j == 0:
                self.op('vector', 'memset', reads=[self.b_H], writes=[self.b_H], ap=self.Hst[:, 0:2, :], constant=0.0)
                if samp:
                    P.dma('sync', self.Hst[:, 2, :], self.i['sre'][l], writes=[self.b_H])
                    P.dma('sync', self.Hst[:, 3, :], self.i['sim'][l], writes=[self.b_H])
            runs = [(0, 1024, 0)] + ([(1024, 8, 2)] if samp else [])
            self.ps_n = 5
            ypsum = {}
            for sc in range(16):
                uc, oc = sc // 4, sc // 4
                th = pr[:, 2, sc:sc + 1]
                for (tab, btab, off) in ((sn, bsn, TWO_PI), (cs, bcs, TWO_PI + 0.5 * math.pi)):
                    self.op('vector', 'tensor_scalar', reads=[bpr], writes=[btab], out=tab[:, :], in0=iot[:, :], scalar1=th, scalar2=off, op0=ALU.mult, op1=ALU.add)
                    self.sin_reduced(tab[:, :], tab[:, :], tfl[:, :], tin[:, :], [bpr], [btab])
                for (t0, T, hs) in runs:
                    xs = []
                    for ri in range(2):
                        for c0 in range(0, T, 512):
                            n = min(512, T - c0)
                            pt, pb = self.ps()
                            self.mm(pt[:, 0:n], Bb[:, sc, ri, :], uT[:, uc, t0 + c0:t0 + c0 + n], True, True, reads=[bBb, bu], writes=[pb])
                            xs.append((ri, c0, n, pt, pb))
                    for (ri, c0, n, pt, pb) in xs:
                        x = pt[:, 0:n]
                        if ri == 0:
                            self.op('vector', 'tensor_tensor', reads=[pb, bcs], writes=[bwr], out=wr[:, c0:c0 + n], in0=x, in1=cs[:, c0:c0 + n], op=ALU.mult)
                            self.op('vector', 'tensor_tensor', reads=[pb, bsn], writes=[bta], out=ta[:, c0:c0 + n], in0=x, in1=sn[:, c0:c0 + n], op=ALU.mult)
                        else:
                            self.op('vector', 'tensor_tensor', reads=[pb, bcs], writes=[bwi], out=wi[:, c0:c0 + n], in0=x, in1=cs[:, c0:c0 + n], op=ALU.mult)
                            self.op('vector', 'tensor_tensor', reads=[pb, bsn], writes=[btc], out=tc[:, c0:c0 + n], in0=x, in1=sn[:, c0:c0 + n], op=ALU.mult)
                    self.op('gpsimd', 'tensor_tensor', reads=[bwr, btc], writes=[bwr], out=wr[:, 0:T], in0=wr[:, 0:T], in1=tc[:, 0:T], op=ALU.add)
                    self.op('gpsimd', 'tensor_tensor', reads=[bwi, bta], writes=[bwi], out=wi[:, 0:T], in0=wi[:, 0:T], in1=ta[:, 0:T], op=ALU.subtract)
                    Hre, Him = self.Hst[:, hs, sc:sc + 1], self.Hst[:, hs + 1, sc:sc + 1]
                    c1_, s1_ = cs[:, 1:2], sn[:, 1:2]
                    I = lambda name, **kw: self.op('vector', name, reads=[self.b_H, bcs, bsn, bini], writes=[bini], **kw)
                    I('tensor_tensor', out=ini[:, 2:3], in0=Him, in1=s1_, op=ALU.mult)
                    I('scalar_tensor_tensor', out=ini[:, 0:1], in0=Hre, scalar=c1_, in1=ini[:, 2:3], op0=ALU.mult, op1=ALU.subtract)
                    I('tensor_tensor', out=ini[:, 3:4], in0=Him, in1=c1_, op=ALU.mult)
                    I('scalar_tensor_tensor', out=ini[:, 1:2], in0=Hre, scalar=s1_, in1=ini[:, 3:4], op0=ALU.mult, op1=ALU.add)
                    rho = pr[:, 3, sc:sc + 1]
                    self.op('vector', 'tensor_tensor_scan', reads=[bwr, bini, bpr], writes=[bwr], out=wr[:, 0:T], data0=rho.to_broadcast([128, T]), data1=wr[:, 0:T],
                            initial=ini[:, 0:1], op0=ALU.mult, op1=ALU.add)
                    self.op('vector', 'tensor_tensor_scan', reads=[bwi, bini, bpr], writes=[bwi], out=wi[:, 0:T], data0=rho.to_broadcast([128, T]), data1=wi[:, 0:T],
                            initial=ini[:, 1:2], op0=ALU.mult, op1=ALU.add)
                    Gp = lambda name, rd, wrb, **kw: self.op('gpsimd', name, reads=rd, writes=wrb, **kw)
                    Gp('tensor_tensor', [bwr, bcs], [bta], out=ta[:, 0:T], in0=wr[:, 0:T], in1=cs[:, 0:T], op=ALU.mult)
                    Gp('tensor_tensor', [bwi, bsn], [btc], out=tc[:, 0:T], in0=wi[:, 0:T], in1=sn[:, 0:T], op=ALU.mult)
                    self.op('vector', 'tensor_tensor', reads=[bta, btc], writes=[bhr], out=hr[:, 0:T], in0=ta[:, 0:T], in1=tc[:, 0:T], op=ALU.subtract)
                    self.op('vector', 'tensor_tensor', reads=[bta, btc, self.b_H], writes=[self.b_H], out=Hre, in0=ta[:, T - 1:T], in1=tc[:, T - 1:T], op=ALU.subtract)
                    Gp('tensor_tensor', [bwr, bsn, bhr], [bta], out=ta[:, 0:T], in0=wr[:, 0:T], in1=sn[:, 0:T], op=ALU.mult)
                    Gp('tensor_tensor', [bwi, bcs, bhr], [btc], out=tc[:, 0:T], in0=wi[:, 0:T], in1=cs[:, 0:T], op=ALU.mult)
                    self.op('vector', 'tensor_tensor', reads=[bta, btc], writes=[bhi], out=hi[:, 0:T], in0=ta[:, 0:T], in1=tc[:, 0:T], op=ALU.add)
                    self.op('vector', 'tensor_tensor', reads=[bta, btc, self.b_H], writes=[self.b_H], out=Him, in0=ta[:, T - 1:T], in1=tc[:, T - 1:T], op=ALU.add)
                    for c0 in range(0, T, 512):
                        n = min(512, T - c0)
                        key = (oc, t0 + c0)
                        pt, pb = self.psb[5 + (0 if t0 + c0 == 0 else (1 if t0 + c0 == 512 else 2))]
                        self.mm(pt[:, 0:n], Cb[:, sc, 0, :], hr[:, c0:c0 + n], sc % 4 == 0, False, reads=[bCb, bhr], writes=[pb], skip=True)
                        self.mm(pt[:, 0:n], Cb[:, sc, 1, :], hi[:, c0:c0 + n], False, sc % 4 == 3, reads=[bCb, bhi], writes=[pb], skip=True)
                        if sc % 4 == 3:
                            self.op('vector', 'scalar_tensor_tensor', reads=[pb, bu, b0], writes=[byT], out=yf[:, oc, t0 + c0:t0 + c0 + n], in0=uT[:, oc, t0 + c0:t0 + c0 + n],
                                    scalar=dcol[:, oc:oc + 1], in1=pt[:, 0:n], op0=ALU.mult, op1=ALU.add)
                            self.op('gpsimd', 'tensor_copy', reads=[byT], writes=[byT], out=yT[:, oc, t0 + c0:t0 + c0 + n], in_=yf[:, oc, t0 + c0:t0 + c0 + n])
            self.ps_n = 8
            if j == self.NCH - 1:
                P.dma('sync', self.o['nsre'][l], self.Hst[:, 0, :], reads=[self.b_H], writes=[self.b_out])
                P.dma('sync', self.o['nsim'][l], self.Hst[:, 1, :], reads=[self.b_H], writes=[self.b_out])
            if samp:
                P.dma('sync', self.o['nssre'][l], self.Hst[:, 2, :], reads=[self.b_H], writes=[self.b_out])
                P.dma('sync', self.o['nssim'][l], self.Hst[:, 3, :], reads=[self.b_H], writes=[self.b_out])
            sg = self.sb('sg', [128, 512], F32, es)
            bsg = Buf()
            for oc2 in range(4):
                for (c0, c1, gi) in groups:
                    n = c1 - c0
                    pt, pb = self.ps()
                    for kc in range(4):
                        self.mm(pt[:, 0:n], wg[:, kc, oc2 * 128:(oc2 + 1) * 128], yT[:, kc, c0:c1], kc == 0, kc == 3, reads=[bwg, byT], writes=[pb])
                    self.act(sg[:, 0:n], pt[:, 0:n], AF.Sigmoid, reads=[pb, b0], writes=[bsg], bias=dcol[:, 4 + oc2:5 + oc2])
                    self.op('vector', 'tensor_tensor', reads=[bsg, byT], writes=bmix[gi], out=mixedT[:, 8 + oc2, c0:c1], in0=sg[:, 0:n], in1=yf[:, oc2, c0:c1], op=ALU.mult)
            with self.phase() as es2:
                self.rmsnorm(es2, lambda ch, c0, c1: mixedT[:, 8 + ch, c0:c1], lambda ch, c0, c1: mixedT[:, 8 + ch, c0:c1], 4,
                             lambda ch: self.gcol[:, 40 + ch:41 + ch], groups, bmix, bmix, 'c')
    def phase_sgu(self, l, j, samp, guT, gv, gvs, bgu, bgv, mixedT, bmix):
        nc, P = self.nc, self.P
        with self.phase() as es:
            wT = self.sb('sguw', [128, 512], F32, es)
            tri = self.sb('tri', [128, 512], F32, es)
            wTb = self.sb('sguwb', [128, 512], BF16, es)
            brow = self.sb('sgubrow', [1, 512], F32, es)
            browb = self.sb('sgubrowb', [1, 512], BF16, es)
            b1, b2 = Buf(), Buf()
            P.dma('sync', wT[:], self.i['sguwT'][l], writes=[b1])
            P.dma('sync', tri[:], self.i['trilT'], writes=[b1])
            P.dma('sync', brow[:], self.i['sgub'][l], writes=[b1])
            self.op('vector', 'tensor_tensor', reads=[b1], writes=[b2], out=wTb[:], in0=wT[:], in1=tri[:], op=ALU.mult)
            self.op('vector', 'tensor_copy', reads=[b1], writes=[b2], out=browb[:], in_=brow[:])
            for h in range(4):
                for half in range(2):
                    pt, pb = self.ps()
                    for q in range(4):
                        tt = half * 4 + q
                        self.mm(pt[:, q * 128:(q + 1) * 128], gv[:, tt, h * 128:(h + 1) * 128], wTb[:, h * 128:(h + 1) * 128], True, False, reads=[bgv, b2], writes=[pb], skip=True)
                        self.mm(pt[:, q * 128:(q + 1) * 128], self.ones_bf[0:1, :], browb[0:1, h * 128:(h + 1) * 128], False, True, reads=[b2], writes=[pb], skip=True)
                    c0 = half * 512
                    self.op('vector', 'tensor_tensor', reads=[pb, bgu], writes=bmix[half], out=mixedT[:, 12 + h, c0:c0 + 512], in0=pt[:, :], in1=guT[:, h, c0:c0 + 512], op=ALU.mult)
                if samp:
                    pt, pb = self.ps()
                    self.mm(pt[:, 0:8], gvs[0:8, h * 128:(h + 1) * 128], wTb[0:8, h * 128:h * 128 + 8], True, False, reads=[bgv, b2], writes=[pb], skip=True)
                    self.mm(pt[:, 0:8], self.ones_bf[0:1, :], browb[0:1, h * 128:h * 128 + 8], False, True, reads=[b2], writes=[pb], skip=True)
                    self.op('vector', 'tensor_tensor', reads=[pb, bgu], writes=bmix[2], out=mixedT[:, 12 + h, 1024:1032], in0=pt[:, 0:8], in1=guT[:, h, 1024:1032], op=ALU.mult)
            with self.phase() as es2:
                self.rmsnorm(es2, lambda ch, c0, c1: mixedT[:, 12 + ch, c0:c1], lambda ch, c0, c1: mixedT[:, 12 + ch, c0:c1], 4,
                             lambda ch: self.gcol[:, 44 + ch:45 + ch], self.groups(j), bmix, bmix, 'd')

    def proj_post(self, es, groups, nkc, lhs_fn, rhs_fn, rbufs_fn, wld_fn, nblk, cpb, gcol_off, tag):
        nc, P = self.nc, self.P
        xT = self.xT
        ng = len(groups)
        wid = sum(c1 - c0 for (c0, c1, gi) in groups)
        yT = self.sb('yT' + tag, [128, NKC, wid], BF16, es)
        byT = Buf()
        sq = [self.sb('psq%s%d' % (tag, k), [128, 512], BF16, es) for k in range(3)]
        bsq = [Buf() for _ in range(3)]
        rstd = self.sb('prstd' + tag, [128, wid], F32, es)
        brs = Buf()
        tmp = [self.sb('ptmp%s%d' % (tag, k), [128, 512], F32, es) for k in range(2)]
        btmp = [Buf(), Buf()]
        self.ps_n = 8 - ng
        ssq = {gi: self.psb[8 - ng + i] for i, (c0, c1, gi) in enumerate(groups)}
        offs = {}
        o = 0
        for (c0, c1, gi) in groups:
            offs[gi] = o
            o += c1 - c0
        pend = []
        sqc = 0
        for blk in range(nblk):
            wbuf, bw = wld_fn(blk)
            for cc in range(cpb):
                ch = blk * cpb + cc
                for (c0, c1, gi) in groups:
                    n = c1 - c0
                    pt, pb = self.ps()
                    for kc in range(nkc):
                        self.mm(pt[:, 0:n], lhs_fn(wbuf, kc, cc), rhs_fn(kc, c0, c1), kc == 0, kc == nkc - 1, reads=[bw] + rbufs_fn(gi), writes=[pb])
                    for f in pend:
                        f()
                    pend = []
                    k = sqc % 3
                    sqc += 1
                    self.act(yT[:, ch, offs[gi]:offs[gi] + n], pt[:, 0:n], AF.Copy, reads=[pb], writes=[byT])
                    self.act(sq[k][:, 0:n], pt[:, 0:n], AF.Square, reads=[pb], writes=[bsq[k]])
                    st, sp = (ch == 0), (ch == NKC - 1)

                    def f(k=k, n=n, gi=gi, st=st, sp=sp):
                        self.mm(ssq[gi][0][:, 0:n], self.ones_bf[:, :], sq[k][:, 0:n], st, sp, reads=[bsq[k]], writes=[ssq[gi][1]], skip=True)
                    pend.append(f)
        for f in pend:
            f()
        for (c0, c1, gi) in groups:
            n = c1 - c0
            r = rstd[:, offs[gi]:offs[gi] + n]
            self.op('vector', 'tensor_scalar', reads=[ssq[gi][1]], writes=[brs], out=r, in0=ssq[gi][0][:, 0:n], scalar1=1.0 / D_MODEL, scalar2=EPS, op0=ALU.mult, op1=ALU.add)
            self.act(r, r, AF.Sqrt, reads=[brs], writes=[brs])
            self.op('vector', 'reciprocal', reads=[brs], writes=[brs], out=r, in_=r)
            for ch in range(NKC):
                k = ch % 2
                eng = 'vector' if k else 'gpsimd'
                self.op('vector', 'scalar_tensor_tensor', reads=[brs, byT, self.b_gcol], writes=[btmp[k]], out=tmp[k][:, 0:n], in0=yT[:, ch, offs[gi]:offs[gi] + n],
                        scalar=self.gcol[:, gcol_off + ch:gcol_off + ch + 1], in1=r, op0=ALU.mult, op1=ALU.mult)
                self.op(eng, 'tensor_tensor', reads=[btmp[k], self.bx[gi]], writes=[self.bx[gi]], out=xT[:, ch, c0:c1], in0=xT[:, ch, c0:c1], in1=tmp[k][:, 0:n], op=ALU.add)
        self.ps_n = 8

    def phase_out(self, l, j, groups, mixedT, bmix):
        with self.phase() as es:
            wbuf = [self.sb('wO%d' % k, [128, NKC, 512], BF16, es) for k in range(2)]
            bw = [Buf(), Buf()]
            wsrc = self.i['w_out'][l]

            def wld(blk):
                k = blk % 2
                self.wload(wbuf[k][:], bw[k], wsrc[:, blk * 512:(blk + 1) * 512].rearrange("(kc p) n -> p kc n", p=128))
                return wbuf[k], bw[k]
            self.proj_post(es, groups, NKC, lambda w, kc, cc: w[:, kc, cc * 128:(cc + 1) * 128], lambda kc, c0, c1: mixedT[:, kc, c0:c1],
                           lambda gi: bmix[gi], wld, 4, 4, 16, 'o')

    def phase_ffn(self, l, j, groups):
        nc, P = self.nc, self.P
        xT = self.xT
        halves = [[g for g in groups if g[2] in (0, 2)], [g for g in groups if g[2] == 1]]
        for hg in halves:
            with self.phase() as es:
                wid = sum(c1 - c0 for (c0, c1, gi) in hg)
                offs = {}
                o = 0
                for (c0, c1, gi) in hg:
                    offs[gi] = o
                    o += c1 - c0
                hT = self.sb('hF', [128, NKC, wid], BF16, es)
                bh = {g[2]: [Buf()] for g in hg}
                with self.phase() as es2:
                    self.rmsnorm(es2, lambda ch, c0, c1: xT[:, ch, c0:c1],
                                 lambda ch, c0, c1: hT[:, ch, offs[0 if c0 == 0 else (2 if c0 == 1024 else 1)] :offs[0 if c0 == 0 else (2 if c0 == 1024 else 1)] + (c1 - c0)],
                                 NKC, lambda ch: self.gcol[:, 48 + ch:49 + ch], hg, {g[2]: [self.bx[g[2]]] for g in hg}, bh, 'f')
                actT = self.sb('actT', [128, NHC, wid], BF16, es)
                bact = Buf()
                with self.phase() as es3:
                    wg = [self.sb('wG%d' % k, [128, NKC, 256], BF16, es3) for k in range(2)]
                    wu = [self.sb('wU%d' % k, [128, NKC, 256], BF16, es3) for k in range(2)]
                    bwg, bwu = [Buf(), Buf()], [Buf(), Buf()]
                    sl = [self.sb('sl%d' % k, [128, 512], F32, es3) for k in range(2)]
                    bsl = [Buf(), Buf()]
                    cnt = 0
                    for blk in range(22):
                        k = blk % 2
                        self.wload(wg[k][:], bwg[k], self.i['w_gate'][l][:, blk * 256:(blk + 1) * 256].rearrange("(kc p) n -> p kc n", p=128))
                        self.wload(wu[k][:], bwu[k], self.i['w_up'][l][:, blk * 256:(blk + 1) * 256].rearrange("(kc p) n -> p kc n", p=128))
                        for cc in range(2):
                            hc = blk * 2 + cc
                            for (c0, c1, gi) in hg:
                                n = c1 - c0
                                pg, pbg = self.ps()
                                pu, pbu = self.ps()
                                for kc in range(NKC):
                                    self.mm(pg[:, 0:n], wg[k][:, kc, cc * 128:(cc + 1) * 128], hT[:, kc, offs[gi]:offs[gi] + n], kc == 0, kc == NKC - 1, reads=[bwg[k], bh[gi][0]], writes=[pbg])
                                for kc in range(NKC):
                                    self.mm(pu[:, 0:n], wu[k][:, kc, cc * 128:(cc + 1) * 128], hT[:, kc, offs[gi]:offs[gi] + n], kc == 0, kc == NKC - 1, reads=[bwu[k], bh[gi][0]], writes=[pbu])
                                m = cnt % 2
                                cnt += 1
                                self.act(sl[m][:, 0:n], pg[:, 0:n], AF.Silu, reads=[pbg], writes=[bsl[m]])
                                self.op('vector', 'tensor_tensor', reads=[bsl[m], pbu], writes=[bact], out=actT[:, hc, offs[gi]:offs[gi] + n], in0=sl[m][:, 0:n], in1=pu[:, 0:n], op=ALU.mult)
                with self.phase() as es4:
                    wd = [self.sb('wD%d' % k, [128, NHC, 128], BF16, es4) for k in range(2)]
                    bwd = [Buf(), Buf()]

                    def wld(blk):
                        k = blk % 2
                        self.wload(wd[k][:], bwd[k], self.i['w_down'][l][:, blk * 128:(blk + 1) * 128].rearrange("(kc p) n -> p kc n", p=128))
                        return wd[k], bwd[k]
                    self.proj_post(es4, hg, NHC, lambda w, kc, cc: w[:, kc, :], lambda kc, c0, c1: actT[:, kc, offs[0 if c0 == 0 else (2 if c0 == 1024 else 1)]:offs[0 if c0 == 0 else (2 if c0 == 1024 else 1)] + (c1 - c0)],
                                   lambda gi: [bact], wld, 16, 1, 64, 'd')

def _t5_bucket_np(dist):
    max_exact = 16
    df = np.maximum(dist, 1).astype(np.float32)
    large = max_exact + (np.log(df / np.float32(max_exact)) / np.float32(math.log(2048 / max_exact)) * np.float32(16)).astype(np.int32)
    large = np.minimum(large, 31)
    return np.where(dist < max_exact, dist, large)


def _consts():
    c = {}
    c['ident'] = np.eye(128, dtype=np.float32)
    c['jflip'] = np.eye(128, dtype=np.float32)[::-1].copy()
    s = np.arange(128)[:, None]
    t = np.arange(128)[None, :]
    c['trilT'] = np.tile((s <= t).astype(np.float32), (1, 4))
    c['iota'] = np.tile(np.arange(1025, dtype=np.float32)[None, :], (128, 1))
    delta = np.arange(TVL) - 127
    m = ((delta >= 0) & (delta <= 128)).astype(np.int32) + ((delta >= 0) & (delta <= 512) & (delta % 4 == 0)) + \
        ((delta >= 0) & (delta <= 2048) & (delta % 16 == 0))
    c33 = np.zeros((33, TVL), np.float32)
    bk = _t5_bucket_np(np.maximum(delta, 0).astype(np.int32))
    valid = m > 0
    c33[bk[valid], np.nonzero(valid)[0]] = 1.0
    c33[32, :] = np.where(valid, np.log(np.maximum(m, 1)).astype(np.float32), np.float32(-30000.0))
    c['c33'] = c33
    return c


def _colT(a, n):
    return np.ascontiguousarray(a.reshape(n, 128).T)


_NC_CACHE = {}


def run_model(inp, DEPTH, NCH):
    f32 = np.float32
    key = (DEPTH, NCH)
    if key not in _NC_CACHE:
        _NC_CACHE[key] = KB(DEPTH, NCH)
        _NC_CACHE[key].build()
    kb = _NC_CACHE[key]
    L = NCH * TOK
    cst = _consts()
    D = DEPTH
    g = lambda k: np.asarray(inp[k], f32)
    gcols = np.stack([np.concatenate([_colT(g(k)[l], 16) for k in ('g_pre_mix', 'g_post_mix', 'g_mix_out', 'g_pre_ffn', 'g_post_ffn')], 1) for l in range(D)])
    ssmcols = np.stack([np.concatenate([_colT(g('ssm_lam_re')[l].reshape(-1), 16), _colT(g('ssm_lam_im')[l].reshape(-1), 16),
                                        _colT(np.repeat(g('ssm_log_step')[l], 64), 16)], 1) for l in range(D)])
    dcols = np.stack([np.concatenate([_colT(g('ssm_d')[l], 4), _colT(g('ssm_b_glu')[l], 4)], 1) for l in range(D)])
    bbd = {}
    for nm, src in (('bbd_re', g('ssm_b_re')), ('bbd_im', g('ssm_b_im'))):
        a = np.zeros((D, 16, 128, 128), f32)
        for sc in range(16):
            for gl in range(2):
                gg = 2 * sc + gl
                r0 = (gg % 8) * 16
                a[:, sc, r0:r0 + 16, gl * 64:(gl + 1) * 64] = np.transpose(src[:D, gg], (0, 2, 1))
        bbd[nm] = a
    for nm, src in (('cbd_re', g('ssm_c_re')), ('cbd_im', g('ssm_c_im'))):
        a = np.zeros((D, 16, 128, 128), f32)
        for sc in range(16):
            for gl in range(2):
                gg = 2 * sc + gl
                c0 = (gg % 8) * 16
                a[:, sc, gl * 64:(gl + 1) * 64, c0:c0 + 16] = np.transpose(src[:D, gg], (0, 2, 1))
        bbd[nm] = a
    sgug = np.stack([np.tile(g('sgu_g')[l][None, :], (128, 1)) for l in range(D)])
    sguwT = np.stack([np.concatenate([g('sgu_w')[l, h].T for h in range(4)], 1) for l in range(D)])
    sgub = g('sgu_b')[:D].reshape(D, 1, 512)
    relb = np.concatenate([g('rel_bias'), np.ones((1, 8), f32)], 0)
    shared = dict(w_in=g('w_in')[:D], w_out=g('w_out')[:D], w_gate=g('w_gate')[:D], w_up=g('w_up')[:D], w_down=g('w_down')[:D],
                  wglu=g('ssm_w_glu')[:D], gcols=gcols, ssmcols=ssmcols, dcols=dcols, sgug=sgug, sguwT=np.ascontiguousarray(sguwT),
                  sgub=np.ascontiguousarray(sgub), relb=relb, **bbd, **cst)
    nb = inp['x_prompt'].shape[0]
    in_maps = []
    for c in range(8):
        m = dict(shared)
        m['xp'] = np.ascontiguousarray(g('x_prompt')[c % nb, :L])
        m['xs'] = np.ascontiguousarray(g('x_sample')[c])
        m['ck'] = np.ascontiguousarray(g('cache_attn_k')[:D, c].reshape(D, 2048, 1024))
        m['cv'] = np.ascontiguousarray(g('cache_attn_v')[:D, c].reshape(D, 2048, 1024))
        m['sre'] = np.stack([_colT(g('state_ssm_re')[l, c].reshape(-1), 16) for l in range(D)])
        m['sim'] = np.stack([_colT(g('state_ssm_im')[l, c].reshape(-1), 16) for l in range(D)])
        in_maps.append(m)
    res = run_bass_kernel_spmd(kb.nc, in_maps, core_ids=list(range(8)))
    R = res.results
    NK = min(2048, L)
    uncol = lambda a: np.ascontiguousarray(a.T).reshape(32, 64)
    yp = np.stack([R[b]['yp'] for b in range(nb)])
    ys = np.stack([R[c]['ys'] for c in range(8)])
    nk = np.stack([np.stack([R[b]['nk'][l].reshape(NK, 8, 128) for b in range(nb)]) for l in range(D)])
    nv = np.stack([np.stack([R[b]['nv'][l].reshape(NK, 8, 128) for b in range(nb)]) for l in range(D)])
    nre = np.stack([np.stack([uncol(R[b]['nsre'][l]) for b in range(nb)]) for l in range(D)])
    nim = np.stack([np.stack([uncol(R[b]['nsim'][l]) for b in range(nb)]) for l in range(D)])
    nks = np.stack([np.stack([R[c]['nks'][l].reshape(8, 8, 128) for c in range(8)]) for l in range(D)])
    nvs = np.stack([np.stack([R[c]['nvs'][l].reshape(8, 8, 128) for c in range(8)]) for l in range(D)])
    nsre = np.stack([np.stack([uncol(R[c]['nssre'][l]) for c in range(8)]) for l in range(D)])
    nsim = np.stack([np.stack([uncol(R[c]['nssim'][l]) for c in range(8)]) for l in range(D)])
    nsgu = np.stack([np.stack([R[c]['nsgu'][l] for c in range(8)]) for l in range(D)])
    if KDBG:
        global DBG_OUT
        DBG_OUT = [R[c]['dbg'].reshape(128, NKC, TW) for c in range(8)]
    return (yp, ys, nk, nv, nre, nim, nks, nvs, nsre, nsim, nsgu)


def kernel(**inputs):
    return run_model(inputs, 4, 4)
```

```python
import contextlib, math
import numpy as np
import concourse.bass as bass
import concourse.mybir as mybir
from concourse.bass_utils import run_bass_kernel_spmd

F32 = mybir.dt.float32
BF16 = mybir.dt.bfloat16
I32 = mybir.dt.int32
ALU = mybir.AluOpType
AF = mybir.ActivationFunctionType
AX = mybir.AxisListType

ENGS = ['tensor', 'vector', 'scalar', 'gpsimd', 'sync']
DMA_R = 8
DEBUG_NAMES = None
import os as _os
KSTOP = int(_os.environ.get('KSTOP', '9'))
KDBG = int(_os.environ.get('KDBG', '0'))
KSUB = int(_os.environ.get('KSUB', '99'))
KVAR = int(_os.environ.get('KVAR', '0'))


class _Stop(Exception):
    pass


class Buf:
    __slots__ = ('name', 'w', 'r', 'excl')

    def __init__(self, name='', excl=False):
        self.name = name
        self.w = None
        self.r = []
        self.excl = excl


class WB:
    def __init__(self, n):
        self.sub = [Buf() for _ in range(n)]


class Ins:
    __slots__ = ('eng', 'fn', 'deps', 'kind', 'dma_no', 'marked', 'cnt')

    def __init__(self, eng, fn, kind):
        self.eng = eng
        self.fn = fn
        self.kind = kind
        self.deps = []
        self.marked = False
        self.cnt = 0
        self.dma_no = -1


class Prog:
    def __init__(self, nc):
        self.nc = nc
        self.q = {e: [] for e in ENGS}
        self.ndma = {e: 0 for e in ENGS}
        self.same_engine_sync = {'vector', 'scalar', 'gpsimd'}
        self.fence_deps = {e: [] for e in ENGS}
        self.since_fence_dma = []

    def emit(self, eng, fn, reads=(), writes=(), kind='c'):
        ins = Ins(eng, fn, kind)
        if kind == 'd':
            ins.dma_no = self.ndma[eng]
            self.ndma[eng] += 1
            self.since_fence_dma.append(ins)
        deps = list(self.fence_deps[eng])
        self.fence_deps[eng] = []
        for b in reads:
            if b.w is not None:
                deps.append(b.w)
            if b.excl:
                deps.extend(x for x in b.r if x.eng != eng)
        for b in writes:
            if b.w is not None:
                deps.append(b.w)
            deps.extend(b.r)
        for d in deps:
            if d is ins:
                continue
            if d.eng == eng and d.kind == 'c' and kind == 'c' and eng not in self.same_engine_sync:
                continue
            ins.deps.append(d)
        for b in reads:
            b.r.append(ins)
        for b in writes:
            b.w = ins
            b.r = []
        self.q[eng].append(ins)
        return ins

    def fence(self):
        tails = []
        for e in ENGS:
            for ins in reversed(self.q[e]):
                if ins.kind == 'c':
                    tails.append(ins)
                    break
        tails.extend(self.since_fence_dma)
        self.since_fence_dma = []
        for e in ENGS:
            self.fence_deps[e] = self.fence_deps[e] + tails

    def dma(self, eng, out, in_, reads=(), writes=(), **kw):
        return self.emit(eng, lambda e: e.dma_start(out=out, in_=in_, **kw), reads, writes, kind='d')

    def finalize(self):
        nc = self.nc
        for e in ENGS:
            for ins in self.q[e]:
                for d in ins.deps:
                    d.marked = True
        for e in ENGS:
            c = 0
            for ins in self.q[e]:
                if ins.kind == 'c' and ins.marked:
                    c += 1
                ins.cnt = c
        with contextlib.ExitStack() as es:
            esem = {e: es.enter_context(nc.semaphore('s_' + e)) for e in ENGS}
            dsem = {e: [es.enter_context(nc.semaphore('d_%s_%d' % (e, i))) for i in range(DMA_R)]
                    for e in ENGS if self.ndma[e] > 0}
            block = es.enter_context(nc.Block())
            prog = self

            def run_engine(ename, eng):
                seen_e = {e2: 0 for e2 in ENGS}
                seen_d = {}
                for ins in prog.q[ename]:
                    need_e = {}
                    need_d = {}
                    for d in ins.deps:
                        if d.kind == 'c':
                            if d.cnt > seen_e[d.eng] and d.cnt > need_e.get(d.eng, 0):
                                need_e[d.eng] = d.cnt
                        else:
                            key = (d.eng, d.dma_no % DMA_R)
                            val = 16 * (d.dma_no // DMA_R + 1)
                            if val > seen_d.get(key, 0) and val > need_d.get(key, 0):
                                need_d[key] = val
                    if ins.kind == 'd' and ins.dma_no >= DMA_R:
                        key = (ename, ins.dma_no % DMA_R)
                        val = 16 * (ins.dma_no // DMA_R)
                        if val > seen_d.get(key, 0) and val > need_d.get(key, 0):
                            need_d[key] = val
                    for e2, v in need_e.items():
                        eng.wait_ge(esem[e2], v)
                        seen_e[e2] = v
                    for key, v in need_d.items():
                        eng.wait_ge(dsem[key[0]][key[1]], v)
                        seen_d[key] = v
                    r = ins.fn(eng)
                    if DEBUG_NAMES is not None:
                        try:
                            DEBUG_NAMES.append((r.ins.name if hasattr(r, 'ins') else getattr(r, 'name', '?'), r.concise()[:400]))
                        except Exception as ex:
                            DEBUG_NAMES.append(str(ex))
                    if ins.kind == 'c':
                        if ins.marked:
                            r.then_inc(esem[ename], 1)
                    else:
                        r.then_inc(dsem[ename][ins.dma_no % DMA_R], 16)
                if prog.ndma[ename] > 0:
                    n = prog.ndma[ename]
                    for k in range(DMA_R):
                        cntk = len(range(k, n, DMA_R))
                        if cntk > 0 and 16 * cntk > seen_d.get((ename, k), 0):
                            eng.wait_ge(dsem[ename][k], 16 * cntk)

            @block.tensor
            def _(eng):
                run_engine('tensor', eng)

            @block.vector
            def _(eng):
                run_engine('vector', eng)

            @block.scalar
            def _(eng):
                run_engine('scalar', eng)

            @block.gpsimd
            def _(eng):
                run_engine('gpsimd', eng)

            @block.sync
            def _(eng):
                run_engine('sync', eng)


D_MODEL = 2048
NKC = 16
ATT_H = 8
IN_COLS = 4608
FFN_H = 5632
NHC = 44
EPS = 1e-6
ATT_SCALE = 128 ** -0.5
TOK = 1024
TW = 1032
ETW = 17 * 128
TVL = ETW + 127
TWO_PI = 2.0 * math.pi
class KB:
    def __init__(self, DEPTH, NCH):
        self.DEPTH = DEPTH
        self.NCH = NCH
        self.L = NCH * TOK
        self.NKEEP = min(2048, self.L)
        nc = bass.Bass("TRN2", target_bir_lowering=False)
        self.nc = nc
        self.P = Prog(nc)
        self.es = contextlib.ExitStack()
        self.rr = 0
        D = DEPTH
        L = self.L

        def din(name, shape, dt=F32):
            return nc.dram_tensor(name, list(shape), dt, kind="ExternalInput").ap()

        def dout(name, shape, dt=F32):
            return nc.dram_tensor(name, list(shape), dt, kind="ExternalOutput").ap()

        self.i = dict(
            xp=din('xp', [L, 2048]), xs=din('xs', [8, 2048]),
            ck=din('ck', [D, 2048, 1024]), cv=din('cv', [D, 2048, 1024]),
            sre=din('sre', [D, 128, 16]), sim=din('sim', [D, 128, 16]),
            w_in=din('w_in', [D, 2048, IN_COLS]), w_out=din('w_out', [D, 2048, 2048]),
            w_gate=din('w_gate', [D, 2048, FFN_H]), w_up=din('w_up', [D, 2048, FFN_H]),
            w_down=din('w_down', [D, FFN_H, 2048]), wglu=din('wglu', [D, 512, 512]),
            gcols=din('gcols', [D, 128, 5 * 16]), ssmcols=din('ssmcols', [D, 128, 3 * 16]),
            dcols=din('dcols', [D, 128, 8]),
            bbd_re=din('bbd_re', [D, 16, 128, 128]), bbd_im=din('bbd_im', [D, 16, 128, 128]),
            cbd_re=din('cbd_re', [D, 16, 128, 128]), cbd_im=din('cbd_im', [D, 16, 128, 128]),
            sgug=din('sgug', [D, 128, 512]), sguwT=din('sguwT', [D, 128, 512]), sgub=din('sgub', [D, 1, 512]),
            ident=din('ident', [128, 128]), jflip=din('jflip', [128, 128]), trilT=din('trilT', [128, 512]),
            iota=din('iota', [128, 1025]), c33=din('c33', [33, TVL]), relb=din('relb', [33, 8]),
        )
        self.o = dict(
            yp=dout('yp', [L, 2048]), ys=dout('ys', [8, 2048]),
            nk=dout('nk', [D, self.NKEEP, 1024]), nv=dout('nv', [D, self.NKEEP, 1024]),
            nsre=dout('nsre', [D, 128, 16]), nsim=dout('nsim', [D, 128, 16]),
            nks=dout('nks', [D, 8, 1024]), nvs=dout('nvs', [D, 8, 1024]),
            nssre=dout('nssre', [D, 128, 16]), nssim=dout('nssim', [D, 128, 16]),
            nsgu=dout('nsgu', [D, 8, 512]),
        )
        if KDBG:
            self.o['dbg'] = dout('dbg', [128, NKC * TW])
        self.xscr = nc.dram_tensor('xscr', [2048, L], F32).ap()
        self.ktscr = nc.dram_tensor('ktscr', [1024, L], BF16).ap()
        self.vscr = nc.dram_tensor('vscr', [L, 1024], BF16).ap()
        self.etv = nc.dram_tensor('etv', [8, TVL], F32)
        self.ett = nc.dram_tensor('ett', [8, 128, ETW], BF16).ap()
        self.b_xscr = [Buf() for _ in range(NCH)]
        self.b_kt = [Buf() for _ in range(NCH)]
        self.b_v = [Buf() for _ in range(NCH)]
        self.b_out = Buf()

    def sb(self, name, shape, dt, es=None):
        self.uid = getattr(self, 'uid', 0) + 1
        return (es or self.es).enter_context(self.nc.sbuf_tensor('%s_u%d' % (name, self.uid), list(shape), dt))

    def op(self, eng, name, reads=(), writes=(), **kw):
        return self.P.emit(eng, lambda e: getattr(e, name)(**kw), reads, writes)

    def mm(self, out, lhsT, rhs, start, stop, reads=(), writes=(), skip=False):
        if skip:
            return self.P.emit('tensor', lambda e: e.matmul(out, lhsT=lhsT, rhs=rhs, start=start, stop=stop, skip_group_check=True), reads, writes)
        return self.P.emit('tensor', lambda e: e.matmul(out, lhsT=lhsT, rhs=rhs, start=start, stop=stop), reads, writes)

    def act(self, out, in_, func, reads=(), writes=(), **kw):
        return self.P.emit('scalar', lambda e: e.activation(out=out, in_=in_, func=func, **kw), reads, writes)

    def ps(self):
        k = self.rr % getattr(self, 'ps_n', 8)
        self.rr += 1
        return self.psb[k]

    def alt(self):
        self.altc = getattr(self, 'altc', 0) + 1
        return 'vector' if self.altc % 2 else 'gpsimd'

    @contextlib.contextmanager
    def phase(self):
        es = contextlib.ExitStack()
        with es:
            yield es
        self.P.fence()

    def build(self):
        nc, P = self.nc, self.P
        with self.es:
            self.psb = []
            for k in range(8):
                t = self.es.enter_context(nc.psum_tensor('ps%d' % k, [128, 512], F32))
                self.psb.append((t, Buf('ps%d' % k, excl=True)))
            self.xT = self.sb('xT', [128, NKC, TW], F32)
            self.bx = [Buf('x0'), Buf('x1'), Buf('xs')]
            self.xS = self.sb('xS', [128, NKC, 8], F32)
            self.ident = self.sb('ident', [128, 128], F32)
            self.ones_bf = self.sb('ones_bf', [128, 128], BF16)
            self.gcol = self.sb('gcol', [128, 80], F32)
            self.b_gcol = Buf('gcol')
            self.Hst = self.sb('Hst', [128, 4, 16], F32)
            self.b_H = Buf('H')
            self.wstg = self.sb('wstg', [128, 4, 512], F32)
            self.bstg = [Buf() for _ in range(4)]
            self.stg_i = 0
            self.setup()
            try:
                for l in range(self.DEPTH):
                    P.dma('sync', self.gcol[:], self.i['gcols'][l], writes=[self.b_gcol])
                    for j in range(self.NCH):
                        self.chunk_layer(l, j)
            except _Stop:
                pass
            P.finalize()
        return nc

    def groups(self, j):
        g = [(0, 512, 0), (512, 1024, 1)]
        if j == 0:
            g.append((1024, 1032, 2))
        return g

    def setup(self):
        nc, P = self.nc, self.P
        b = Buf()
        P.dma('sync', self.ident[:], self.i['ident'], writes=[b])
        self.op('vector', 'memset', writes=[b], ap=self.ones_bf[:], constant=1.0)
        with self.phase() as es:
            c33 = self.sb('c33', [33, TVL], F32, es)
            relb = self.sb('relb', [33, 8], F32, es)
            jf = self.sb('jf', [128, 128], F32, es)
            tv = self.sb('tv', [8, TVL], F32, es)
            hk = self.sb('hk', [128, ETW], F32, es)
            tt = self.sb('tt', [128, ETW], BF16, es)
            b1, b2, b3, b4, b5, b6, b7 = [Buf() for _ in range(7)]
            P.dma('sync', c33[:], self.i['c33'], writes=[b1])
            P.dma('sync', relb[:], self.i['relb'], writes=[b2])
            P.dma('sync', jf[:], self.i['jflip'], writes=[b3])
            c0 = 0
            while c0 < TVL:
                n = min(512, TVL - c0)
                pt, pb = self.ps()
                self.mm(pt[0:8, 0:n], relb[:, :], c33[:, c0:c0 + n], True, True, reads=[b1, b2], writes=[pb])
                self.act(tv[:, c0:c0 + n], pt[0:8, 0:n], AF.Exp, reads=[pb], writes=[b4])
                c0 += n
            P.dma('sync', self.etv.ap(), tv[:], reads=[b4], writes=[b5])
            for h in range(8):
                src = bass.AP(self.etv, h * TVL, [[1, 128], [1, ETW]])
                P.dma('sync', hk[:], src, reads=[b5], writes=[b6])
                c0 = 0
                while c0 < ETW:
                    n = min(512, ETW - c0)
                    pt, pb = self.ps()
                    self.mm(pt[:, 0:n], jf[:, :], hk[:, c0:c0 + n], True, True, reads=[b3, b6], writes=[pb])
                    self.op('vector', 'tensor_copy', reads=[pb], writes=[b7], out=tt[:, c0:c0 + n], in_=pt[:, 0:n])
                    c0 += n
                P.dma('sync', self.ett[h], tt[:], reads=[b7], writes=[Buf()])

    def rmsnorm(self, es, src_fn, dst_fn, nch, gcol_fn, groups, rbufs, wbufs, tag, in_dt=F32):
        sq = [self.sb('sq%s%d' % (tag, k), [128, 512], BF16, es) for k in range(2)]
        bsq = [Buf(), Buf()]
        rstd = self.sb('rstd' + tag, [128, 512], F32, es)
        brs = Buf()
        for (c0, c1, gi) in groups:
            n = c1 - c0
            pt, pb = self.ps()
            for ch in range(nch):
                k = ch % 2
                self.act(sq[k][:, 0:n], src_fn(ch, c0, c1), AF.Square, reads=rbufs[gi], writes=[bsq[k]])
                self.mm(pt[:, 0:n], self.ones_bf[:, :], sq[k][:, 0:n], ch == 0, ch == nch - 1, reads=[bsq[k]], writes=[pb])
            self.op('vector', 'tensor_scalar', reads=[pb], writes=[brs], out=rstd[:, 0:n], in0=pt[:, 0:n],
                    scalar1=1.0 / (nch * 128), scalar2=EPS, op0=ALU.mult, op1=ALU.add)
            self.act(rstd[:, 0:n], rstd[:, 0:n], AF.Sqrt, reads=[brs], writes=[brs])
            self.op('vector', 'reciprocal', reads=[brs], writes=[brs], out=rstd[:, 0:n], in_=rstd[:, 0:n])
            for ch in range(nch):
                self.op('vector', 'scalar_tensor_tensor', reads=[brs, self.b_gcol] + list(rbufs[gi]), writes=wbufs[gi],
                        out=dst_fn(ch, c0, c1), in0=src_fn(ch, c0, c1), scalar=gcol_fn(ch), in1=rstd[:, 0:n],
                        op0=ALU.mult, op1=ALU.mult)
    def wload(self, wt, wb, src):
        nk, ncol = wt.shape[1], wt.shape[2]
        g = max(1, 512 // ncol)
        for k0 in range(0, nk, g):
            k1 = min(nk, k0 + g)
            i = self.stg_i % 4
            self.stg_i += 1
            st = self.wstg[:, i, 0:(k1 - k0) * ncol].rearrange("p (a b) -> p a b", b=ncol)
            self.P.dma('sync', st, src[:, k0:k1, :], writes=[self.bstg[i]])
            eng = ('gpsimd', 'scalar', 'vector')[self.stg_i % 3]
            subs = [wb.sub[kc] for kc in range(k0, k1)]
            if eng == 'scalar':
                self.act(wt[:, k0:k1, :], st, AF.Copy, reads=[self.bstg[i]], writes=subs)
            else:
                self.op(eng, 'tensor_copy', reads=[self.bstg[i]], writes=subs, out=wt[:, k0:k1, :], in_=st)

    def gelu_from_psum(self, es_bufs, pt, pb, rows, n, out_ap, wb_out):
        t1, t2, b1, b2 = es_bufs
        x = pt[0:rows, 0:n]
        self.act(t1[0:rows, 0:n], x, AF.Square, reads=[pb], writes=[b1])
        self.op('vector', 'tensor_scalar', reads=[b1], writes=[b1], out=t1[0:rows, 0:n], in0=t1[0:rows, 0:n],
                scalar1=0.044715, scalar2=1.0, op0=ALU.mult, op1=ALU.add)
        self.op('vector', 'tensor_tensor', reads=[b1, pb], writes=[b2], out=t2[0:rows, 0:n], in0=t1[0:rows, 0:n], in1=x, op=ALU.mult)
        self.act(t1[0:rows, 0:n], t2[0:rows, 0:n], AF.Sigmoid, reads=[b2], writes=[b1], scale=1.5957691216057308)
        self.op('vector', 'tensor_tensor', reads=[b1, pb], writes=wb_out, out=out_ap, in0=t1[0:rows, 0:n], in1=x, op=ALU.mult)

    def chunk_layer(self, l, j):
        nc, P = self.nc, self.P
        base = j * TOK
        groups = self.groups(j)
        samp = (j == 0)
        xT = self.xT
        last = (l == self.DEPTH - 1)
        with self.phase() as es:
            if l == 0:
                xin = [self.sb('xin%d' % k, [128, 2048], F32, es) for k in range(2)]
                bxin = [Buf(), Buf()]
                for tt in range(8):
                    k = tt % 2
                    P.dma('sync', xin[k][:], self.i['xp'][base + tt * 128: base + (tt + 1) * 128, :], writes=[bxin[k]])
                    for c4 in range(4):
                        pt, pb = self.ps()
                        for q in range(4):
                            ch = c4 * 4 + q
                            self.mm(pt[:, q * 128:(q + 1) * 128], xin[k][:, ch * 128:(ch + 1) * 128], self.ident[:, :], True, True, reads=[bxin[k]], writes=[pb])
                        self.op('vector', 'tensor_copy', reads=[pb], writes=[self.bx[tt // 4]],
                                out=xT[:, c4 * 4:(c4 + 1) * 4, tt * 128:(tt + 1) * 128],
                                in_=pt[:, :].rearrange("p (q t) -> p q t", q=4))
                if samp:
                    P.dma('sync', xin[0][0:8, :], self.i['xs'], writes=[bxin[0]])
                    for c4 in range(4):
                        pt, pb = self.ps()
                        for q in range(4):
                            ch = c4 * 4 + q
                            self.mm(pt[:, q * 8:(q + 1) * 8], xin[0][0:8, ch * 128:(ch + 1) * 128], self.ident[0:8, 0:8], True, True, reads=[bxin[0]], writes=[pb])
                        self.op('vector', 'tensor_copy', reads=[pb], writes=[self.bx[2]],
                                out=xT[:, c4 * 4:(c4 + 1) * 4, 1024:1032], in_=pt[:, 0:32].rearrange("p (q t) -> p q t", q=4))
            else:
                for ch in range(NKC):
                    P.dma('sync', xT[:, ch, 0:1024], self.xscr[ch * 128:(ch + 1) * 128, base:base + 1024],
                          reads=[self.b_xscr[j]], writes=[self.bx[0], self.bx[1]])
                if samp:
                    self.op('vector', 'tensor_copy', writes=[self.bx[2]], out=xT[:, :, 1024:1032], in_=self.xS[:, :, :])
        with self.phase() as es_mix:
            hm = self.sb('hm', [128, NKC, TW], BF16, es_mix)
            mixedT = hm
            bmix = {gi: [Buf()] for gi in range(3)}
            with self.phase() as es_u:
                uT = self.sb('uT', [128, 4, TW], BF16, es_u)
                bu = Buf()
                with self.phase() as es_g:
                    guT = self.sb('guT', [128, 4, TW], BF16, es_g)
                    gv = self.sb('gv', [128, 8, 512], BF16, es_g)
                    gvs = self.sb('gvs', [8, 512], BF16, es_g)
                    bgu, bgv = Buf(), Buf()
                    with self.phase() as es_q:
                        qT = self.sb('qT', [128, 8, TW], BF16, es_q)
                        KTs = self.sb('KTs', [128, 8, 8], BF16, es_q)
                        Vs = self.sb('Vs', [8, 1024], BF16, es_q)
                        bq, bkts, bvs = Buf(), Buf(), Buf()
                        if KSTOP >= 1:
                            self.phase_A(l, j, groups, samp, hm, qT, uT, guT, gv, gvs, KTs, Vs, bq, bu, bgu, bgv, bkts, bvs)
                        if KSTOP >= 2:
                            self.phase_att(l, j, samp, qT, KTs, Vs, mixedT, bmix, bq, bkts, bvs)
                    if KSTOP >= 3:
                        self.phase_sgu(l, j, samp, guT, gv, gvs, bgu, bgv, mixedT, bmix)
                if KSTOP >= 4:
                    self.phase_ssm(l, j, groups, samp, uT, bu, mixedT, bmix)
            if KDBG and l == 0 and j == 0:
                self.P.fence()
                self.P.dma('gpsimd', self.o['dbg'], hm[:].rearrange("p a b -> p (a b)"), writes=[self.b_out])
                self.P.fence()
            if KSTOP >= 5:
                self.phase_out(l, j, groups, mixedT, bmix)
        if KSTOP >= 6:
            self.phase_ffn(l, j, groups)
        with self.phase() as es:
            if samp:
                self.op('vector', 'tensor_copy', reads=[self.bx[2]], writes=[Buf()], out=self.xS[:, :, :], in_=xT[:, :, 1024:1032])
            if not last:
                for ch in range(NKC):
                    P.dma('sync', self.xscr[ch * 128:(ch + 1) * 128, base:base + 1024], xT[:, ch, 0:1024],
                          reads=[self.bx[0], self.bx[1]], writes=[self.b_xscr[j]])
            else:
                yo = [self.sb('yo%d' % k, [128, 2048], F32, es) for k in range(2)]
                byo = [Buf(), Buf()]
                for tt in range(8):
                    k = tt % 2
                    for c4 in range(4):
                        pt, pb = self.ps()
                        for q in range(4):
                            ch = c4 * 4 + q
                            self.mm(pt[:, q * 128:(q + 1) * 128], xT[:, ch, tt * 128:(tt + 1) * 128], self.ident[:, :], True, True,
                                    reads=[self.bx[tt // 4]], writes=[pb])
                        self.op('vector', 'tensor_copy', reads=[pb], writes=[byo[k]], out=yo[k][:, c4 * 512:(c4 + 1) * 512], in_=pt[:, :])
                    P.dma('sync', self.o['yp'][base + tt * 128: base + (tt + 1) * 128, :], yo[k][:], reads=[byo[k]], writes=[self.b_out])
                if samp:
                    for c4 in range(4):
                        pt, pb = self.ps()
                        for q in range(4):
                            ch = c4 * 4 + q
                            self.mm(pt[0:8, q * 128:(q + 1) * 128], xT[:, ch, 1024:1032], self.ident[:, :], True, True, reads=[self.bx[2]], writes=[pb])
                        self.op('vector', 'tensor_copy', reads=[pb], writes=[byo[0]], out=yo[0][0:8, c4 * 512:(c4 + 1) * 512], in_=pt[0:8, :])
                    P.dma('sync', self.o['ys'], yo[0][0:8, :], reads=[byo[0]], writes=[self.b_out])

    def phase_A(self, l, j, groups, samp, hT, qT, uT, guT, gv, gvs, KTs, Vs, bq, bu, bgu, bgv, bkts, bvs):
        nc, P = self.nc, self.P
        base = j * TOK
        xT = self.xT
        keep_lo = self.L - self.NKEEP
        with self.phase() as es:
            bh = {gi: [Buf()] for gi in range(3)}
            with self.phase() as es2:
                self.rmsnorm(es2, lambda ch, c0, c1: xT[:, ch, c0:c1], lambda ch, c0, c1: hT[:, ch, c0:c1], NKC,
                             lambda ch: self.gcol[:, ch:ch + 1], groups, {gi: [self.bx[gi]] for gi in range(3)}, bh, 'a')
            allh = [bh[g[2]][0] for g in groups]
            wbuf = [self.sb('wA%d' % k, [128, NKC, 512], BF16, es) for k in range(2)]
            bw = [WB(NKC), WB(NKC)]
            kst = [self.sb('kst%d' % k, [128, 1024], BF16, es) for k in range(2)]
            bkst = [Buf(), Buf()]
            vst = [self.sb('vst%d' % k, [128, 512], BF16, es) for k in range(2)]
            bvst = [Buf(), Buf()]
            fst = [self.sb('fst%d' % k, [128, 512], F32, es) for k in range(2)]
            bfst = [Buf(), Buf()]
            t1 = self.sb('gt1', [128, 512], F32, es)
            t2 = self.sb('gt2', [128, 512], F32, es)
            gb = (t1, t2, Buf(), Buf())
            lnx = self.sb('lnx', [128, 512], F32, es)
            blnx = Buf()
            lnj = self.sb('lnj', [128, 512], F32, es)
            sm = self.sb('lnsm', [128, 8], F32, es)
            bsm = Buf()
            sgug = self.sb('sgug', [128, 512], F32, es)
            bsg = Buf()
            P.dma('sync', sgug[:], self.i['sgug'][l], writes=[bsg])
            win = self.i['w_in'][l]
            cnt = 0
            for blk in range(9):
                if blk >= KSUB:
                    break
                k = blk % 2
                self.wload(wbuf[k][:], bw[k], win[:, blk * 512:(blk + 1) * 512].rearrange("(kc p) n -> p kc n", p=128))
                fm = blk in (0, 1, 2, 3, 6, 7)
                if fm:
                    for cc in range(4):
                        head = (blk % 2) * 4 + cc
                        for (c0, c1, gi) in groups:
                            n = c1 - c0
                            pt, pb = self.ps()
                            for kc in range(NKC):
                                self.mm(pt[:, 0:n], wbuf[k][:, kc, cc * 128:(cc + 1) * 128], hT[:, kc, c0:c1], kc == 0, kc == NKC - 1,
                                        reads=[bw[k].sub[kc], bh[gi][0]], writes=[pb])
                            if blk in (0, 1):
                                self.act(qT[:, head, c0:c1], pt[:, 0:n], AF.Copy, reads=[pb], writes=[bq])
                            elif blk in (2, 3):
                                if gi < 2:
                                    kk = head % 2
                                    self.act(kst[kk][:, c0:c1], pt[:, 0:n], AF.Copy, reads=[pb], writes=[bkst[kk]])
                                    if gi == 1:
                                        P.dma('sync', self.ktscr[head * 128:(head + 1) * 128, base:base + 1024], kst[kk][:, :],
                                              reads=[bkst[kk]], writes=[self.b_kt[j]])
                                else:
                                    self.act(KTs[:, head, :], pt[:, 0:n], AF.Copy, reads=[pb], writes=[bkts])
                            elif blk == 6:
                                self.act(uT[:, cc, c0:c1], pt[:, 0:n], AF.Copy, reads=[pb], writes=[bu])
                            else:
                                self.gelu_from_psum(gb, pt, pb, 128, n, guT[:, cc, c0:c1], [bgu])
                tokmaj = blk in (4, 5, 8) or (blk in (2, 3))
                if tokmaj:
                    tts = list(range(8)) + ([8] if samp else [])
                    for tt in tts:
                        rows = 128 if tt < 8 else 8
                        tok0 = base + tt * 128
                        if blk in (2, 3):
                            if tt < 8 and tok0 < keep_lo:
                                continue
                        c0 = tt * 128 if tt < 8 else 1024
                        gi = (tt // 4) if tt < 8 else 2
                        pt, pb = self.ps()
                        for kc in range(NKC):
                            self.mm(pt[0:rows, :], hT[:, kc, c0:c0 + rows], wbuf[k][:, kc, :], kc == 0, kc == NKC - 1,
                                    reads=[bw[k].sub[kc], bh[gi][0]], writes=[pb])
                        half = blk % 2
                        if blk in (2, 3, 4, 5):
                            isk = blk in (2, 3)
                            if tt < 8:
                                if not isk and KVAR != 1:
                                    kk = cnt % 2
                                    self.op('vector', 'tensor_copy', reads=[pb], writes=[bvst[kk]], out=vst[kk][:, :], in_=pt[:, :])
                                    P.dma('sync', self.vscr[tok0:tok0 + 128, half * 512:(half + 1) * 512], vst[kk][:, :],
                                          reads=[bvst[kk]], writes=[self.b_v[j]])
                                if tok0 >= keep_lo:
                                    kk = cnt % 2
                                    self.act(fst[kk][:, :], pt[:, :], AF.Copy, reads=[pb], writes=[bfst[kk]])
                                    dst = self.o['nk' if isk else 'nv'][l, tok0 - keep_lo: tok0 - keep_lo + 128, half * 512:(half + 1) * 512]
                                    P.dma('sync', dst, fst[kk][:, :], reads=[bfst[kk]], writes=[self.b_out])
                                cnt += 1
                            else:
                                kk = cnt % 2
                                cnt += 1
                                if not isk and KVAR != 2:
                                    self.op('vector', 'tensor_copy', reads=[pb], writes=[bvs], out=Vs[:, half * 512:(half + 1) * 512], in_=pt[0:8, :])
                                self.act(fst[kk][0:8, :], pt[0:8, :], AF.Copy, reads=[pb], writes=[bfst[kk]])
                                dst = self.o['nks' if isk else 'nvs'][l, :, half * 512:(half + 1) * 512]
                                P.dma('sync', dst, fst[kk][0:8, :], reads=[bfst[kk]], writes=[self.b_out])
                        else:
                            self.gelu_from_psum(gb, pt, pb, rows, 512, lnx[0:rows, :], [blnx])
                            self.op('vector', 'tensor_reduce', reads=[blnx], writes=[bsm], out=sm[0:rows, 0:1], in_=lnx[0:rows, :], axis=AX.X, op=ALU.add)
                            self.act(lnj[0:rows, :], lnx[0:rows, :], AF.Square, reads=[blnx], writes=[gb[2]])
                            self.op('vector', 'tensor_reduce', reads=[gb[2]], writes=[bsm], out=sm[0:rows, 1:2], in_=lnj[0:rows, :], axis=AX.X, op=ALU.add)
                            self.op('vector', 'tensor_scalar', reads=[bsm], writes=[bsm], out=sm[0:rows, 2:3], in0=sm[0:rows, 0:1], scalar1=1.0 / 512, scalar2=None, op0=ALU.mult)
                            self.op('vector', 'tensor_tensor', reads=[bsm], writes=[bsm], out=sm[0:rows, 3:4], in0=sm[0:rows, 2:3], in1=sm[0:rows, 2:3], op=ALU.mult)
                            self.op('vector', 'scalar_tensor_tensor', reads=[bsm], writes=[bsm], out=sm[0:rows, 4:5], in0=sm[0:rows, 1:2], scalar=1.0 / 512, in1=sm[0:rows, 3:4], op0=ALU.mult, op1=ALU.subtract)
                            self.op('vector', 'tensor_scalar', reads=[bsm], writes=[bsm], out=sm[0:rows, 5:6], in0=sm[0:rows, 4:5], scalar1=EPS, scalar2=None, op0=ALU.add)
                            self.act(sm[0:rows, 5:6], sm[0:rows, 5:6], AF.Sqrt, reads=[bsm], writes=[bsm])
                            self.op('vector', 'reciprocal', reads=[bsm], writes=[bsm], out=sm[0:rows, 5:6], in_=sm[0:rows, 5:6])
                            self.op('vector', 'tensor_scalar', reads=[blnx, bsm], writes=[blnx], out=lnx[0:rows, :], in0=lnx[0:rows, :],
                                    scalar1=sm[0:rows, 2:3], scalar2=sm[0:rows, 5:6], op0=ALU.subtract, op1=ALU.mult)
                            if tt < 8:
                                self.op('vector', 'tensor_tensor', reads=[blnx, bsg], writes=[bgv], out=gv[:, tt, :], in0=lnx[:, :], in1=sgug[:, :], op=ALU.mult)
                            else:
                                kk = 0
                                self.op('vector', 'tensor_tensor', reads=[blnx, bsg], writes=[bfst[kk]], out=fst[kk][0:8, :], in0=lnx[0:8, :], in1=sgug[0:8, :], op=ALU.mult)
                                self.op('vector', 'tensor_copy', reads=[bfst[kk]], writes=[bgv], out=gvs[:, :], in_=fst[kk][0:8, :])
                                P.dma('sync', self.o['nsgu'][l], fst[kk][0:8, :], reads=[bfst[kk]], writes=[self.b_out])
    def phase_att(self, l, j, samp, qT, KTs, Vs, mixedT, bmix, bq, bkts, bvs):
        nc, P = self.nc, self.P
        base = j * TOK
        lo_tile = max(0, 16 - 8 * j)
        lo_tok = base - 2048 + lo_tile * 128
        ntile = 24 - lo_tile
        with self.phase() as es:
            KTh = [self.sb('KTh%d' % k, [128, 3072], BF16, es) for k in range(2)]
            Vh = [self.sb('Vh%d' % k, [128, 24, 128], BF16, es) for k in range(2)]
            Eh = [self.sb('Eh%d' % k, [128, ETW], BF16, es) for k in range(2)]
            bKT, bV, bE = [Buf(), Buf()], [Buf(), Buf()], [Buf(), Buf()]
            Pt = [self.sb('Pt%d' % k, [128, 512], BF16, es) for k in range(3)]
            Pm = [self.sb('Pm%d' % k, [128, 512], BF16, es) for k in range(3)]
            bPt, bPm = [Buf() for _ in range(3)], [Buf() for _ in range(3)]
            rz = self.sb('rz', [128, 512], F32, es)
            brz = Buf()
            if samp:
                ckt = self.sb('ckt', [128, 16, 128], F32, es)
                bck = Buf()
                KTc = self.sb('KTc', [128, 2048], BF16, es)
                bKTc = Buf()
                Vc = self.sb('Vc', [128, 16, 128], BF16, es)
                bVc = Buf()
            pc = 0
            self.ps_n = 4
            accn = 0
            for h in range(8):
                k = h % 2
                P.dma('sync', KTh[k][:, lo_tile * 128:3072], self.ktscr[h * 128:(h + 1) * 128, lo_tok:base + 1024],
                      reads=[self.b_kt[jj] for jj in range(max(0, j - 2), j + 1)], writes=[bKT[k]])
                vsrc = self.vscr[lo_tok:base + 1024, h * 128:(h + 1) * 128].rearrange("(t p) e -> p t e", p=128)
                P.dma('sync', Vh[k][:, lo_tile:24, :], vsrc,
                      reads=[self.b_v[jj] for jj in range(max(0, j - 2), j + 1)], writes=[bV[k]])
                P.dma('sync', Eh[k][:, :], self.ett[h], writes=[bE[k]])
                for qg in range(2):
                    G = 16 + 4 * qg
                    q0 = qg * 512
                    po, pbo = self.psb[4 + 2 * (accn % 2)]
                    pz, pbz = self.psb[5 + 2 * (accn % 2)]
                    accn += 1
                    kts = [kt for kt in range(G - 16, G + 4) if kt >= lo_tile]
                    kts = [G] + [kt for kt in kts if kt != G]
                    for idx, kt in enumerate(kts):
                        i0 = max(0, kt - G)
                        i1 = min(3, kt - G + 16)
                        c0, c1 = i0 * 128, (i1 + 1) * 128
                        n = c1 - c0
                        e0 = (G + i0 - kt) * 128
                        pt, pb = self.ps()
                        m = pc % 3
                        pc += 1
                        self.mm(pt[:, 0:n], KTh[k][:, kt * 128:(kt + 1) * 128], qT[:, h, q0 + c0:q0 + c1], True, True,
                                reads=[bKT[k], bq], writes=[pb])
                        self.act(Pt[m][:, 0:n], pt[:, 0:n], AF.Exp, reads=[pb], writes=[bPt[m]], scale=ATT_SCALE)
                        self.op(self.alt(), 'tensor_tensor', reads=[bPt[m], bE[k]], writes=[bPm[m]], out=Pm[m][:, 0:n], in0=Pt[m][:, 0:n],
                                in1=Eh[k][:, e0:e0 + n], op=ALU.mult)
                        st, sp = idx == 0, idx == len(kts) - 1
                        self.mm(po[:, c0:c1], Vh[k][:, kt, :], Pm[m][:, 0:n], st, sp, reads=[bV[k], bPm[m]], writes=[pbo], skip=True)
                        self.mm(pz[:, c0:c1], self.ones_bf[:, :], Pm[m][:, 0:n], st, sp, reads=[bPm[m]], writes=[pbz], skip=True)
                    self.op('vector', 'reciprocal', reads=[pbz], writes=[brz], out=rz[:, :], in_=pz[:, :])
                    self.op('vector', 'tensor_tensor', reads=[pbo, brz], writes=bmix[qg], out=mixedT[:, h, q0:q0 + 512], in0=po[:, :], in1=rz[:, :], op=ALU.mult)
                if samp:
                    c = None
                    P.dma('sync', ckt[:], self.i['ck'][l][:, h * 128:(h + 1) * 128].rearrange("(t p) e -> p t e", p=128), writes=[bck])
                    P.dma('gpsimd', Vc[:], self.i['cv'][l][:, h * 128:(h + 1) * 128].rearrange("(t p) e -> p t e", p=128), writes=[bVc])
                    for c4 in range(4):
                        pt, pb = self.ps()
                        for q in range(4):
                            ct = c4 * 4 + q
                            self.mm(pt[:, q * 128:(q + 1) * 128], ckt[:, ct, :], self.ident[:, :], True, True, reads=[bck], writes=[pb])
                        self.act(KTc[:, c4 * 512:(c4 + 1) * 512], pt[:, :], AF.Copy, reads=[pb], writes=[bKTc])
                    ps_, pbs = self.ps()
                    qs = qT[:, h, 1024:1032]
                    self.op('vector', 'memset', writes=[pbs], ap=ps_[:, 0:8], constant=0.0)
                    self.mm(ps_[0:8, 0:8], KTs[:, h, :], qs, True, True, reads=[bkts, bq], writes=[pbs], skip=True)
                    for s in range(1, 17):
                        ct = 16 - s
                        self.mm(ps_[:, s * 8:(s + 1) * 8], KTc[:, ct * 128:(ct + 1) * 128], qs, True, True, reads=[bKTc, bq], writes=[pbs], skip=True)
                    m = pc % 3
                    pc += 1
                    self.act(Pt[m][:, 0:136], ps_[:, 0:136], AF.Exp, reads=[pbs], writes=[bPt[m]], scale=ATT_SCALE)
                    self.op('vector', 'tensor_tensor', reads=[bPt[m], bE[k]], writes=[bPm[m]],
                            out=Pm[m][:, 0:136].rearrange("p (s t) -> p s t", t=8), in0=Pt[m][:, 0:136].rearrange("p (s t) -> p s t", t=8),
                            in1=Eh[k][:, :].rearrange("p (s t) -> p s t", t=128)[:, :, 0:8], op=ALU.mult)
                    po, pbo = self.psb[4 + 2 * (accn % 2)]
                    pz, pbz = self.psb[5 + 2 * (accn % 2)]
                    accn += 1
                    self.mm(po[:, 0:8], Vs[0:8, h * 128:(h + 1) * 128], Pm[m][0:8, 0:8], True, False, reads=[bvs, bPm[m]], writes=[pbo], skip=True)
                    self.mm(pz[:, 0:8], self.ones_bf[0:8, :], Pm[m][0:8, 0:8], True, False, reads=[bPm[m]], writes=[pbz], skip=True)
                    for s in range(1, 17):
                        ct = 16 - s
                        self.mm(po[:, 0:8], Vc[:, ct, :], Pm[m][:, s * 8:(s + 1) * 8], False, s == 16, reads=[bVc, bPm[m]], writes=[pbo], skip=True)
                        self.mm(pz[:, 0:8], self.ones_bf[:, :], Pm[m][:, s * 8:(s + 1) * 8], False, s == 16, reads=[bPm[m]], writes=[pbz], skip=True)
                    self.op('vector', 'reciprocal', reads=[pbz], writes=[brz], out=rz[:, 0:8], in_=pz[:, 0:8])
                    self.op('vector', 'tensor_tensor', reads=[pbo, brz], writes=bmix[2], out=mixedT[:, h, 1024:1032], in0=po[:, 0:8], in1=rz[:, 0:8], op=ALU.mult)
        self.ps_n = 8
        groups = self.groups(j)
        with self.phase() as es2:
            self.rmsnorm(es2, lambda ch, c0, c1: mixedT[:, ch, c0:c1], lambda ch, c0, c1: mixedT[:, ch, c0:c1], 8,
                         lambda ch: self.gcol[:, 32 + ch:33 + ch], groups, bmix, bmix, 'b')

    def sin_reduced(self, dst, x, tf, ti, rb, wb):
        V = lambda name, **kw: self.op('vector', name, reads=rb + wb, writes=wb, **kw)
        V('tensor_scalar', out=tf, in0=x, scalar1=1.0 / TWO_PI, scalar2=0.5, op0=ALU.mult, op1=ALU.add)
        V('tensor_copy', out=ti, in_=tf)
        V('tensor_copy', out=tf, in_=ti)
        V('scalar_tensor_tensor', out=x, in0=tf, scalar=-TWO_PI, in1=x, op0=ALU.mult, op1=ALU.add)
        V('tensor_scalar', out=tf, in0=x, scalar1=-math.pi, scalar2=TWO_PI, op0=ALU.is_lt, op1=ALU.mult)
        V('tensor_tensor', out=x, in0=x, in1=tf, op=ALU.add)
        V('tensor_scalar', out=tf, in0=x, scalar1=math.pi, scalar2=-TWO_PI, op0=ALU.is_gt, op1=ALU.mult)
        V('tensor_tensor', out=x, in0=x, in1=tf, op=ALU.add)
        self.act(dst, x, AF.Sin, reads=rb + wb, writes=wb)

    def phase_ssm(self, l, j, groups, samp, uT, bu, mixedT, bmix):
        nc, P = self.nc, self.P
        with self.phase() as es:
            es_s = contextlib.ExitStack()
            cols = self.sb('ssmcols', [128, 48], F32, es)
            dcol = self.sb('dcol', [128, 8], F32, es)
            pr = self.sb('ssmpr', [128, 16, 16], F32, es)
            Bb = self.sb('Bb', [128, 16, 2, 128], BF16, es)
            Cb = self.sb('Cb', [128, 16, 2, 128], BF16, es)
            wg = self.sb('wglu', [128, 4, 512], BF16, es)
            bbr = self.sb('bbr', [128, 16, 128], F32, es_s)
            bbi = self.sb('bbi', [128, 16, 128], F32, es_s)
            b0, bpr, bbb, bBb, bCb, bwg = [Buf() for _ in range(6)]
            P.dma('sync', cols[:], self.i['ssmcols'][l], writes=[b0])
            P.dma('sync', dcol[:], self.i['dcols'][l], writes=[b0])
            P.dma('sync', bbr[:], self.i['bbd_re'][l].rearrange("s p n -> p s n"), writes=[bbb])
            P.dma('sync', bbi[:], self.i['bbd_im'][l].rearrange("s p n -> p s n"), writes=[bbb])
            P.dma('gpsimd', Cb[:, :, 0, :], self.i['cbd_re'][l].rearrange("s p n -> p s n"), writes=[bCb])
            P.dma('gpsimd', Cb[:, :, 1, :], self.i['cbd_im'][l].rearrange("s p n -> p s n"), writes=[bCb])
            P.dma('gpsimd', wg[:], self.i['wglu'][l].rearrange("(kc p) n -> p kc n", p=128), writes=[bwg])
            lre, lim, lst = cols[:, 0:16], cols[:, 16:32], cols[:, 32:48]
            K = lambda i: pr[:, i, :]
            V = lambda name, **kw: self.op('vector', name, reads=[b0, bpr], writes=[bpr], **kw)
            self.act(K(0), lst, AF.Exp, reads=[b0], writes=[bpr])
            V('tensor_tensor', out=K(1), in0=lre, in1=K(0), op=ALU.mult)
            V('tensor_tensor', out=K(2), in0=lim, in1=K(0), op=ALU.mult)
            self.act(K(3), K(1), AF.Exp, reads=[bpr], writes=[bpr])
            kti = self.sb('kti', [128, 16], I32, es_s)
            V('tensor_scalar', out=K(11), in0=K(2), scalar1=TWO_PI, scalar2=None, op0=ALU.add)
            self.sin_reduced(K(5), K(11), K(12), kti[:, :], [b0], [bpr])
            V('tensor_scalar', out=K(11), in0=K(2), scalar1=TWO_PI + 0.5 * math.pi, scalar2=None, op0=ALU.add)
            self.sin_reduced(K(4), K(11), K(12), kti[:, :], [b0], [bpr])
            V('tensor_tensor', out=K(6), in0=K(3), in1=K(4), op=ALU.mult)
            V('tensor_scalar', out=K(6), in0=K(6), scalar1=-1.0, scalar2=None, op0=ALU.add)
            V('tensor_tensor', out=K(7), in0=K(3), in1=K(5), op=ALU.mult)
            V('tensor_tensor', out=K(8), in0=lre, in1=lre, op=ALU.mult)
            V('tensor_tensor', out=K(11), in0=lim, in1=lim, op=ALU.mult)
            V('tensor_tensor', out=K(8), in0=K(8), in1=K(11), op=ALU.add)
            V('reciprocal', out=K(8), in_=K(8))
            V('tensor_tensor', out=K(9), in0=K(6), in1=lre, op=ALU.mult)
            V('tensor_tensor', out=K(11), in0=K(7), in1=lim, op=ALU.mult)
            V('tensor_tensor', out=K(9), in0=K(9), in1=K(11), op=ALU.add)
            V('tensor_tensor', out=K(9), in0=K(9), in1=K(8), op=ALU.mult)
            V('tensor_tensor', out=K(10), in0=K(7), in1=lre, op=ALU.mult)
            V('tensor_tensor', out=K(11), in0=K(6), in1=lim, op=ALU.mult)
            V('tensor_tensor', out=K(10), in0=K(10), in1=K(11), op=ALU.subtract)
            V('tensor_tensor', out=K(10), in0=K(10), in1=K(8), op=ALU.mult)
            ones_f = self.sb('ones_f', [128, 128], F32, es_s)
            crow = self.sb('crow', [128, 16, 2, 128], F32, es_s)
            bcr = Buf()
            self.op('vector', 'memset', writes=[bcr], ap=ones_f[:], constant=1.0)
            dg = self.sb('dg', [128, 128], F32, es_s)
            bdg = Buf()
            for sc in range(16):
                for ri in range(2):
                    self.op('vector', 'tensor_scalar', reads=[bpr], writes=[bdg], out=dg[:, :], in0=self.ident[:, :],
                            scalar1=pr[:, 9 + ri, sc:sc + 1], scalar2=None, op0=ALU.mult)
                    pt, pb = self.ps()
                    self.mm(pt[:, 0:128], ones_f[:, :], dg[:, :], True, True, reads=[bdg, bcr], writes=[pb])
                    self.op('vector', 'tensor_copy', reads=[pb], writes=[bcr], out=crow[:, sc, ri, :], in_=pt[:, 0:128])
            tb = self.sb('tb', [128, 16, 128], F32, es_s)
            tb2 = self.sb('tb2', [128, 16, 128], F32, es_s)
            btb = Buf()
            G = lambda name, **kw: self.op('gpsimd', name, reads=[bbb, bcr, btb], writes=[btb], **kw)
            G('tensor_tensor', out=tb[:], in0=bbr[:], in1=crow[:, :, 0, :], op=ALU.mult)
            G('tensor_tensor', out=tb2[:], in0=bbi[:], in1=crow[:, :, 1, :], op=ALU.mult)
            self.op('gpsimd', 'tensor_tensor', reads=[btb], writes=[bBb], out=Bb[:, :, 0, :], in0=tb[:], in1=tb2[:], op=ALU.subtract)
            G('tensor_tensor', out=tb[:], in0=bbr[:], in1=crow[:, :, 1, :], op=ALU.mult)
            G('tensor_tensor', out=tb2[:], in0=bbi[:], in1=crow[:, :, 0, :], op=ALU.mult)
            self.op('gpsimd', 'tensor_tensor', reads=[btb], writes=[bBb], out=Bb[:, :, 1, :], in0=tb[:], in1=tb2[:], op=ALU.add)
            self.op('gpsimd', 'tensor_scalar', reads=[bCb], writes=[bCb], out=Cb[:, :, 1, :], in0=Cb[:, :, 1, :], scalar1=-1.0, scalar2=None, op0=ALU.mult)
            self.P.fence()
            es_s.close()
            iot = self.sb('iot', [128, 1025], F32, es)
            P.dma('sync', iot[:], self.i['iota'], writes=[Buf()])
            cs = self.sb('cs', [128, 1025], F32, es)
            tfl = self.sb('tfl', [128, 1025], F32, es)
            tin = self.sb('tin', [128, 1025], I32, es)
            sn = self.sb('sn', [128, 1025], F32, es)
            wr = self.sb('wr', [128, 1024], F32, es)
            wi = self.sb('wi', [128, 1024], F32, es)
            ta = self.sb('ta', [128, 1024], F32, es)
            tc = self.sb('tc', [128, 1024], F32, es)
            hr = self.sb('hr', [128, 1024], BF16, es)
            hi = self.sb('hi', [128, 1024], BF16, es)
            ini = self.sb('ini', [128, 8], F32, es)
            yT = self.sb('ysT', [128, 4, TW], BF16, es)
            yf = self.sb('ysf', [128, 4, TW], F32, es)
            bcs, bsn, bwr, bwi, bta, btc, bhr, bhi, bini, byT = [Buf() for _ in range(10)]
            self.P.fence()
            if j == 0:
                self.op('vector', 'memset', reads=[self.b_H], writes=[self.b_H], ap=self.Hst[:, 0:2, :], constant=0.0)
                if samp:
                    P.dma('sync', self.Hst[:, 2, :], self.i['sre'][l], writes=[self.b_H])
                    P.dma('sync', self.Hst[:, 3, :], self.i['sim'][l], writes=[self.b_H])
            runs = [(0, 1024, 0)] + ([(1024, 8, 2)] if samp else [])
            self.ps_n = 5
            ypsum = {}
            for sc in range(16):
                uc, oc = sc // 4, sc // 4
                th = pr[:, 2, sc:sc + 1]
                for (tab, btab, off) in ((sn, bsn, TWO_PI), (cs, bcs, TWO_PI + 0.5 * math.pi)):
                    self.op('vector', 'tensor_scalar', reads=[bpr], writes=[btab], out=tab[:, :], in0=iot[:, :], scalar1=th, scalar2=off, op0=ALU.mult, op1=ALU.add)
                    self.sin_reduced(tab[:, :], tab[:, :], tfl[:, :], tin[:, :], [bpr], [btab])
                for (t0, T, hs) in runs:
                    xs = []
                    for ri in range(2):
                        for c0 in range(0, T, 512):
                            n = min(512, T - c0)
                            pt, pb = self.ps()
                            self.mm(pt[:, 0:n], Bb[:, sc, ri, :], uT[:, uc, t0 + c0:t0 + c0 + n], True, True, reads=[bBb, bu], writes=[pb])
                            xs.append((ri, c0, n, pt, pb))
                    for (ri, c0, n, pt, pb) in xs:
                        x = pt[:, 0:n]
                        if ri == 0:
                            self.op('vector', 'tensor_tensor', reads=[pb, bcs], writes=[bwr], out=wr[:, c0:c0 + n], in0=x, in1=cs[:, c0:c0 + n], op=ALU.mult)
                            self.op('vector', 'tensor_tensor', reads=[pb, bsn], writes=[bta], out=ta[:, c0:c0 + n], in0=x, in1=sn[:, c0:c0 + n], op=ALU.mult)
                        else:
                            self.op('vector', 'tensor_tensor', reads=[pb, bcs], writes=[bwi], out=wi[:, c0:c0 + n], in0=x, in1=cs[:, c0:c0 + n], op=ALU.mult)
                            self.op('vector', 'tensor_tensor', reads=[pb, bsn], writes=[btc], out=tc[:, c0:c0 + n], in0=x, in1=sn[:, c0:c0 + n], op=ALU.mult)
                    self.op('gpsimd', 'tensor_tensor', reads=[bwr, btc], writes=[bwr], out=wr[:, 0:T], in0=wr[:, 0:T], in1=tc[:, 0:T], op=ALU.add)
                    self.op('gpsimd', 'tensor_tensor', reads=[bwi, bta], writes=[bwi], out=wi[:, 0:T], in0=wi[:, 0:T], in1=ta[:, 0:T], op=ALU.subtract)
                    Hre, Him = self.Hst[:, hs, sc:sc + 1], self.Hst[:, hs + 1, sc:sc + 1]
                    c1_, s1_ = cs[:, 1:2], sn[:, 1:2]
                    I = lambda name, **kw: self.op('vector', name, reads=[self.b_H, bcs, bsn, bini], writes=[bini], **kw)
                    I('tensor_tensor', out=ini[:, 2:3], in0=Him, in1=s1_, op=ALU.mult)
                    I('scalar_tensor_tensor', out=ini[:, 0:1], in0=Hre, scalar=c1_, in1=ini[:, 2:3], op0=ALU.mult, op1=ALU.subtract)
                    I('tensor_tensor', out=ini[:, 3:4], in0=Him, in1=c1_, op=ALU.mult)
                    I('scalar_tensor_tensor', out=ini[:, 1:2], in0=Hre, scalar=s1_, in1=ini[:, 3:4], op0=ALU.mult, op1=ALU.add)
                    rho = pr[:, 3, sc:sc + 1]
                    self.op('vector', 'tensor_tensor_scan', reads=[bwr, bini, bpr], writes=[bwr], out=wr[:, 0:T], data0=rho.to_broadcast([128, T]), data1=wr[:, 0:T],
                            initial=ini[:, 0:1], op0=ALU.mult, op1=ALU.add)
                    self.op('vector', 'tensor_tensor_scan', reads=[bwi, bini, bpr], writes=[bwi], out=wi[:, 0:T], data0=rho.to_broadcast([128, T]), data1=wi[:, 0:T],
                            initial=ini[:, 1:2], op0=ALU.mult, op1=ALU.add)
                    Gp = lambda name, rd, wrb, **kw: self.op('gpsimd', name, reads=rd, writes=wrb, **kw)
                    Gp('tensor_tensor', [bwr, bcs], [bta], out=ta[:, 0:T], in0=wr[:, 0:T], in1=cs[:, 0:T], op=ALU.mult)
                    Gp('tensor_tensor', [bwi, bsn], [btc], out=tc[:, 0:T], in0=wi[:, 0:T], in1=sn[:, 0:T], op=ALU.mult)
                    self.op('vector', 'tensor_tensor', reads=[bta, btc], writes=[bhr], out=hr[:, 0:T], in0=ta[:, 0:T], in1=tc[:, 0:T], op=ALU.subtract)
                    self.op('vector', 'tensor_tensor', reads=[bta, btc, self.b_H], writes=[self.b_H], out=Hre, in0=ta[:, T - 1:T], in1=tc[:, T - 1:T], op=ALU.subtract)
                    Gp('tensor_tensor', [bwr, bsn, bhr], [bta], out=ta[:, 0:T], in0=wr[:, 0:T], in1=sn[:, 0:T], op=ALU.mult)
                    Gp('tensor_tensor', [bwi, bcs, bhr], [btc], out=tc[:, 0:T], in0=wi[:, 0:T], in1=cs[:, 0:T], op=ALU.mult)
                    self.op('vector', 'tensor_tensor', reads=[bta, btc], writes=[bhi], out=hi[:, 0:T], in0=ta[:, 0:T], in1=tc[:, 0:T], op=ALU.add)
                    self.op('vector', 'tensor_tensor', reads=[bta, btc, self.b_H], writes=[self.b_H], out=Him, in0=ta[:, T - 1:T], in1=tc[:, T - 1:T], op=ALU.add)
                    for c0 in range(0, T, 512):
                        n = min(512, T - c0)
                        key = (oc, t0 + c0)
                        pt, pb = self.psb[5 + (0 if t0 + c0 == 0 else (1 if t0 + c0 == 512 else 2))]
                        self.mm(pt[:, 0:n], Cb[:, sc, 0, :], hr[:, c0:c0 + n], sc % 4 == 0, False, reads=[bCb, bhr], writes=[pb], skip=True)
                        self.mm(pt[:, 0:n], Cb[:, sc, 1, :], hi[:, c0:c0 + n], False, sc % 4 == 3, reads=[bCb, bhi], writes=[pb], skip=True)
                        if sc % 4 == 3:
                            self.op('vector', 'scalar_tensor_tensor', reads=[pb, bu, b0], writes=[byT], out=yf[:, oc, t0 + c0:t0 + c0 + n], in0=uT[:, oc, t0 + c0:t0 + c0 + n],
                                    scalar=dcol[:, oc:oc + 1], in1=pt[:, 0:n], op0=ALU.mult, op1=ALU.add)
                            self.op('gpsimd', 'tensor_copy', reads=[byT], writes=[byT], out=yT[:, oc, t0 + c0:t0 + c0 + n], in_=yf[:, oc, t0 + c0:t0 + c0 + n])
            self.ps_n = 8
            if j == self.NCH - 1:
                P.dma('sync', self.o['nsre'][l], self.Hst[:, 0, :], reads=[self.b_H], writes=[self.b_out])
                P.dma('sync', self.o['nsim'][l], self.Hst[:, 1, :], reads=[self.b_H], writes=[self.b_out])
            if samp:
                P.dma('sync', self.o['nssre'][l], self.Hst[:, 2, :], reads=[self.b_H], writes=[self.b_out])
                P.dma('sync', self.o['nssim'][l], self.Hst[:, 3, :], reads=[self.b_H], writes=[self.b_out])
            sg = self.sb('sg', [128, 512], F32, es)
            bsg = Buf()
            for oc2 in range(4):
                for (c0, c1, gi) in groups:
                    n = c1 - c0
                    pt, pb = self.ps()
                    for kc in range(4):
                        self.mm(pt[:, 0:n], wg[:, kc, oc2 * 128:(oc2 + 1) * 128], yT[:, kc, c0:c1], kc == 0, kc == 3, reads=[bwg, byT], writes=[pb])
                    self.act(sg[:, 0:n], pt[:, 0:n], AF.Sigmoid, reads=[pb, b0], writes=[bsg], bias=dcol[:, 4 + oc2:5 + oc2])
                    self.op('vector', 'tensor_tensor', reads=[bsg, byT], writes=bmix[gi], out=mixedT[:, 8 + oc2, c0:c1], in0=sg[:, 0:n], in1=yf[:, oc2, c0:c1], op=ALU.mult)
            with self.phase() as es2:
                self.rmsnorm(es2, lambda ch, c0, c1: mixedT[:, 8 + ch, c0:c1], lambda ch, c0, c1: mixedT[:, 8 + ch, c0:c1], 4,
                             lambda ch: self.gcol[:, 40 + ch:41 + ch], groups, bmix, bmix, 'c')
    def phase_sgu(self, l, j, samp, guT, gv, gvs, bgu, bgv, mixedT, bmix):
        nc, P = self.nc, self.P
        with self.phase() as es:
            wT = self.sb('sguw', [128, 512], F32, es)
            tri = self.sb('tri', [128, 512], F32, es)
            wTb = self.sb('sguwb', [128, 512], BF16, es)
            brow = self.sb('sgubrow', [1, 512], F32, es)
            browb = self.sb('sgubrowb', [1, 512], BF16, es)
            b1, b2 = Buf(), Buf()
            P.dma('sync', wT[:], self.i['sguwT'][l], writes=[b1])
            P.dma('sync', tri[:], self.i['trilT'], writes=[b1])
            P.dma('sync', brow[:], self.i['sgub'][l], writes=[b1])
            self.op('vector', 'tensor_tensor', reads=[b1], writes=[b2], out=wTb[:], in0=wT[:], in1=tri[:], op=ALU.mult)
            self.op('vector', 'tensor_copy', reads=[b1], writes=[b2], out=browb[:], in_=brow[:])
            for h in range(4):
                for half in range(2):
                    pt, pb = self.ps()
                    for q in range(4):
                        tt = half * 4 + q
                        self.mm(pt[:, q * 128:(q + 1) * 128], gv[:, tt, h * 128:(h + 1) * 128], wTb[:, h * 128:(h + 1) * 128], True, False, reads=[bgv, b2], writes=[pb], skip=True)
                        self.mm(pt[:, q * 128:(q + 1) * 128], self.ones_bf[0:1, :], browb[0:1, h * 128:(h + 1) * 128], False, True, reads=[b2], writes=[pb], skip=True)
                    c0 = half * 512
                    self.op('vector', 'tensor_tensor', reads=[pb, bgu], writes=bmix[half], out=mixedT[:, 12 + h, c0:c0 + 512], in0=pt[:, :], in1=guT[:, h, c0:c0 + 512], op=ALU.mult)
                if samp:
                    pt, pb = self.ps()
                    self.mm(pt[:, 0:8], gvs[0:8, h * 128:(h + 1) * 128], wTb[0:8, h * 128:h * 128 + 8], True, False, reads=[bgv, b2], writes=[pb], skip=True)
                    self.mm(pt[:, 0:8], self.ones_bf[0:1, :], browb[0:1, h * 128:h * 128 + 8], False, True, reads=[b2], writes=[pb], skip=True)
                    self.op('vector', 'tensor_tensor', reads=[pb, bgu], writes=bmix[2], out=mixedT[:, 12 + h, 1024:1032], in0=pt[:, 0:8], in1=guT[:, h, 1024:1032], op=ALU.mult)
            with self.phase() as es2:
                self.rmsnorm(es2, lambda ch, c0, c1: mixedT[:, 12 + ch, c0:c1], lambda ch, c0, c1: mixedT[:, 12 + ch, c0:c1], 4,
                             lambda ch: self.gcol[:, 44 + ch:45 + ch], self.groups(j), bmix, bmix, 'd')

    def proj_post(self, es, groups, nkc, lhs_fn, rhs_fn, rbufs_fn, wld_fn, nblk, cpb, gcol_off, tag):
        nc, P = self.nc, self.P
        xT = self.xT
        ng = len(groups)
        wid = sum(c1 - c0 for (c0, c1, gi) in groups)
        yT = self.sb('yT' + tag, [128, NKC, wid], BF16, es)
        byT = Buf()
        sq = [self.sb('psq%s%d' % (tag, k), [128, 512], BF16, es) for k in range(3)]
        bsq = [Buf() for _ in range(3)]
        rstd = self.sb('prstd' + tag, [128, wid], F32, es)
        brs = Buf()
        tmp = [self.sb('ptmp%s%d' % (tag, k), [128, 512], F32, es) for k in range(2)]
        btmp = [Buf(), Buf()]
        self.ps_n = 8 - ng
        ssq = {gi: self.psb[8 - ng + i] for i, (c0, c1, gi) in enumerate(groups)}
        offs = {}
        o = 0
        for (c0, c1, gi) in groups:
            offs[gi] = o
            o += c1 - c0
        pend = []
        sqc = 0
        for blk in range(nblk):
            wbuf, bw = wld_fn(blk)
            for cc in range(cpb):
                ch = blk * cpb + cc
                for (c0, c1, gi) in groups:
                    n = c1 - c0
                    pt, pb = self.ps()
                    for kc in range(nkc):
                        self.mm(pt[:, 0:n], lhs_fn(wbuf, kc, cc), rhs_fn(kc, c0, c1), kc == 0, kc == nkc - 1, reads=[bw.sub[kc]] + rbufs_fn(gi), writes=[pb])
                    for f in pend:
                        f()
                    pend = []
                    k = sqc % 3
                    sqc += 1
                    self.act(yT[:, ch, offs[gi]:offs[gi] + n], pt[:, 0:n], AF.Copy, reads=[pb], writes=[byT])
                    self.act(sq[k][:, 0:n], pt[:, 0:n], AF.Square, reads=[pb], writes=[bsq[k]])
                    st, sp = (ch == 0), (ch == NKC - 1)

                    def f(k=k, n=n, gi=gi, st=st, sp=sp):
                        self.mm(ssq[gi][0][:, 0:n], self.ones_bf[:, :], sq[k][:, 0:n], st, sp, reads=[bsq[k]], writes=[ssq[gi][1]], skip=True)
                    pend.append(f)
        for f in pend:
            f()
        for (c0, c1, gi) in groups:
            n = c1 - c0
            r = rstd[:, offs[gi]:offs[gi] + n]
            self.op('vector', 'tensor_scalar', reads=[ssq[gi][1]], writes=[brs], out=r, in0=ssq[gi][0][:, 0:n], scalar1=1.0 / D_MODEL, scalar2=EPS, op0=ALU.mult, op1=ALU.add)
            self.act(r, r, AF.Sqrt, reads=[brs], writes=[brs])
            self.op('vector', 'reciprocal', reads=[brs], writes=[brs], out=r, in_=r)
            for ch in range(NKC):
                k = ch % 2
                eng = 'vector' if k else 'gpsimd'
                self.op('vector', 'scalar_tensor_tensor', reads=[brs, byT, self.b_gcol], writes=[btmp[k]], out=tmp[k][:, 0:n], in0=yT[:, ch, offs[gi]:offs[gi] + n],
                        scalar=self.gcol[:, gcol_off + ch:gcol_off + ch + 1], in1=r, op0=ALU.mult, op1=ALU.mult)
                self.op(eng, 'tensor_tensor', reads=[btmp[k], self.bx[gi]], writes=[self.bx[gi]], out=xT[:, ch, c0:c1], in0=xT[:, ch, c0:c1], in1=tmp[k][:, 0:n], op=ALU.add)
        self.ps_n = 8

    def phase_out(self, l, j, groups, mixedT, bmix):
        with self.phase() as es:
            wbuf = [self.sb('wO%d' % k, [128, NKC, 512], BF16, es) for k in range(2)]
            bw = [WB(NKC), WB(NKC)]
            wsrc = self.i['w_out'][l]

            def wld(blk):
                k = blk % 2
                self.wload(wbuf[k][:], bw[k], wsrc[:, blk * 512:(blk + 1) * 512].rearrange("(kc p) n -> p kc n", p=128))
                return wbuf[k], bw[k]
            self.proj_post(es, groups, NKC, lambda w, kc, cc: w[:, kc, cc * 128:(cc + 1) * 128], lambda kc, c0, c1: mixedT[:, kc, c0:c1],
                           lambda gi: bmix[gi], wld, 4, 4, 16, 'o')

    def phase_ffn(self, l, j, groups):
        nc, P = self.nc, self.P
        xT = self.xT
        halves = [[g for g in groups if g[2] in (0, 2)], [g for g in groups if g[2] == 1]]
        for hg in halves:
            with self.phase() as es:
                wid = sum(c1 - c0 for (c0, c1, gi) in hg)
                offs = {}
                o = 0
                for (c0, c1, gi) in hg:
                    offs[gi] = o
                    o += c1 - c0
                hT = self.sb('hF', [128, NKC, wid], BF16, es)
                bh = {g[2]: [Buf()] for g in hg}
                with self.phase() as es2:
                    self.rmsnorm(es2, lambda ch, c0, c1: xT[:, ch, c0:c1],
                                 lambda ch, c0, c1: hT[:, ch, offs[0 if c0 == 0 else (2 if c0 == 1024 else 1)] :offs[0 if c0 == 0 else (2 if c0 == 1024 else 1)] + (c1 - c0)],
                                 NKC, lambda ch: self.gcol[:, 48 + ch:49 + ch], hg, {g[2]: [self.bx[g[2]]] for g in hg}, bh, 'f')
                actT = self.sb('actT', [128, NHC, wid], BF16, es)
                bact = Buf()
                with self.phase() as es3:
                    wg = [self.sb('wG%d' % k, [128, NKC, 256], BF16, es3) for k in range(2)]
                    wu = [self.sb('wU%d' % k, [128, NKC, 256], BF16, es3) for k in range(2)]
                    bwg, bwu = [WB(NKC), WB(NKC)], [WB(NKC), WB(NKC)]
                    sl = [self.sb('sl%d' % k, [128, 512], F32, es3) for k in range(2)]
                    bsl = [Buf(), Buf()]
                    cnt = 0
                    for blk in range(22):
                        k = blk % 2
                        self.wload(wg[k][:], bwg[k], self.i['w_gate'][l][:, blk * 256:(blk + 1) * 256].rearrange("(kc p) n -> p kc n", p=128))
                        self.wload(wu[k][:], bwu[k], self.i['w_up'][l][:, blk * 256:(blk + 1) * 256].rearrange("(kc p) n -> p kc n", p=128))
                        for cc in range(2):
                            hc = blk * 2 + cc
                            for (c0, c1, gi) in hg:
                                n = c1 - c0
                                pg, pbg = self.ps()
                                pu, pbu = self.ps()
                                for kc in range(NKC):
                                    self.mm(pg[:, 0:n], wg[k][:, kc, cc * 128:(cc + 1) * 128], hT[:, kc, offs[gi]:offs[gi] + n], kc == 0, kc == NKC - 1, reads=[bwg[k].sub[kc], bh[gi][0]], writes=[pbg])
                                for kc in range(NKC):
                                    self.mm(pu[:, 0:n], wu[k][:, kc, cc * 128:(cc + 1) * 128], hT[:, kc, offs[gi]:offs[gi] + n], kc == 0, kc == NKC - 1, reads=[bwu[k].sub[kc], bh[gi][0]], writes=[pbu])
                                m = cnt % 2
                                cnt += 1
                                self.act(sl[m][:, 0:n], pg[:, 0:n], AF.Silu, reads=[pbg], writes=[bsl[m]])
                                self.op('vector', 'tensor_tensor', reads=[bsl[m], pbu], writes=[bact], out=actT[:, hc, offs[gi]:offs[gi] + n], in0=sl[m][:, 0:n], in1=pu[:, 0:n], op=ALU.mult)
                with self.phase() as es4:
                    wd = [self.sb('wD%d' % k, [128, NHC, 128], BF16, es4) for k in range(2)]
                    bwd = [WB(NHC), WB(NHC)]

                    def wld(blk):
                        k = blk % 2
                        self.wload(wd[k][:], bwd[k], self.i['w_down'][l][:, blk * 128:(blk + 1) * 128].rearrange("(kc p) n -> p kc n", p=128))
                        return wd[k], bwd[k]
                    self.proj_post(es4, hg, NHC, lambda w, kc, cc: w[:, kc, :], lambda kc, c0, c1: actT[:, kc, offs[0 if c0 == 0 else (2 if c0 == 1024 else 1)]:offs[0 if c0 == 0 else (2 if c0 == 1024 else 1)] + (c1 - c0)],
                                   lambda gi: [bact], wld, 16, 1, 64, 'd')

def _t5_bucket_np(dist):
    max_exact = 16
    df = np.maximum(dist, 1).astype(np.float32)
    large = max_exact + (np.log(df / np.float32(max_exact)) / np.float32(math.log(2048 / max_exact)) * np.float32(16)).astype(np.int32)
    large = np.minimum(large, 31)
    return np.where(dist < max_exact, dist, large)


def _consts():
    c = {}
    c['ident'] = np.eye(128, dtype=np.float32)
    c['jflip'] = np.eye(128, dtype=np.float32)[::-1].copy()
    s = np.arange(128)[:, None]
    t = np.arange(128)[None, :]
    c['trilT'] = np.tile((s <= t).astype(np.float32), (1, 4))
    c['iota'] = np.tile(np.arange(1025, dtype=np.float32)[None, :], (128, 1))
    delta = np.arange(TVL) - 127
    m = ((delta >= 0) & (delta <= 128)).astype(np.int32) + ((delta >= 0) & (delta <= 512) & (delta % 4 == 0)) + \
        ((delta >= 0) & (delta <= 2048) & (delta % 16 == 0))
    c33 = np.zeros((33, TVL), np.float32)
    bk = _t5_bucket_np(np.maximum(delta, 0).astype(np.int32))
    valid = m > 0
    c33[bk[valid], np.nonzero(valid)[0]] = 1.0
    c33[32, :] = np.where(valid, np.log(np.maximum(m, 1)).astype(np.float32), np.float32(-30000.0))
    c['c33'] = c33
    return c


def _colT(a, n):
    return np.ascontiguousarray(a.reshape(n, 128).T)


_NC_CACHE = {}


def run_model(inp, DEPTH, NCH):
    f32 = np.float32
    key = (DEPTH, NCH)
    if key not in _NC_CACHE:
        _NC_CACHE[key] = KB(DEPTH, NCH)
        _NC_CACHE[key].build()
    kb = _NC_CACHE[key]
    L = NCH * TOK
    cst = _consts()
    D = DEPTH
    g = lambda k: np.asarray(inp[k], f32)
    gcols = np.stack([np.concatenate([_colT(g(k)[l], 16) for k in ('g_pre_mix', 'g_post_mix', 'g_mix_out', 'g_pre_ffn', 'g_post_ffn')], 1) for l in range(D)])
    ssmcols = np.stack([np.concatenate([_colT(g('ssm_lam_re')[l].reshape(-1), 16), _colT(g('ssm_lam_im')[l].reshape(-1), 16),
                                        _colT(np.repeat(g('ssm_log_step')[l], 64), 16)], 1) for l in range(D)])
    dcols = np.stack([np.concatenate([_colT(g('ssm_d')[l], 4), _colT(g('ssm_b_glu')[l], 4)], 1) for l in range(D)])
    bbd = {}
    for nm, src in (('bbd_re', g('ssm_b_re')), ('bbd_im', g('ssm_b_im'))):
        a = np.zeros((D, 16, 128, 128), f32)
        for sc in range(16):
            for gl in range(2):
                gg = 2 * sc + gl
                r0 = (gg % 8) * 16
                a[:, sc, r0:r0 + 16, gl * 64:(gl + 1) * 64] = np.transpose(src[:D, gg], (0, 2, 1))
        bbd[nm] = a
    for nm, src in (('cbd_re', g('ssm_c_re')), ('cbd_im', g('ssm_c_im'))):
        a = np.zeros((D, 16, 128, 128), f32)
        for sc in range(16):
            for gl in range(2):
                gg = 2 * sc + gl
                c0 = (gg % 8) * 16
                a[:, sc, gl * 64:(gl + 1) * 64, c0:c0 + 16] = np.transpose(src[:D, gg], (0, 2, 1))
        bbd[nm] = a
    sgug = np.stack([np.tile(g('sgu_g')[l][None, :], (128, 1)) for l in range(D)])
    sguwT = np.stack([np.concatenate([g('sgu_w')[l, h].T for h in range(4)], 1) for l in range(D)])
    sgub = g('sgu_b')[:D].reshape(D, 1, 512)
    relb = np.concatenate([g('rel_bias'), np.ones((1, 8), f32)], 0)
    shared = dict(w_in=g('w_in')[:D], w_out=g('w_out')[:D], w_gate=g('w_gate')[:D], w_up=g('w_up')[:D], w_down=g('w_down')[:D],
                  wglu=g('ssm_w_glu')[:D], gcols=gcols, ssmcols=ssmcols, dcols=dcols, sgug=sgug, sguwT=np.ascontiguousarray(sguwT),
                  sgub=np.ascontiguousarray(sgub), relb=relb, **bbd, **cst)
    nb = inp['x_prompt'].shape[0]
    in_maps = []
    for c in range(8):
        m = dict(shared)
        m['xp'] = np.ascontiguousarray(g('x_prompt')[c % nb, :L])
        m['xs'] = np.ascontiguousarray(g('x_sample')[c])
        m['ck'] = np.ascontiguousarray(g('cache_attn_k')[:D, c].reshape(D, 2048, 1024))
        m['cv'] = np.ascontiguousarray(g('cache_attn_v')[:D, c].reshape(D, 2048, 1024))
        m['sre'] = np.stack([_colT(g('state_ssm_re')[l, c].reshape(-1), 16) for l in range(D)])
        m['sim'] = np.stack([_colT(g('state_ssm_im')[l, c].reshape(-1), 16) for l in range(D)])
        in_maps.append(m)
    res = run_bass_kernel_spmd(kb.nc, in_maps, core_ids=list(range(8)))
    R = res.results
    NK = min(2048, L)
    uncol = lambda a: np.ascontiguousarray(a.T).reshape(32, 64)
    yp = np.stack([R[b]['yp'] for b in range(nb)])
    ys = np.stack([R[c]['ys'] for c in range(8)])
    nk = np.stack([np.stack([R[b]['nk'][l].reshape(NK, 8, 128) for b in range(nb)]) for l in range(D)])
    nv = np.stack([np.stack([R[b]['nv'][l].reshape(NK, 8, 128) for b in range(nb)]) for l in range(D)])
    nre = np.stack([np.stack([uncol(R[b]['nsre'][l]) for b in range(nb)]) for l in range(D)])
    nim = np.stack([np.stack([uncol(R[b]['nsim'][l]) for b in range(nb)]) for l in range(D)])
    nks = np.stack([np.stack([R[c]['nks'][l].reshape(8, 8, 128) for c in range(8)]) for l in range(D)])
    nvs = np.stack([np.stack([R[c]['nvs'][l].reshape(8, 8, 128) for c in range(8)]) for l in range(D)])
    nsre = np.stack([np.stack([uncol(R[c]['nssre'][l]) for c in range(8)]) for l in range(D)])
    nsim = np.stack([np.stack([uncol(R[c]['nssim'][l]) for c in range(8)]) for l in range(D)])
    nsgu = np.stack([np.stack([R[c]['nsgu'][l] for c in range(8)]) for l in range(D)])
    if KDBG:
        global DBG_OUT
        DBG_OUT = [R[c]['dbg'].reshape(128, NKC, TW) for c in range(8)]
    return (yp, ys, nk, nv, nre, nim, nks, nvs, nsre, nsim, nsgu)


def kernel(**inputs):
    return run_model(inputs, 4, 4)
```

```python
import contextlib, math
import numpy as np
import concourse.bass as bass
import concourse.mybir as mybir
from concourse.bass_utils import run_bass_kernel_spmd

F32 = mybir.dt.float32
BF16 = mybir.dt.bfloat16
I32 = mybir.dt.int32
ALU = mybir.AluOpType
AF = mybir.ActivationFunctionType
AX = mybir.AxisListType

ENGS = ['tensor', 'vector', 'scalar', 'gpsimd', 'sync']
DMA_R = 16
DEBUG_NAMES = None
import os as _os
KSTOP = int(_os.environ.get('KSTOP', '9'))
KDBG = int(_os.environ.get('KDBG', '0'))
KSUB = int(_os.environ.get('KSUB', '99'))
KVAR = int(_os.environ.get('KVAR', '0'))


class _Stop(Exception):
    pass


class Buf:
    __slots__ = ('name', 'w', 'r', 'excl')

    def __init__(self, name='', excl=False):
        self.name = name
        self.w = None
        self.r = []
        self.excl = excl


class WB:
    def __init__(self, n):
        self.sub = [Buf() for _ in range(n)]


class Ins:
    __slots__ = ('eng', 'fn', 'deps', 'kind', 'dma_no', 'marked', 'cnt')

    def __init__(self, eng, fn, kind):
        self.eng = eng
        self.fn = fn
        self.kind = kind
        self.deps = []
        self.marked = False
        self.cnt = 0
        self.dma_no = -1


class Prog:
    def __init__(self, nc):
        self.nc = nc
        self.q = {e: [] for e in ENGS}
        self.ndma = {e: 0 for e in ENGS}
        self.same_engine_sync = {'vector', 'scalar', 'gpsimd'}
        self.fence_deps = {e: [] for e in ENGS}
        self.since_fence_dma = []

    def emit(self, eng, fn, reads=(), writes=(), kind='c'):
        ins = Ins(eng, fn, kind)
        if kind == 'd':
            ins.dma_no = self.ndma[eng]
            self.ndma[eng] += 1
            self.since_fence_dma.append(ins)
        deps = list(self.fence_deps[eng])
        self.fence_deps[eng] = []
        for b in reads:
            if b.w is not None:
                deps.append(b.w)
            if b.excl:
                deps.extend(x for x in b.r if x.eng != eng)
        for b in writes:
            if b.w is not None:
                deps.append(b.w)
            deps.extend(b.r)
        for d in deps:
            if d is ins:
                continue
            if d.eng == eng and d.kind == 'c' and kind == 'c' and eng not in self.same_engine_sync:
                continue
            ins.deps.append(d)
        for b in reads:
            b.r.append(ins)
        for b in writes:
            b.w = ins
            b.r = []
        self.q[eng].append(ins)
        return ins

    def fence(self):
        tails = []
        for e in ENGS:
            for ins in reversed(self.q[e]):
                if ins.kind == 'c':
                    tails.append(ins)
                    break
        tails.extend(self.since_fence_dma)
        self.since_fence_dma = []
        for e in ENGS:
            self.fence_deps[e] = self.fence_deps[e] + tails

    def dma(self, eng, out, in_, reads=(), writes=(), **kw):
        return self.emit(eng, lambda e: e.dma_start(out=out, in_=in_, **kw), reads, writes, kind='d')

    def finalize(self):
        nc = self.nc
        for e in ENGS:
            for ins in self.q[e]:
                for d in ins.deps:
                    d.marked = True
        for e in ENGS:
            c = 0
            for ins in self.q[e]:
                if ins.kind == 'c' and ins.marked:
                    c += 1
                ins.cnt = c
        with contextlib.ExitStack() as es:
            esem = {e: es.enter_context(nc.semaphore('s_' + e)) for e in ENGS}
            dsem = {e: [es.enter_context(nc.semaphore('d_%s_%d' % (e, i))) for i in range(DMA_R)]
                    for e in ENGS if self.ndma[e] > 0}
            block = es.enter_context(nc.Block())
            prog = self

            def run_engine(ename, eng):
                seen_e = {e2: 0 for e2 in ENGS}
                seen_d = {}
                for ins in prog.q[ename]:
                    need_e = {}
                    need_d = {}
                    for d in ins.deps:
                        if d.kind == 'c':
                            if d.cnt > seen_e[d.eng] and d.cnt > need_e.get(d.eng, 0):
                                need_e[d.eng] = d.cnt
                        else:
                            key = (d.eng, d.dma_no % DMA_R)
                            val = 16 * (d.dma_no // DMA_R + 1)
                            if val > seen_d.get(key, 0) and val > need_d.get(key, 0):
                                need_d[key] = val
                    if ins.kind == 'd' and ins.dma_no >= DMA_R:
                        key = (ename, ins.dma_no % DMA_R)
                        val = 16 * (ins.dma_no // DMA_R)
                        if val > seen_d.get(key, 0) and val > need_d.get(key, 0):
                            need_d[key] = val
                    for e2, v in need_e.items():
                        eng.wait_ge(esem[e2], v)
                        seen_e[e2] = v
                    for key, v in need_d.items():
                        eng.wait_ge(dsem[key[0]][key[1]], v)
                        seen_d[key] = v
                    r = ins.fn(eng)
                    if DEBUG_NAMES is not None:
                        try:
                            DEBUG_NAMES.append((r.ins.name if hasattr(r, 'ins') else getattr(r, 'name', '?'), r.concise()[:400]))
                        except Exception as ex:
                            DEBUG_NAMES.append(str(ex))
                    if ins.kind == 'c':
                        if ins.marked:
                            r.then_inc(esem[ename], 1)
                    else:
                        r.then_inc(dsem[ename][ins.dma_no % DMA_R], 16)
                if prog.ndma[ename] > 0:
                    n = prog.ndma[ename]
                    for k in range(DMA_R):
                        cntk = len(range(k, n, DMA_R))
                        if cntk > 0 and 16 * cntk > seen_d.get((ename, k), 0):
                            eng.wait_ge(dsem[ename][k], 16 * cntk)

            @block.tensor
            def _(eng):
                run_engine('tensor', eng)

            @block.vector
            def _(eng):
                run_engine('vector', eng)

            @block.scalar
            def _(eng):
                run_engine('scalar', eng)

            @block.gpsimd
            def _(eng):
                run_engine('gpsimd', eng)

            @block.sync
            def _(eng):
                run_engine('sync', eng)


D_MODEL = 2048
NKC = 16
ATT_H = 8
IN_COLS = 4608
FFN_H = 5632
NHC = 44
EPS = 1e-6
ATT_SCALE = 128 ** -0.5
TOK = 1024
TW = 1032
ETW = 17 * 128
TVL = ETW + 127
TWO_PI = 2.0 * math.pi
class KB:
    def __init__(self, DEPTH, NCH):
        self.DEPTH = DEPTH
        self.NCH = NCH
        self.L = NCH * TOK
        self.NKEEP = min(2048, self.L)
        nc = bass.Bass("TRN2", target_bir_lowering=False)
        self.nc = nc
        self.P = Prog(nc)
        self.es = contextlib.ExitStack()
        self.rr = 0
        D = DEPTH
        L = self.L

        def din(name, shape, dt=F32):
            return nc.dram_tensor(name, list(shape), dt, kind="ExternalInput").ap()

        def dout(name, shape, dt=F32):
            return nc.dram_tensor(name, list(shape), dt, kind="ExternalOutput").ap()

        self.i = dict(
            xp=din('xp', [L, 2048]), xs=din('xs', [8, 2048]),
            ck=din('ck', [D, 2048, 1024]), cv=din('cv', [D, 2048, 1024]),
            sre=din('sre', [D, 128, 16]), sim=din('sim', [D, 128, 16]),
            w_in=din('w_in', [D, 2048, IN_COLS]), w_out=din('w_out', [D, 2048, 2048]),
            w_gate=din('w_gate', [D, 2048, FFN_H]), w_up=din('w_up', [D, 2048, FFN_H]),
            w_down=din('w_down', [D, FFN_H, 2048]), wglu=din('wglu', [D, 512, 512]),
            gcols=din('gcols', [D, 128, 5 * 16]), ssmcols=din('ssmcols', [D, 128, 3 * 16]),
            dcols=din('dcols', [D, 128, 8]),
            bbd_re=din('bbd_re', [D, 16, 128, 128]), bbd_im=din('bbd_im', [D, 16, 128, 128]),
            cbd_re=din('cbd_re', [D, 16, 128, 128]), cbd_im=din('cbd_im', [D, 16, 128, 128]),
            sgug=din('sgug', [D, 128, 512]), sguwT=din('sguwT', [D, 128, 512]), sgub=din('sgub', [D, 1, 512]),
            ident=din('ident', [128, 128]), jflip=din('jflip', [128, 128]), trilT=din('trilT', [128, 512]),
            iota=din('iota', [128, 1025]), c33=din('c33', [33, TVL]), relb=din('relb', [33, 8]),
        )
        self.o = dict(
            yp=dout('yp', [L, 2048]), ys=dout('ys', [8, 2048]),
            nk=dout('nk', [D, self.NKEEP, 1024]), nv=dout('nv', [D, self.NKEEP, 1024]),
            nsre=dout('nsre', [D, 128, 16]), nsim=dout('nsim', [D, 128, 16]),
            nks=dout('nks', [D, 8, 1024]), nvs=dout('nvs', [D, 8, 1024]),
            nssre=dout('nssre', [D, 128, 16]), nssim=dout('nssim', [D, 128, 16]),
            nsgu=dout('nsgu', [D, 8, 512]),
        )
        if KDBG:
            self.o['dbg'] = dout('dbg', [128, NKC * TW])
        self.xscr = nc.dram_tensor('xscr', [2048, L], F32).ap()
        self.ktscr = nc.dram_tensor('ktscr', [1024, L], BF16).ap()
        self.vscr = nc.dram_tensor('vscr', [L, 1024], BF16).ap()
        self.etv = nc.dram_tensor('etv', [8, TVL], F32)
        self.ett = nc.dram_tensor('ett', [8, 128, ETW], BF16).ap()
        self.b_xscr = [Buf() for _ in range(NCH)]
        self.b_kt = [Buf() for _ in range(NCH)]
        self.b_v = [Buf() for _ in range(NCH)]
        self.b_out = Buf()

    def sb(self, name, shape, dt, es=None):
        self.uid = getattr(self, 'uid', 0) + 1
        return (es or self.es).enter_context(self.nc.sbuf_tensor('%s_u%d' % (name, self.uid), list(shape), dt))

    def op(self, eng, name, reads=(), writes=(), **kw):
        return self.P.emit(eng, lambda e: getattr(e, name)(**kw), reads, writes)

    def mm(self, out, lhsT, rhs, start, stop, reads=(), writes=(), skip=False):
        if skip:
            return self.P.emit('tensor', lambda e: e.matmul(out, lhsT=lhsT, rhs=rhs, start=start, stop=stop, skip_group_check=True), reads, writes)
        return self.P.emit('tensor', lambda e: e.matmul(out, lhsT=lhsT, rhs=rhs, start=start, stop=stop), reads, writes)

    def act(self, out, in_, func, reads=(), writes=(), **kw):
        return self.P.emit('scalar', lambda e: e.activation(out=out, in_=in_, func=func, **kw), reads, writes)

    def ps(self):
        k = self.rr % getattr(self, 'ps_n', 8)
        self.rr += 1
        return self.psb[k]

    def alt(self):
        self.altc = getattr(self, 'altc', 0) + 1
        return 'vector' if self.altc % 2 else 'gpsimd'

    @contextlib.contextmanager
    def phase(self):
        es = contextlib.ExitStack()
        with es:
            yield es
        self.P.fence()

    def build(self):
        nc, P = self.nc, self.P
        with self.es:
            self.psb = []
            for k in range(8):
                t = self.es.enter_context(nc.psum_tensor('ps%d' % k, [128, 512], F32))
                self.psb.append((t, Buf('ps%d' % k, excl=True)))
            self.xT = self.sb('xT', [128, NKC, TW], F32)
            self.bx = [Buf('x0'), Buf('x1'), Buf('xs')]
            self.xS = self.sb('xS', [128, NKC, 8], F32)
            self.ident = self.sb('ident', [128, 128], F32)
            self.ones_bf = self.sb('ones_bf', [128, 128], BF16)
            self.gcol = self.sb('gcol', [128, 80], F32)
            self.b_gcol = Buf('gcol')
            self.Hst = self.sb('Hst', [128, 4, 16], F32)
            self.b_H = Buf('H')
            self.wstg = self.sb('wstg', [128, 4, 512], F32)
            self.bstg = [Buf() for _ in range(4)]
            self.stg_i = 0
            self.cur_stg = (self.wstg, self.bstg, 4)
            self.setup()
            try:
                for l in range(self.DEPTH):
                    P.dma('sync', self.gcol[:], self.i['gcols'][l], writes=[self.b_gcol])
                    for j in range(self.NCH):
                        self.chunk_layer(l, j)
            except _Stop:
                pass
            P.finalize()
        return nc

    def groups(self, j):
        g = [(0, 512, 0), (512, 1024, 1)]
        if j == 0:
            g.append((1024, 1032, 2))
        return g

    def setup(self):
        nc, P = self.nc, self.P
        b = Buf()
        P.dma('sync', self.ident[:], self.i['ident'], writes=[b])
        self.op('vector', 'memset', writes=[b], ap=self.ones_bf[:], constant=1.0)
        with self.phase() as es:
            c33 = self.sb('c33', [33, TVL], F32, es)
            relb = self.sb('relb', [33, 8], F32, es)
            jf = self.sb('jf', [128, 128], F32, es)
            tv = self.sb('tv', [8, TVL], F32, es)
            hk = self.sb('hk', [128, ETW], F32, es)
            tt = self.sb('tt', [128, ETW], BF16, es)
            b1, b2, b3, b4, b5, b6, b7 = [Buf() for _ in range(7)]
            P.dma('sync', c33[:], self.i['c33'], writes=[b1])
            P.dma('sync', relb[:], self.i['relb'], writes=[b2])
            P.dma('sync', jf[:], self.i['jflip'], writes=[b3])
            c0 = 0
            while c0 < TVL:
                n = min(512, TVL - c0)
                pt, pb = self.ps()
                self.mm(pt[0:8, 0:n], relb[:, :], c33[:, c0:c0 + n], True, True, reads=[b1, b2], writes=[pb])
                self.act(tv[:, c0:c0 + n], pt[0:8, 0:n], AF.Exp, reads=[pb], writes=[b4])
                c0 += n
            P.dma('sync', self.etv.ap(), tv[:], reads=[b4], writes=[b5])
            for h in range(8):
                src = bass.AP(self.etv, h * TVL, [[1, 128], [1, ETW]])
                P.dma('sync', hk[:], src, reads=[b5], writes=[b6])
                c0 = 0
                while c0 < ETW:
                    n = min(512, ETW - c0)
                    pt, pb = self.ps()
                    self.mm(pt[:, 0:n], jf[:, :], hk[:, c0:c0 + n], True, True, reads=[b3, b6], writes=[pb])
                    self.op('vector', 'tensor_copy', reads=[pb], writes=[b7], out=tt[:, c0:c0 + n], in_=pt[:, 0:n])
                    c0 += n
                P.dma('sync', self.ett[h], tt[:], reads=[b7], writes=[Buf()])

    def rmsnorm(self, es, src_fn, dst_fn, nch, gcol_fn, groups, rbufs, wbufs, tag, in_dt=F32):
        sq = [self.sb('sq%s%d' % (tag, k), [128, 512], BF16, es) for k in range(2)]
        bsq = [Buf(), Buf()]
        rstd = self.sb('rstd' + tag, [128, 512], F32, es)
        brs = Buf()
        for (c0, c1, gi) in groups:
            n = c1 - c0
            pt, pb = self.ps()
            for ch in range(nch):
                k = ch % 2
                self.act(sq[k][:, 0:n], src_fn(ch, c0, c1), AF.Square, reads=rbufs[gi], writes=[bsq[k]])
                self.mm(pt[:, 0:n], self.ones_bf[:, :], sq[k][:, 0:n], ch == 0, ch == nch - 1, reads=[bsq[k]], writes=[pb])
            self.op('vector', 'tensor_scalar', reads=[pb], writes=[brs], out=rstd[:, 0:n], in0=pt[:, 0:n],
                    scalar1=1.0 / (nch * 128), scalar2=EPS, op0=ALU.mult, op1=ALU.add)
            self.act(rstd[:, 0:n], rstd[:, 0:n], AF.Sqrt, reads=[brs], writes=[brs])
            self.op('vector', 'reciprocal', reads=[brs], writes=[brs], out=rstd[:, 0:n], in_=rstd[:, 0:n])
            for ch in range(nch):
                self.op('vector', 'scalar_tensor_tensor', reads=[brs, self.b_gcol] + list(rbufs[gi]), writes=wbufs[gi],
                        out=dst_fn(ch, c0, c1), in0=src_fn(ch, c0, c1), scalar=gcol_fn(ch), in1=rstd[:, 0:n],
                        op0=ALU.mult, op1=ALU.mult)
    def wload(self, wt, wb, src):
        nk, ncol = wt.shape[1], wt.shape[2]
        g = max(1, 512 // ncol)
        for k0 in range(0, nk, g):
            k1 = min(nk, k0 + g)
            stg_t, stg_b, stg_n = self.cur_stg
            i = self.stg_i % stg_n
            self.stg_i += 1
            st = stg_t[:, i, 0:(k1 - k0) * ncol].rearrange("p (a b) -> p a b", b=ncol)
            self.P.dma('sync', st, src[:, k0:k1, :], writes=[stg_b[i]])
            eng = ('gpsimd', 'scalar', 'vector', 'scalar', 'vector')[self.stg_i % 5]
            subs = [wb.sub[kc] for kc in range(k0, k1)]
            if eng == 'scalar':
                self.act(wt[:, k0:k1, :], st, AF.Copy, reads=[stg_b[i]], writes=subs)
            else:
                self.op(eng, 'tensor_copy', reads=[stg_b[i]], writes=subs, out=wt[:, k0:k1, :], in_=st)

    def gelu_from_psum(self, es_bufs, pt, pb, rows, n, out_ap, wb_out):
        t1, t2, b1, b2 = es_bufs
        x = pt[0:rows, 0:n]
        self.act(t1[0:rows, 0:n], x, AF.Square, reads=[pb], writes=[b1])
        self.op('vector', 'tensor_scalar', reads=[b1], writes=[b1], out=t1[0:rows, 0:n], in0=t1[0:rows, 0:n],
                scalar1=0.044715, scalar2=1.0, op0=ALU.mult, op1=ALU.add)
        self.op('vector', 'tensor_tensor', reads=[b1, pb], writes=[b2], out=t2[0:rows, 0:n], in0=t1[0:rows, 0:n], in1=x, op=ALU.mult)
        self.act(t1[0:rows, 0:n], t2[0:rows, 0:n], AF.Sigmoid, reads=[b2], writes=[b1], scale=1.5957691216057308)
        self.op('vector', 'tensor_tensor', reads=[b1, pb], writes=wb_out, out=out_ap, in0=t1[0:rows, 0:n], in1=x, op=ALU.mult)

    def chunk_layer(self, l, j):
        nc, P = self.nc, self.P
        base = j * TOK
        groups = self.groups(j)
        samp = (j == 0)
        xT = self.xT
        last = (l == self.DEPTH - 1)
        self.cur_stg = (self.wstg, self.bstg, 4)
        with self.phase() as es:
            if l == 0:
                xin = [self.sb('xin%d' % k, [128, 2048], F32, es) for k in range(2)]
                bxin = [Buf(), Buf()]
                for tt in range(8):
                    k = tt % 2
                    P.dma('sync', xin[k][:], self.i['xp'][base + tt * 128: base + (tt + 1) * 128, :], writes=[bxin[k]])
                    for c4 in range(4):
                        pt, pb = self.ps()
                        for q in range(4):
                            ch = c4 * 4 + q
                            self.mm(pt[:, q * 128:(q + 1) * 128], xin[k][:, ch * 128:(ch + 1) * 128], self.ident[:, :], True, True, reads=[bxin[k]], writes=[pb])
                        self.op('vector', 'tensor_copy', reads=[pb], writes=[self.bx[tt // 4]],
                                out=xT[:, c4 * 4:(c4 + 1) * 4, tt * 128:(tt + 1) * 128],
                                in_=pt[:, :].rearrange("p (q t) -> p q t", q=4))
                if samp:
                    P.dma('sync', xin[0][0:8, :], self.i['xs'], writes=[bxin[0]])
                    for c4 in range(4):
                        pt, pb = self.ps()
                        for q in range(4):
                            ch = c4 * 4 + q
                            self.mm(pt[:, q * 8:(q + 1) * 8], xin[0][0:8, ch * 128:(ch + 1) * 128], self.ident[0:8, 0:8], True, True, reads=[bxin[0]], writes=[pb])
                        self.op('vector', 'tensor_copy', reads=[pb], writes=[self.bx[2]],
                                out=xT[:, c4 * 4:(c4 + 1) * 4, 1024:1032], in_=pt[:, 0:32].rearrange("p (q t) -> p q t", q=4))
            else:
                for ch in range(NKC):
                    P.dma('sync', xT[:, ch, 0:1024], self.xscr[ch * 128:(ch + 1) * 128, base:base + 1024],
                          reads=[self.b_xscr[j]], writes=[self.bx[0], self.bx[1]])
                if samp:
                    self.op('vector', 'tensor_copy', writes=[self.bx[2]], out=xT[:, :, 1024:1032], in_=self.xS[:, :, :])
        with self.phase() as es_mix:
            hm = self.sb('hm', [128, NKC, TW], BF16, es_mix)
            mixedT = hm
            bmix = {gi: [Buf()] for gi in range(3)}
            with self.phase() as es_u:
                uT = self.sb('uT', [128, 4, TW], BF16, es_u)
                bu = Buf()
                with self.phase() as es_g:
                    guT = self.sb('guT', [128, 4, TW], BF16, es_g)
                    gv = self.sb('gv', [128, 8, 512], BF16, es_g)
                    gvs = self.sb('gvs', [8, 512], BF16, es_g)
                    bgu, bgv = Buf(), Buf()
                    with self.phase() as es_q:
                        qT = self.sb('qT', [128, 8, TW], BF16, es_q)
                        KTs = self.sb('KTs', [128, 8, 8], BF16, es_q)
                        Vs = self.sb('Vs', [8, 1024], BF16, es_q)
                        bq, bkts, bvs = Buf(), Buf(), Buf()
                        if KSTOP >= 1:
                            self.phase_A(l, j, groups, samp, hm, qT, uT, guT, gv, gvs, KTs, Vs, bq, bu, bgu, bgv, bkts, bvs)
                        if KSTOP >= 2:
                            self.phase_att(l, j, samp, qT, KTs, Vs, mixedT, bmix, bq, bkts, bvs)
                    if KSTOP >= 3:
                        self.phase_sgu(l, j, samp, guT, gv, gvs, bgu, bgv, mixedT, bmix)
                if KSTOP >= 4:
                    self.phase_ssm(l, j, groups, samp, uT, bu, mixedT, bmix)
            if KDBG and l == 0 and j == 0:
                self.P.fence()
                self.P.dma('gpsimd', self.o['dbg'], hm[:].rearrange("p a b -> p (a b)"), writes=[self.b_out])
                self.P.fence()
            if KSTOP >= 5:
                self.phase_out(l, j, groups, mixedT, bmix)
            self.cur_stg = (self.wstg, self.bstg, 4)
        if KSTOP >= 6:
            self.phase_ffn(l, j, groups)
        with self.phase() as es:
            if samp:
                self.op('vector', 'tensor_copy', reads=[self.bx[2]], writes=[Buf()], out=self.xS[:, :, :], in_=xT[:, :, 1024:1032])
            if not last:
                for ch in range(NKC):
                    P.dma('sync', self.xscr[ch * 128:(ch + 1) * 128, base:base + 1024], xT[:, ch, 0:1024],
                          reads=[self.bx[0], self.bx[1]], writes=[self.b_xscr[j]])
            else:
                yo = [self.sb('yo%d' % k, [128, 2048], F32, es) for k in range(2)]
                byo = [Buf(), Buf()]
                for tt in range(8):
                    k = tt % 2
                    for c4 in range(4):
                        pt, pb = self.ps()
                        for q in range(4):
                            ch = c4 * 4 + q
                            self.mm(pt[:, q * 128:(q + 1) * 128], xT[:, ch, tt * 128:(tt + 1) * 128], self.ident[:, :], True, True,
                                    reads=[self.bx[tt // 4]], writes=[pb])
                        self.op('vector', 'tensor_copy', reads=[pb], writes=[byo[k]], out=yo[k][:, c4 * 512:(c4 + 1) * 512], in_=pt[:, :])
                    P.dma('sync', self.o['yp'][base + tt * 128: base + (tt + 1) * 128, :], yo[k][:], reads=[byo[k]], writes=[self.b_out])
                if samp:
                    for c4 in range(4):
                        pt, pb = self.ps()
                        for q in range(4):
                            ch = c4 * 4 + q
                            self.mm(pt[0:8, q * 128:(q + 1) * 128], xT[:, ch, 1024:1032], self.ident[:, :], True, True, reads=[self.bx[2]], writes=[pb])
                        self.op('vector', 'tensor_copy', reads=[pb], writes=[byo[0]], out=yo[0][0:8, c4 * 512:(c4 + 1) * 512], in_=pt[0:8, :])
                    P.dma('sync', self.o['ys'], yo[0][0:8, :], reads=[byo[0]], writes=[self.b_out])

    def phase_A(self, l, j, groups, samp, hT, qT, uT, guT, gv, gvs, KTs, Vs, bq, bu, bgu, bgv, bkts, bvs):
        nc, P = self.nc, self.P
        base = j * TOK
        xT = self.xT
        keep_lo = self.L - self.NKEEP
        with self.phase() as es:
            bh = {gi: [Buf()] for gi in range(3)}
            with self.phase() as es2:
                self.rmsnorm(es2, lambda ch, c0, c1: xT[:, ch, c0:c1], lambda ch, c0, c1: hT[:, ch, c0:c1], NKC,
                             lambda ch: self.gcol[:, ch:ch + 1], groups, {gi: [self.bx[gi]] for gi in range(3)}, bh, 'a')
            allh = [bh[g[2]][0] for g in groups]
            wbuf = [self.sb('wA%d' % k, [128, NKC, 512], BF16, es) for k in range(2)]
            bw = [WB(NKC), WB(NKC)]
            kst = [self.sb('kst%d' % k, [128, 1024], BF16, es) for k in range(2)]
            bkst = [Buf(), Buf()]
            vst = [self.sb('vst%d' % k, [128, 512], BF16, es) for k in range(2)]
            bvst = [Buf(), Buf()]
            fst = [self.sb('fst%d' % k, [128, 512], F32, es) for k in range(2)]
            bfst = [Buf(), Buf()]
            t1 = self.sb('gt1', [128, 512], F32, es)
            t2 = self.sb('gt2', [128, 512], F32, es)
            gb = (t1, t2, Buf(), Buf())
            lnx = self.sb('lnx', [128, 512], F32, es)
            blnx = Buf()
            lnj = self.sb('lnj', [128, 512], F32, es)
            sm = self.sb('lnsm', [128, 8], F32, es)
            bsm = Buf()
            sgug = self.sb('sgug', [128, 512], F32, es)
            bsg = Buf()
            P.dma('sync', sgug[:], self.i['sgug'][l], writes=[bsg])
            win = self.i['w_in'][l]
            cnt = 0
            for blk in range(9):
                if blk >= KSUB:
                    break
                k = blk % 2
                self.wload(wbuf[k][:], bw[k], win[:, blk * 512:(blk + 1) * 512].rearrange("(kc p) n -> p kc n", p=128))
                fm = blk in (0, 1, 2, 3, 6, 7)
                if fm:
                    for cc in range(4):
                        head = (blk % 2) * 4 + cc
                        for (c0, c1, gi) in groups:
                            n = c1 - c0
                            pt, pb = self.ps()
                            for kc in range(NKC):
                                self.mm(pt[:, 0:n], wbuf[k][:, kc, cc * 128:(cc + 1) * 128], hT[:, kc, c0:c1], kc == 0, kc == NKC - 1,
                                        reads=[bw[k].sub[kc], bh[gi][0]], writes=[pb])
                            if blk in (0, 1):
                                self.act(qT[:, head, c0:c1], pt[:, 0:n], AF.Copy, reads=[pb], writes=[bq])
                            elif blk in (2, 3):
                                if gi < 2:
                                    kk = head % 2
                                    self.act(kst[kk][:, c0:c1], pt[:, 0:n], AF.Copy, reads=[pb], writes=[bkst[kk]])
                                    if gi == 1:
                                        P.dma('sync', self.ktscr[head * 128:(head + 1) * 128, base:base + 1024], kst[kk][:, :],
                                              reads=[bkst[kk]], writes=[self.b_kt[j]])
                                else:
                                    self.act(KTs[:, head, :], pt[:, 0:n], AF.Copy, reads=[pb], writes=[bkts])
                            elif blk == 6:
                                self.act(uT[:, cc, c0:c1], pt[:, 0:n], AF.Copy, reads=[pb], writes=[bu])
                            else:
                                self.gelu_from_psum(gb, pt, pb, 128, n, guT[:, cc, c0:c1], [bgu])
                tokmaj = blk in (4, 5, 8) or (blk in (2, 3))
                if tokmaj:
                    tts = list(range(8)) + ([8] if samp else [])
                    for tt in tts:
                        rows = 128 if tt < 8 else 8
                        tok0 = base + tt * 128
                        if blk in (2, 3):
                            if tt < 8 and tok0 < keep_lo:
                                continue
                        c0 = tt * 128 if tt < 8 else 1024
                        gi = (tt // 4) if tt < 8 else 2
                        pt, pb = self.ps()
                        for kc in range(NKC):
                            self.mm(pt[0:rows, :], hT[:, kc, c0:c0 + rows], wbuf[k][:, kc, :], kc == 0, kc == NKC - 1,
                                    reads=[bw[k].sub[kc], bh[gi][0]], writes=[pb])
                        half = blk % 2
                        if blk in (2, 3, 4, 5):
                            isk = blk in (2, 3)
                            if tt < 8:
                                if not isk and KVAR != 1:
                                    kk = cnt % 2
                                    self.op('vector', 'tensor_copy', reads=[pb], writes=[bvst[kk]], out=vst[kk][:, :], in_=pt[:, :])
                                    P.dma('sync', self.vscr[tok0:tok0 + 128, half * 512:(half + 1) * 512], vst[kk][:, :],
                                          reads=[bvst[kk]], writes=[self.b_v[j]])
                                if tok0 >= keep_lo:
                                    kk = cnt % 2
                                    self.act(fst[kk][:, :], pt[:, :], AF.Copy, reads=[pb], writes=[bfst[kk]])
                                    dst = self.o['nk' if isk else 'nv'][l, tok0 - keep_lo: tok0 - keep_lo + 128, half * 512:(half + 1) * 512]
                                    P.dma('sync', dst, fst[kk][:, :], reads=[bfst[kk]], writes=[self.b_out])
                                cnt += 1
                            else:
                                kk = cnt % 2
                                cnt += 1
                                if not isk and KVAR != 2:
                                    self.op('vector', 'tensor_copy', reads=[pb], writes=[bvs], out=Vs[:, half * 512:(half + 1) * 512], in_=pt[0:8, :])
                                self.act(fst[kk][0:8, :], pt[0:8, :], AF.Copy, reads=[pb], writes=[bfst[kk]])
                                dst = self.o['nks' if isk else 'nvs'][l, :, half * 512:(half + 1) * 512]
                                P.dma('sync', dst, fst[kk][0:8, :], reads=[bfst[kk]], writes=[self.b_out])
                        else:
                            self.gelu_from_psum(gb, pt, pb, rows, 512, lnx[0:rows, :], [blnx])
                            self.op('vector', 'tensor_reduce', reads=[blnx], writes=[bsm], out=sm[0:rows, 0:1], in_=lnx[0:rows, :], axis=AX.X, op=ALU.add)
                            self.act(lnj[0:rows, :], lnx[0:rows, :], AF.Square, reads=[blnx], writes=[gb[2]])
                            self.op('vector', 'tensor_reduce', reads=[gb[2]], writes=[bsm], out=sm[0:rows, 1:2], in_=lnj[0:rows, :], axis=AX.X, op=ALU.add)
                            self.op('vector', 'tensor_scalar', reads=[bsm], writes=[bsm], out=sm[0:rows, 2:3], in0=sm[0:rows, 0:1], scalar1=1.0 / 512, scalar2=None, op0=ALU.mult)
                            self.op('vector', 'tensor_tensor', reads=[bsm], writes=[bsm], out=sm[0:rows, 3:4], in0=sm[0:rows, 2:3], in1=sm[0:rows, 2:3], op=ALU.mult)
                            self.op('vector', 'scalar_tensor_tensor', reads=[bsm], writes=[bsm], out=sm[0:rows, 4:5], in0=sm[0:rows, 1:2], scalar=1.0 / 512, in1=sm[0:rows, 3:4], op0=ALU.mult, op1=ALU.subtract)
                            self.op('vector', 'tensor_scalar', reads=[bsm], writes=[bsm], out=sm[0:rows, 5:6], in0=sm[0:rows, 4:5], scalar1=EPS, scalar2=None, op0=ALU.add)
                            self.act(sm[0:rows, 5:6], sm[0:rows, 5:6], AF.Sqrt, reads=[bsm], writes=[bsm])
                            self.op('vector', 'reciprocal', reads=[bsm], writes=[bsm], out=sm[0:rows, 5:6], in_=sm[0:rows, 5:6])
                            self.op('vector', 'tensor_scalar', reads=[blnx, bsm], writes=[blnx], out=lnx[0:rows, :], in0=lnx[0:rows, :],
                                    scalar1=sm[0:rows, 2:3], scalar2=sm[0:rows, 5:6], op0=ALU.subtract, op1=ALU.mult)
                            if tt < 8:
                                self.op('vector', 'tensor_tensor', reads=[blnx, bsg], writes=[bgv], out=gv[:, tt, :], in0=lnx[:, :], in1=sgug[:, :], op=ALU.mult)
                            else:
                                kk = 0
                                self.op('vector', 'tensor_tensor', reads=[blnx, bsg], writes=[bfst[kk]], out=fst[kk][0:8, :], in0=lnx[0:8, :], in1=sgug[0:8, :], op=ALU.mult)
                                self.op('vector', 'tensor_copy', reads=[bfst[kk]], writes=[bgv], out=gvs[:, :], in_=fst[kk][0:8, :])
                                P.dma('sync', self.o['nsgu'][l], fst[kk][0:8, :], reads=[bfst[kk]], writes=[self.b_out])
    def phase_att(self, l, j, samp, qT, KTs, Vs, mixedT, bmix, bq, bkts, bvs):
        nc, P = self.nc, self.P
        base = j * TOK
        lo_tile = max(0, 16 - 8 * j)
        lo_tok = base - 2048 + lo_tile * 128
        ntile = 24 - lo_tile
        with self.phase() as es:
            KTh = [self.sb('KTh%d' % k, [128, 3072], BF16, es) for k in range(2)]
            Vh = [self.sb('Vh%d' % k, [128, 24, 128], BF16, es) for k in range(2)]
            Eh = [self.sb('Eh%d' % k, [128, ETW], BF16, es) for k in range(2)]
            bKT, bV, bE = [Buf(), Buf()], [Buf(), Buf()], [Buf(), Buf()]
            Pt = [self.sb('Pt%d' % k, [128, 512], BF16, es) for k in range(3)]
            Pm = [self.sb('Pm%d' % k, [128, 512], BF16, es) for k in range(3)]
            bPt, bPm = [Buf() for _ in range(3)], [Buf() for _ in range(3)]
            rz = self.sb('rz', [128, 512], F32, es)
            brz = Buf()
            if samp:
                ckt = self.sb('ckt', [128, 16, 128], F32, es)
                bck = Buf()
                KTc = self.sb('KTc', [128, 2048], BF16, es)
                bKTc = Buf()
                Vc = self.sb('Vc', [128, 16, 128], BF16, es)
                bVc = Buf()
            pc = 0
            self.ps_n = 4
            accn = 0
            for h in range(8):
                k = h % 2
                P.dma('sync', KTh[k][:, lo_tile * 128:3072], self.ktscr[h * 128:(h + 1) * 128, lo_tok:base + 1024],
                      reads=[self.b_kt[jj] for jj in range(max(0, j - 2), j + 1)], writes=[bKT[k]])
                vsrc = self.vscr[lo_tok:base + 1024, h * 128:(h + 1) * 128].rearrange("(t p) e -> p t e", p=128)
                P.dma('sync', Vh[k][:, lo_tile:24, :], vsrc,
                      reads=[self.b_v[jj] for jj in range(max(0, j - 2), j + 1)], writes=[bV[k]])
                P.dma('sync', Eh[k][:, :], self.ett[h], writes=[bE[k]])
                for qg in range(2):
                    G = 16 + 4 * qg
                    q0 = qg * 512
                    po, pbo = self.psb[4 + 2 * (accn % 2)]
                    pz, pbz = self.psb[5 + 2 * (accn % 2)]
                    accn += 1
                    kts = [kt for kt in range(G - 16, G + 4) if kt >= lo_tile]
                    kts = [G] + [kt for kt in kts if kt != G]
                    for idx, kt in enumerate(kts):
                        i0 = max(0, kt - G)
                        i1 = min(3, kt - G + 16)
                        c0, c1 = i0 * 128, (i1 + 1) * 128
                        n = c1 - c0
                        e0 = (G + i0 - kt) * 128
                        pt, pb = self.ps()
                        m = pc % 3
                        pc += 1
                        self.mm(pt[:, 0:n], KTh[k][:, kt * 128:(kt + 1) * 128], qT[:, h, q0 + c0:q0 + c1], True, True,
                                reads=[bKT[k], bq], writes=[pb])
                        self.act(Pt[m][:, 0:n], pt[:, 0:n], AF.Exp, reads=[pb], writes=[bPt[m]], scale=ATT_SCALE)
                        self.op(self.alt(), 'tensor_tensor', reads=[bPt[m], bE[k]], writes=[bPm[m]], out=Pm[m][:, 0:n], in0=Pt[m][:, 0:n],
                                in1=Eh[k][:, e0:e0 + n], op=ALU.mult)
                        st, sp = idx == 0, idx == len(kts) - 1
                        self.mm(po[:, c0:c1], Vh[k][:, kt, :], Pm[m][:, 0:n], st, sp, reads=[bV[k], bPm[m]], writes=[pbo], skip=True)
                        self.mm(pz[:, c0:c1], self.ones_bf[:, :], Pm[m][:, 0:n], st, sp, reads=[bPm[m]], writes=[pbz], skip=True)
                    self.op('vector', 'reciprocal', reads=[pbz], writes=[brz], out=rz[:, :], in_=pz[:, :])
                    self.op('vector', 'tensor_tensor', reads=[pbo, brz], writes=bmix[qg], out=mixedT[:, h, q0:q0 + 512], in0=po[:, :], in1=rz[:, :], op=ALU.mult)
                if samp:
                    c = None
                    P.dma('sync', ckt[:], self.i['ck'][l][:, h * 128:(h + 1) * 128].rearrange("(t p) e -> p t e", p=128), writes=[bck])
                    P.dma('gpsimd', Vc[:], self.i['cv'][l][:, h * 128:(h + 1) * 128].rearrange("(t p) e -> p t e", p=128), writes=[bVc])
                    for c4 in range(4):
                        pt, pb = self.ps()
                        for q in range(4):
                            ct = c4 * 4 + q
                            self.mm(pt[:, q * 128:(q + 1) * 128], ckt[:, ct, :], self.ident[:, :], True, True, reads=[bck], writes=[pb])
                        self.act(KTc[:, c4 * 512:(c4 + 1) * 512], pt[:, :], AF.Copy, reads=[pb], writes=[bKTc])
                    ps_, pbs = self.ps()
                    qs = qT[:, h, 1024:1032]
                    self.op('vector', 'memset', writes=[pbs], ap=ps_[:, 0:8], constant=0.0)
                    self.mm(ps_[0:8, 0:8], KTs[:, h, :], qs, True, True, reads=[bkts, bq], writes=[pbs], skip=True)
                    for s in range(1, 17):
                        ct = 16 - s
                        self.mm(ps_[:, s * 8:(s + 1) * 8], KTc[:, ct * 128:(ct + 1) * 128], qs, True, True, reads=[bKTc, bq], writes=[pbs], skip=True)
                    m = pc % 3
                    pc += 1
                    self.act(Pt[m][:, 0:136], ps_[:, 0:136], AF.Exp, reads=[pbs], writes=[bPt[m]], scale=ATT_SCALE)
                    self.op('vector', 'tensor_tensor', reads=[bPt[m], bE[k]], writes=[bPm[m]],
                            out=Pm[m][:, 0:136].rearrange("p (s t) -> p s t", t=8), in0=Pt[m][:, 0:136].rearrange("p (s t) -> p s t", t=8),
                            in1=Eh[k][:, :].rearrange("p (s t) -> p s t", t=128)[:, :, 0:8], op=ALU.mult)
                    po, pbo = self.psb[4 + 2 * (accn % 2)]
                    pz, pbz = self.psb[5 + 2 * (accn % 2)]
                    accn += 1
                    self.mm(po[:, 0:8], Vs[0:8, h * 128:(h + 1) * 128], Pm[m][0:8, 0:8], True, False, reads=[bvs, bPm[m]], writes=[pbo], skip=True)
                    self.mm(pz[:, 0:8], self.ones_bf[0:8, :], Pm[m][0:8, 0:8], True, False, reads=[bPm[m]], writes=[pbz], skip=True)
                    for s in range(1, 17):
                        ct = 16 - s
                        self.mm(po[:, 0:8], Vc[:, ct, :], Pm[m][:, s * 8:(s + 1) * 8], False, s == 16, reads=[bVc, bPm[m]], writes=[pbo], skip=True)
                        self.mm(pz[:, 0:8], self.ones_bf[:, :], Pm[m][:, s * 8:(s + 1) * 8], False, s == 16, reads=[bPm[m]], writes=[pbz], skip=True)
                    self.op('vector', 'reciprocal', reads=[pbz], writes=[brz], out=rz[:, 0:8], in_=pz[:, 0:8])
                    self.op('vector', 'tensor_tensor', reads=[pbo, brz], writes=bmix[2], out=mixedT[:, h, 1024:1032], in0=po[:, 0:8], in1=rz[:, 0:8], op=ALU.mult)
        self.ps_n = 8
        groups = self.groups(j)
        with self.phase() as es2:
            self.rmsnorm(es2, lambda ch, c0, c1: mixedT[:, ch, c0:c1], lambda ch, c0, c1: mixedT[:, ch, c0:c1], 8,
                         lambda ch: self.gcol[:, 32 + ch:33 + ch], groups, bmix, bmix, 'b')

    def sin_reduced(self, dst, x, tf, ti, rb, wb):
        V = lambda name, **kw: self.op('vector', name, reads=rb + wb, writes=wb, **kw)
        V('tensor_scalar', out=tf, in0=x, scalar1=1.0 / TWO_PI, scalar2=0.5, op0=ALU.mult, op1=ALU.add)
        V('tensor_copy', out=ti, in_=tf)
        V('tensor_copy', out=tf, in_=ti)
        V('scalar_tensor_tensor', out=x, in0=tf, scalar=-TWO_PI, in1=x, op0=ALU.mult, op1=ALU.add)
        V('tensor_scalar', out=tf, in0=x, scalar1=-math.pi, scalar2=TWO_PI, op0=ALU.is_lt, op1=ALU.mult)
        V('tensor_tensor', out=x, in0=x, in1=tf, op=ALU.add)
        V('tensor_scalar', out=tf, in0=x, scalar1=math.pi, scalar2=-TWO_PI, op0=ALU.is_gt, op1=ALU.mult)
        V('tensor_tensor', out=x, in0=x, in1=tf, op=ALU.add)
        self.act(dst, x, AF.Sin, reads=rb + wb, writes=wb)

    def phase_ssm(self, l, j, groups, samp, uT, bu, mixedT, bmix):
        nc, P = self.nc, self.P
        with self.phase() as es:
            es_s = contextlib.ExitStack()
            cols = self.sb('ssmcols', [128, 48], F32, es)
            dcol = self.sb('dcol', [128, 8], F32, es)
            pr = self.sb('ssmpr', [128, 16, 16], F32, es)
            Bb = self.sb('Bb', [128, 16, 2, 128], BF16, es)
            Cb = self.sb('Cb', [128, 16, 2, 128], BF16, es)
            wg = self.sb('wglu', [128, 4, 512], BF16, es)
            bbr = self.sb('bbr', [128, 16, 128], F32, es_s)
            bbi = self.sb('bbi', [128, 16, 128], F32, es_s)
            b0, bpr, bbb, bBb, bCb, bwg = [Buf() for _ in range(6)]
            P.dma('sync', cols[:], self.i['ssmcols'][l], writes=[b0])
            P.dma('sync', dcol[:], self.i['dcols'][l], writes=[b0])
            P.dma('sync', bbr[:], self.i['bbd_re'][l].rearrange("s p n -> p s n"), writes=[bbb])
            P.dma('sync', bbi[:], self.i['bbd_im'][l].rearrange("s p n -> p s n"), writes=[bbb])
            P.dma('gpsimd', Cb[:, :, 0, :], self.i['cbd_re'][l].rearrange("s p n -> p s n"), writes=[bCb])
            P.dma('gpsimd', Cb[:, :, 1, :], self.i['cbd_im'][l].rearrange("s p n -> p s n"), writes=[bCb])
            P.dma('gpsimd', wg[:], self.i['wglu'][l].rearrange("(kc p) n -> p kc n", p=128), writes=[bwg])
            lre, lim, lst = cols[:, 0:16], cols[:, 16:32], cols[:, 32:48]
            K = lambda i: pr[:, i, :]
            V = lambda name, **kw: self.op('vector', name, reads=[b0, bpr], writes=[bpr], **kw)
            self.act(K(0), lst, AF.Exp, reads=[b0], writes=[bpr])
            V('tensor_tensor', out=K(1), in0=lre, in1=K(0), op=ALU.mult)
            V('tensor_tensor', out=K(2), in0=lim, in1=K(0), op=ALU.mult)
            self.act(K(3), K(1), AF.Exp, reads=[bpr], writes=[bpr])
            kti = self.sb('kti', [128, 16], I32, es_s)
            V('tensor_scalar', out=K(11), in0=K(2), scalar1=TWO_PI, scalar2=None, op0=ALU.add)
            self.sin_reduced(K(5), K(11), K(12), kti[:, :], [b0], [bpr])
            V('tensor_scalar', out=K(11), in0=K(2), scalar1=TWO_PI + 0.5 * math.pi, scalar2=None, op0=ALU.add)
            self.sin_reduced(K(4), K(11), K(12), kti[:, :], [b0], [bpr])
            V('tensor_tensor', out=K(6), in0=K(3), in1=K(4), op=ALU.mult)
            V('tensor_scalar', out=K(6), in0=K(6), scalar1=-1.0, scalar2=None, op0=ALU.add)
            V('tensor_tensor', out=K(7), in0=K(3), in1=K(5), op=ALU.mult)
            V('tensor_tensor', out=K(8), in0=lre, in1=lre, op=ALU.mult)
            V('tensor_tensor', out=K(11), in0=lim, in1=lim, op=ALU.mult)
            V('tensor_tensor', out=K(8), in0=K(8), in1=K(11), op=ALU.add)
            V('reciprocal', out=K(8), in_=K(8))
            V('tensor_tensor', out=K(9), in0=K(6), in1=lre, op=ALU.mult)
            V('tensor_tensor', out=K(11), in0=K(7), in1=lim, op=ALU.mult)
            V('tensor_tensor', out=K(9), in0=K(9), in1=K(11), op=ALU.add)
            V('tensor_tensor', out=K(9), in0=K(9), in1=K(8), op=ALU.mult)
            V('tensor_tensor', out=K(10), in0=K(7), in1=lre, op=ALU.mult)
            V('tensor_tensor', out=K(11), in0=K(6), in1=lim, op=ALU.mult)
            V('tensor_tensor', out=K(10), in0=K(10), in1=K(11), op=ALU.subtract)
            V('tensor_tensor', out=K(10), in0=K(10), in1=K(8), op=ALU.mult)
            ones_f = self.sb('ones_f', [128, 128], F32, es_s)
            crow = self.sb('crow', [128, 16, 2, 128], F32, es_s)
            bcr = Buf()
            self.op('vector', 'memset', writes=[bcr], ap=ones_f[:], constant=1.0)
            dg = self.sb('dg', [128, 128], F32, es_s)
            bdg = Buf()
            for sc in range(16):
                for ri in range(2):
                    self.op('vector', 'tensor_scalar', reads=[bpr], writes=[bdg], out=dg[:, :], in0=self.ident[:, :],
                            scalar1=pr[:, 9 + ri, sc:sc + 1], scalar2=None, op0=ALU.mult)
                    pt, pb = self.ps()
                    self.mm(pt[:, 0:128], ones_f[:, :], dg[:, :], True, True, reads=[bdg, bcr], writes=[pb])
                    self.op('vector', 'tensor_copy', reads=[pb], writes=[bcr], out=crow[:, sc, ri, :], in_=pt[:, 0:128])
            tb = self.sb('tb', [128, 16, 128], F32, es_s)
            tb2 = self.sb('tb2', [128, 16, 128], F32, es_s)
            btb = Buf()
            G = lambda name, **kw: self.op('gpsimd', name, reads=[bbb, bcr, btb], writes=[btb], **kw)
            G('tensor_tensor', out=tb[:], in0=bbr[:], in1=crow[:, :, 0, :], op=ALU.mult)
            G('tensor_tensor', out=tb2[:], in0=bbi[:], in1=crow[:, :, 1, :], op=ALU.mult)
            self.op('gpsimd', 'tensor_tensor', reads=[btb], writes=[bBb], out=Bb[:, :, 0, :], in0=tb[:], in1=tb2[:], op=ALU.subtract)
            G('tensor_tensor', out=tb[:], in0=bbr[:], in1=crow[:, :, 1, :], op=ALU.mult)
            G('tensor_tensor', out=tb2[:], in0=bbi[:], in1=crow[:, :, 0, :], op=ALU.mult)
            self.op('gpsimd', 'tensor_tensor', reads=[btb], writes=[bBb], out=Bb[:, :, 1, :], in0=tb[:], in1=tb2[:], op=ALU.add)
            self.op('gpsimd', 'tensor_scalar', reads=[bCb], writes=[bCb], out=Cb[:, :, 1, :], in0=Cb[:, :, 1, :], scalar1=-1.0, scalar2=None, op0=ALU.mult)
            self.P.fence()
            es_s.close()
            iot = self.sb('iot', [128, 1025], F32, es)
            P.dma('sync', iot[:], self.i['iota'], writes=[Buf()])
            cs = self.sb('cs', [128, 1025], F32, es)
            tfl = self.sb('tfl', [128, 1025], F32, es)
            tin = self.sb('tin', [128, 1025], I32, es)
            sn = self.sb('sn', [128, 1025], F32, es)
            wr = self.sb('wr', [128, 1024], F32, es)
            wi = self.sb('wi', [128, 1024], F32, es)
            ta = self.sb('ta', [128, 1024], F32, es)
            tc = self.sb('tc', [128, 1024], F32, es)
            hr = self.sb('hr', [128, 1024], BF16, es)
            hi = self.sb('hi', [128, 1024], BF16, es)
            ini = self.sb('ini', [128, 8], F32, es)
            yT = self.sb('ysT', [128, 4, TW], BF16, es)
            yf = self.sb('ysf', [128, 4, TW], F32, es)
            bcs, bsn, bwr, bwi, bta, btc, bhr, bhi, bini, byT = [Buf() for _ in range(10)]
            self.P.fence()
            if j == 0:
                self.op('vector', 'memset', reads=[self.b_H], writes=[self.b_H], ap=self.Hst[:, 0:2, :], constant=0.0)
                if samp:
                    P.dma('sync', self.Hst[:, 2, :], self.i['sre'][l], writes=[self.b_H])
                    P.dma('sync', self.Hst[:, 3, :], self.i['sim'][l], writes=[self.b_H])
            runs = [(0, 1024, 0)] + ([(1024, 8, 2)] if samp else [])
            self.ps_n = 5
            ypsum = {}
            for sc in range(16):
                uc, oc = sc // 4, sc // 4
                th = pr[:, 2, sc:sc + 1]
                for (tab, btab, off) in ((sn, bsn, TWO_PI), (cs, bcs, TWO_PI + 0.5 * math.pi)):
                    self.op('vector', 'tensor_scalar', reads=[bpr], writes=[btab], out=tab[:, :], in0=iot[:, :], scalar1=th, scalar2=off, op0=ALU.mult, op1=ALU.add)
                    self.sin_reduced(tab[:, :], tab[:, :], tfl[:, :], tin[:, :], [bpr], [btab])
                for (t0, T, hs) in runs:
                    xs = []
                    for ri in range(2):
                        for c0 in range(0, T, 512):
                            n = min(512, T - c0)
                            pt, pb = self.ps()
                            self.mm(pt[:, 0:n], Bb[:, sc, ri, :], uT[:, uc, t0 + c0:t0 + c0 + n], True, True, reads=[bBb, bu], writes=[pb])
                            xs.append((ri, c0, n, pt, pb))
                    for (ri, c0, n, pt, pb) in xs:
                        x = pt[:, 0:n]
                        if ri == 0:
                            self.op('vector', 'tensor_tensor', reads=[pb, bcs], writes=[bwr], out=wr[:, c0:c0 + n], in0=x, in1=cs[:, c0:c0 + n], op=ALU.mult)
                            self.op('vector', 'tensor_tensor', reads=[pb, bsn], writes=[bta], out=ta[:, c0:c0 + n], in0=x, in1=sn[:, c0:c0 + n], op=ALU.mult)
                        else:
                            self.op('vector', 'tensor_tensor', reads=[pb, bcs], writes=[bwi], out=wi[:, c0:c0 + n], in0=x, in1=cs[:, c0:c0 + n], op=ALU.mult)
                            self.op('vector', 'tensor_tensor', reads=[pb, bsn], writes=[btc], out=tc[:, c0:c0 + n], in0=x, in1=sn[:, c0:c0 + n], op=ALU.mult)
                    self.op('gpsimd', 'tensor_tensor', reads=[bwr, btc], writes=[bwr], out=wr[:, 0:T], in0=wr[:, 0:T], in1=tc[:, 0:T], op=ALU.add)
                    self.op('gpsimd', 'tensor_tensor', reads=[bwi, bta], writes=[bwi], out=wi[:, 0:T], in0=wi[:, 0:T], in1=ta[:, 0:T], op=ALU.subtract)
                    Hre, Him = self.Hst[:, hs, sc:sc + 1], self.Hst[:, hs + 1, sc:sc + 1]
                    c1_, s1_ = cs[:, 1:2], sn[:, 1:2]
                    I = lambda name, **kw: self.op('vector', name, reads=[self.b_H, bcs, bsn, bini], writes=[bini], **kw)
                    I('tensor_tensor', out=ini[:, 2:3], in0=Him, in1=s1_, op=ALU.mult)
                    I('scalar_tensor_tensor', out=ini[:, 0:1], in0=Hre, scalar=c1_, in1=ini[:, 2:3], op0=ALU.mult, op1=ALU.subtract)
                    I('tensor_tensor', out=ini[:, 3:4], in0=Him, in1=c1_, op=ALU.mult)
                    I('scalar_tensor_tensor', out=ini[:, 1:2], in0=Hre, scalar=s1_, in1=ini[:, 3:4], op0=ALU.mult, op1=ALU.add)
                    rho = pr[:, 3, sc:sc + 1]
                    self.op('vector', 'tensor_tensor_scan', reads=[bwr, bini, bpr], writes=[bwr], out=wr[:, 0:T], data0=rho.to_broadcast([128, T]), data1=wr[:, 0:T],
                            initial=ini[:, 0:1], op0=ALU.mult, op1=ALU.add)
                    self.op('vector', 'tensor_tensor_scan', reads=[bwi, bini, bpr], writes=[bwi], out=wi[:, 0:T], data0=rho.to_broadcast([128, T]), data1=wi[:, 0:T],
                            initial=ini[:, 1:2], op0=ALU.mult, op1=ALU.add)
                    Gp = lambda name, rd, wrb, **kw: self.op('gpsimd', name, reads=rd, writes=wrb, **kw)
                    Gp('tensor_tensor', [bwr, bcs], [bta], out=ta[:, 0:T], in0=wr[:, 0:T], in1=cs[:, 0:T], op=ALU.mult)
                    Gp('tensor_tensor', [bwi, bsn], [btc], out=tc[:, 0:T], in0=wi[:, 0:T], in1=sn[:, 0:T], op=ALU.mult)
                    self.op('vector', 'tensor_tensor', reads=[bta, btc], writes=[bhr], out=hr[:, 0:T], in0=ta[:, 0:T], in1=tc[:, 0:T], op=ALU.subtract)
                    self.op('vector', 'tensor_tensor', reads=[bta, btc, self.b_H], writes=[self.b_H], out=Hre, in0=ta[:, T - 1:T], in1=tc[:, T - 1:T], op=ALU.subtract)
                    Gp('tensor_tensor', [bwr, bsn, bhr], [bta], out=ta[:, 0:T], in0=wr[:, 0:T], in1=sn[:, 0:T], op=ALU.mult)
                    Gp('tensor_tensor', [bwi, bcs, bhr], [btc], out=tc[:, 0:T], in0=wi[:, 0:T], in1=cs[:, 0:T], op=ALU.mult)
                    self.op('vector', 'tensor_tensor', reads=[bta, btc], writes=[bhi], out=hi[:, 0:T], in0=ta[:, 0:T], in1=tc[:, 0:T], op=ALU.add)
                    self.op('vector', 'tensor_tensor', reads=[bta, btc, self.b_H], writes=[self.b_H], out=Him, in0=ta[:, T - 1:T], in1=tc[:, T - 1:T], op=ALU.add)
                    for c0 in range(0, T, 512):
                        n = min(512, T - c0)
                        key = (oc, t0 + c0)
                        pt, pb = self.psb[5 + (0 if t0 + c0 == 0 else (1 if t0 + c0 == 512 else 2))]
                        self.mm(pt[:, 0:n], Cb[:, sc, 0, :], hr[:, c0:c0 + n], sc % 4 == 0, False, reads=[bCb, bhr], writes=[pb], skip=True)
                        self.mm(pt[:, 0:n], Cb[:, sc, 1, :], hi[:, c0:c0 + n], False, sc % 4 == 3, reads=[bCb, bhi], writes=[pb], skip=True)
                        if sc % 4 == 3:
                            self.op('vector', 'scalar_tensor_tensor', reads=[pb, bu, b0], writes=[byT], out=yf[:, oc, t0 + c0:t0 + c0 + n], in0=uT[:, oc, t0 + c0:t0 + c0 + n],
                                    scalar=dcol[:, oc:oc + 1], in1=pt[:, 0:n], op0=ALU.mult, op1=ALU.add)
                            self.op('gpsimd', 'tensor_copy', reads=[byT], writes=[byT], out=yT[:, oc, t0 + c0:t0 + c0 + n], in_=yf[:, oc, t0 + c0:t0 + c0 + n])
            self.ps_n = 8
            if j == self.NCH - 1:
                P.dma('sync', self.o['nsre'][l], self.Hst[:, 0, :], reads=[self.b_H], writes=[self.b_out])
                P.dma('sync', self.o['nsim'][l], self.Hst[:, 1, :], reads=[self.b_H], writes=[self.b_out])
            if samp:
                P.dma('sync', self.o['nssre'][l], self.Hst[:, 2, :], reads=[self.b_H], writes=[self.b_out])
                P.dma('sync', self.o['nssim'][l], self.Hst[:, 3, :], reads=[self.b_H], writes=[self.b_out])
            sg = self.sb('sg', [128, 512], F32, es)
            bsg = Buf()
            for oc2 in range(4):
                for (c0, c1, gi) in groups:
                    n = c1 - c0
                    pt, pb = self.ps()
                    for kc in range(4):
                        self.mm(pt[:, 0:n], wg[:, kc, oc2 * 128:(oc2 + 1) * 128], yT[:, kc, c0:c1], kc == 0, kc == 3, reads=[bwg, byT], writes=[pb])
                    self.act(sg[:, 0:n], pt[:, 0:n], AF.Sigmoid, reads=[pb, b0], writes=[bsg], bias=dcol[:, 4 + oc2:5 + oc2])
                    self.op('vector', 'tensor_tensor', reads=[bsg, byT], writes=bmix[gi], out=mixedT[:, 8 + oc2, c0:c1], in0=sg[:, 0:n], in1=yf[:, oc2, c0:c1], op=ALU.mult)
            with self.phase() as es2:
                self.rmsnorm(es2, lambda ch, c0, c1: mixedT[:, 8 + ch, c0:c1], lambda ch, c0, c1: mixedT[:, 8 + ch, c0:c1], 4,
                             lambda ch: self.gcol[:, 40 + ch:41 + ch], groups, bmix, bmix, 'c')
    def phase_sgu(self, l, j, samp, guT, gv, gvs, bgu, bgv, mixedT, bmix):
        nc, P = self.nc, self.P
        with self.phase() as es:
            wT = self.sb('sguw', [128, 512], F32, es)
            tri = self.sb('tri', [128, 512], F32, es)
            wTb = self.sb('sguwb', [128, 512], BF16, es)
            brow = self.sb('sgubrow', [1, 512], F32, es)
            browb = self.sb('sgubrowb', [1, 512], BF16, es)
            b1, b2 = Buf(), Buf()
            P.dma('sync', wT[:], self.i['sguwT'][l], writes=[b1])
            P.dma('sync', tri[:], self.i['trilT'], writes=[b1])
            P.dma('sync', brow[:], self.i['sgub'][l], writes=[b1])
            self.op('vector', 'tensor_tensor', reads=[b1], writes=[b2], out=wTb[:], in0=wT[:], in1=tri[:], op=ALU.mult)
            self.op('vector', 'tensor_copy', reads=[b1], writes=[b2], out=browb[:], in_=brow[:])
            for h in range(4):
                for half in range(2):
                    pt, pb = self.ps()
                    for q in range(4):
                        tt = half * 4 + q
                        self.mm(pt[:, q * 128:(q + 1) * 128], gv[:, tt, h * 128:(h + 1) * 128], wTb[:, h * 128:(h + 1) * 128], True, False, reads=[bgv, b2], writes=[pb], skip=True)
                        self.mm(pt[:, q * 128:(q + 1) * 128], self.ones_bf[0:1, :], browb[0:1, h * 128:(h + 1) * 128], False, True, reads=[b2], writes=[pb], skip=True)
                    c0 = half * 512
                    self.op('vector', 'tensor_tensor', reads=[pb, bgu], writes=bmix[half], out=mixedT[:, 12 + h, c0:c0 + 512], in0=pt[:, :], in1=guT[:, h, c0:c0 + 512], op=ALU.mult)
                if samp:
                    pt, pb = self.ps()
                    self.mm(pt[:, 0:8], gvs[0:8, h * 128:(h + 1) * 128], wTb[0:8, h * 128:h * 128 + 8], True, False, reads=[bgv, b2], writes=[pb], skip=True)
                    self.mm(pt[:, 0:8], self.ones_bf[0:1, :], browb[0:1, h * 128:h * 128 + 8], False, True, reads=[b2], writes=[pb], skip=True)
                    self.op('vector', 'tensor_tensor', reads=[pb, bgu], writes=bmix[2], out=mixedT[:, 12 + h, 1024:1032], in0=pt[:, 0:8], in1=guT[:, h, 1024:1032], op=ALU.mult)
            with self.phase() as es2:
                self.rmsnorm(es2, lambda ch, c0, c1: mixedT[:, 12 + ch, c0:c1], lambda ch, c0, c1: mixedT[:, 12 + ch, c0:c1], 4,
                             lambda ch: self.gcol[:, 44 + ch:45 + ch], self.groups(j), bmix, bmix, 'd')

    def proj_post(self, es, groups, nkc, lhs_fn, rhs_fn, rbufs_fn, wld_fn, nblk, cpb, gcol_off, tag):
        nc, P = self.nc, self.P
        xT = self.xT
        ng = len(groups)
        wid = sum(c1 - c0 for (c0, c1, gi) in groups)
        yT = self.sb('yT' + tag, [128, NKC, wid], BF16, es)
        byT = Buf()
        sq = [self.sb('psq%s%d' % (tag, k), [128, 512], BF16, es) for k in range(3)]
        bsq = [Buf() for _ in range(3)]
        rstd = self.sb('prstd' + tag, [128, wid], F32, es)
        brs = Buf()
        tmp = [self.sb('ptmp%s%d' % (tag, k), [128, 512], F32, es) for k in range(2)]
        btmp = [Buf(), Buf()]
        self.ps_n = 8 - ng
        ssq = {gi: self.psb[8 - ng + i] for i, (c0, c1, gi) in enumerate(groups)}
        offs = {}
        o = 0
        for (c0, c1, gi) in groups:
            offs[gi] = o
            o += c1 - c0
        pend = []
        sqc = 0
        for blk in range(nblk):
            wbuf, bw = wld_fn(blk)
            for cc in range(cpb):
                ch = blk * cpb + cc
                for (c0, c1, gi) in groups:
                    n = c1 - c0
                    pt, pb = self.ps()
                    for kc in range(nkc):
                        self.mm(pt[:, 0:n], lhs_fn(wbuf, kc, cc), rhs_fn(kc, c0, c1), kc == 0, kc == nkc - 1, reads=[bw.sub[kc]] + rbufs_fn(gi), writes=[pb])
                    for f in pend:
                        f()
                    pend = []
                    k = sqc % 3
                    sqc += 1
                    self.act(yT[:, ch, offs[gi]:offs[gi] + n], pt[:, 0:n], AF.Copy, reads=[pb], writes=[byT])
                    self.act(sq[k][:, 0:n], pt[:, 0:n], AF.Square, reads=[pb], writes=[bsq[k]])
                    st, sp = (ch == 0), (ch == NKC - 1)

                    def f(k=k, n=n, gi=gi, st=st, sp=sp):
                        self.mm(ssq[gi][0][:, 0:n], self.ones_bf[:, :], sq[k][:, 0:n], st, sp, reads=[bsq[k]], writes=[ssq[gi][1]], skip=True)
                    pend.append(f)
        for f in pend:
            f()
        for (c0, c1, gi) in groups:
            n = c1 - c0
            r = rstd[:, offs[gi]:offs[gi] + n]
            self.op('vector', 'tensor_scalar', reads=[ssq[gi][1]], writes=[brs], out=r, in0=ssq[gi][0][:, 0:n], scalar1=1.0 / D_MODEL, scalar2=EPS, op0=ALU.mult, op1=ALU.add)
            self.act(r, r, AF.Sqrt, reads=[brs], writes=[brs])
            self.op('vector', 'reciprocal', reads=[brs], writes=[brs], out=r, in_=r)
            for ch in range(NKC):
                k = ch % 2
                eng = 'vector' if k else 'gpsimd'
                self.op('vector', 'scalar_tensor_tensor', reads=[brs, byT, self.b_gcol], writes=[btmp[k]], out=tmp[k][:, 0:n], in0=yT[:, ch, offs[gi]:offs[gi] + n],
                        scalar=self.gcol[:, gcol_off + ch:gcol_off + ch + 1], in1=r, op0=ALU.mult, op1=ALU.mult)
                self.op(eng, 'tensor_tensor', reads=[btmp[k], self.bx[gi]], writes=[self.bx[gi]], out=xT[:, ch, c0:c1], in0=xT[:, ch, c0:c1], in1=tmp[k][:, 0:n], op=ALU.add)
        self.ps_n = 8

    def phase_out(self, l, j, groups, mixedT, bmix):
        with self.phase() as es:
            self.cur_stg = (self.sb('stgO', [128, 10, 512], F32, es), [Buf() for _ in range(10)], 10)
            wbuf = [self.sb('wO%d' % k, [128, NKC, 512], BF16, es) for k in range(2)]
            bw = [WB(NKC), WB(NKC)]
            wsrc = self.i['w_out'][l]

            def wld(blk):
                k = blk % 2
                self.wload(wbuf[k][:], bw[k], wsrc[:, blk * 512:(blk + 1) * 512].rearrange("(kc p) n -> p kc n", p=128))
                return wbuf[k], bw[k]
            self.proj_post(es, groups, NKC, lambda w, kc, cc: w[:, kc, cc * 128:(cc + 1) * 128], lambda kc, c0, c1: mixedT[:, kc, c0:c1],
                           lambda gi: bmix[gi], wld, 4, 4, 16, 'o')

    def phase_ffn(self, l, j, groups):
        nc, P = self.nc, self.P
        xT = self.xT
        halves = [[g for g in groups if g[2] in (0, 2)], [g for g in groups if g[2] == 1]]
        for hg in halves:
            with self.phase() as es:
                wid = sum(c1 - c0 for (c0, c1, gi) in hg)
                offs = {}
                o = 0
                for (c0, c1, gi) in hg:
                    offs[gi] = o
                    o += c1 - c0
                hT = self.sb('hF', [128, NKC, wid], BF16, es)
                bh = {g[2]: [Buf()] for g in hg}
                with self.phase() as es2:
                    self.rmsnorm(es2, lambda ch, c0, c1: xT[:, ch, c0:c1],
                                 lambda ch, c0, c1: hT[:, ch, offs[0 if c0 == 0 else (2 if c0 == 1024 else 1)] :offs[0 if c0 == 0 else (2 if c0 == 1024 else 1)] + (c1 - c0)],
                                 NKC, lambda ch: self.gcol[:, 48 + ch:49 + ch], hg, {g[2]: [self.bx[g[2]]] for g in hg}, bh, 'f')
                actT = self.sb('actT', [128, NHC, wid], BF16, es)
                bact = Buf()
                with self.phase() as es3:
                    self.cur_stg = (self.sb('stgF', [128, 12, 512], F32, es3), [Buf() for _ in range(12)], 12)
                    wg = [self.sb('wG%d' % k, [128, NKC, 256], BF16, es3) for k in range(2)]
                    wu = [self.sb('wU%d' % k, [128, NKC, 256], BF16, es3) for k in range(2)]
                    bwg, bwu = [WB(NKC), WB(NKC)], [WB(NKC), WB(NKC)]
                    sl = [self.sb('sl%d' % k, [128, 512], F32, es3) for k in range(2)]
                    bsl = [Buf(), Buf()]
                    cnt = 0
                    for blk in range(22):
                        k = blk % 2
                        self.wload(wg[k][:], bwg[k], self.i['w_gate'][l][:, blk * 256:(blk + 1) * 256].rearrange("(kc p) n -> p kc n", p=128))
                        self.wload(wu[k][:], bwu[k], self.i['w_up'][l][:, blk * 256:(blk + 1) * 256].rearrange("(kc p) n -> p kc n", p=128))
                        for cc in range(2):
                            hc = blk * 2 + cc
                            for (c0, c1, gi) in hg:
                                n = c1 - c0
                                pg, pbg = self.ps()
                                pu, pbu = self.ps()
                                for kc in range(NKC):
                                    self.mm(pg[:, 0:n], wg[k][:, kc, cc * 128:(cc + 1) * 128], hT[:, kc, offs[gi]:offs[gi] + n], kc == 0, kc == NKC - 1, reads=[bwg[k].sub[kc], bh[gi][0]], writes=[pbg])
                                for kc in range(NKC):
                                    self.mm(pu[:, 0:n], wu[k][:, kc, cc * 128:(cc + 1) * 128], hT[:, kc, offs[gi]:offs[gi] + n], kc == 0, kc == NKC - 1, reads=[bwu[k].sub[kc], bh[gi][0]], writes=[pbu])
                                m = cnt % 2
                                cnt += 1
                                self.act(sl[m][:, 0:n], pg[:, 0:n], AF.Silu, reads=[pbg], writes=[bsl[m]])
                                self.op('vector', 'tensor_tensor', reads=[bsl[m], pbu], writes=[bact], out=actT[:, hc, offs[gi]:offs[gi] + n], in0=sl[m][:, 0:n], in1=pu[:, 0:n], op=ALU.mult)
                self.cur_stg = (self.wstg, self.bstg, 4)
                with self.phase() as es4:
                    self.cur_stg = (self.sb('stgD', [128, 10, 512], F32, es4), [Buf() for _ in range(10)], 10)
                    wd = [self.sb('wD%d' % k, [128, NHC, 128], BF16, es4) for k in range(2)]
                    bwd = [WB(NHC), WB(NHC)]

                    def wld(blk):
                        k = blk % 2
                        self.wload(wd[k][:], bwd[k], self.i['w_down'][l][:, blk * 128:(blk + 1) * 128].rearrange("(kc p) n -> p kc n", p=128))
                        return wd[k], bwd[k]
                    self.proj_post(es4, hg, NHC, lambda w, kc, cc: w[:, kc, :], lambda kc, c0, c1: actT[:, kc, offs[0 if c0 == 0 else (2 if c0 == 1024 else 1)]:offs[0 if c0 == 0 else (2 if c0 == 1024 else 1)] + (c1 - c0)],
                                   lambda gi: [bact], wld, 16, 1, 64, 'd')
                self.cur_stg = (self.wstg, self.bstg, 4)

def _t5_bucket_np(dist):
    max_exact = 16
    df = np.maximum(dist, 1).astype(np.float32)
    large = max_exact + (np.log(df / np.float32(max_exact)) / np.float32(math.log(2048 / max_exact)) * np.float32(16)).astype(np.int32)
    large = np.minimum(large, 31)
    return np.where(dist < max_exact, dist, large)


def _consts():
    c = {}
    c['ident'] = np.eye(128, dtype=np.float32)
    c['jflip'] = np.eye(128, dtype=np.float32)[::-1].copy()
    s = np.arange(128)[:, None]
    t = np.arange(128)[None, :]
    c['trilT'] = np.tile((s <= t).astype(np.float32), (1, 4))
    c['iota'] = np.tile(np.arange(1025, dtype=np.float32)[None, :], (128, 1))
    delta = np.arange(TVL) - 127
    m = ((delta >= 0) & (delta <= 128)).astype(np.int32) + ((delta >= 0) & (delta <= 512) & (delta % 4 == 0)) + \
        ((delta >= 0) & (delta <= 2048) & (delta % 16 == 0))
    c33 = np.zeros((33, TVL), np.float32)
    bk = _t5_bucket_np(np.maximum(delta, 0).astype(np.int32))
    valid = m > 0
    c33[bk[valid], np.nonzero(valid)[0]] = 1.0
    c33[32, :] = np.where(valid, np.log(np.maximum(m, 1)).astype(np.float32), np.float32(-30000.0))
    c['c33'] = c33
    return c


def _colT(a, n):
    return np.ascontiguousarray(a.reshape(n, 128).T)


_NC_CACHE = {}


def run_model(inp, DEPTH, NCH):
    f32 = np.float32
    key = (DEPTH, NCH)
    if key not in _NC_CACHE:
        _NC_CACHE[key] = KB(DEPTH, NCH)
        _NC_CACHE[key].build()
    kb = _NC_CACHE[key]
    L = NCH * TOK
    cst = _consts()
    D = DEPTH
    g = lambda k: np.asarray(inp[k], f32)
    gcols = np.stack([np.concatenate([_colT(g(k)[l], 16) for k in ('g_pre_mix', 'g_post_mix', 'g_mix_out', 'g_pre_ffn', 'g_post_ffn')], 1) for l in range(D)])
    ssmcols = np.stack([np.concatenate([_colT(g('ssm_lam_re')[l].reshape(-1), 16), _colT(g('ssm_lam_im')[l].reshape(-1), 16),
                                        _colT(np.repeat(g('ssm_log_step')[l], 64), 16)], 1) for l in range(D)])
    dcols = np.stack([np.concatenate([_colT(g('ssm_d')[l], 4), _colT(g('ssm_b_glu')[l], 4)], 1) for l in range(D)])
    bbd = {}
    for nm, src in (('bbd_re', g('ssm_b_re')), ('bbd_im', g('ssm_b_im'))):
        a = np.zeros((D, 16, 128, 128), f32)
        for sc in range(16):
            for gl in range(2):
                gg = 2 * sc + gl
                r0 = (gg % 8) * 16
                a[:, sc, r0:r0 + 16, gl * 64:(gl + 1) * 64] = np.transpose(src[:D, gg], (0, 2, 1))
        bbd[nm] = a
    for nm, src in (('cbd_re', g('ssm_c_re')), ('cbd_im', g('ssm_c_im'))):
        a = np.zeros((D, 16, 128, 128), f32)
        for sc in range(16):
            for gl in range(2):
                gg = 2 * sc + gl
                c0 = (gg % 8) * 16
                a[:, sc, gl * 64:(gl + 1) * 64, c0:c0 + 16] = np.transpose(src[:D, gg], (0, 2, 1))
        bbd[nm] = a
    sgug = np.stack([np.tile(g('sgu_g')[l][None, :], (128, 1)) for l in range(D)])
    sguwT = np.stack([np.concatenate([g('sgu_w')[l, h].T for h in range(4)], 1) for l in range(D)])
    sgub = g('sgu_b')[:D].reshape(D, 1, 512)
    relb = np.concatenate([g('rel_bias'), np.ones((1, 8), f32)], 0)
    shared = dict(w_in=g('w_in')[:D], w_out=g('w_out')[:D], w_gate=g('w_gate')[:D], w_up=g('w_up')[:D], w_down=g('w_down')[:D],
                  wglu=g('ssm_w_glu')[:D], gcols=gcols, ssmcols=ssmcols, dcols=dcols, sgug=sgug, sguwT=np.ascontiguousarray(sguwT),
                  sgub=np.ascontiguousarray(sgub), relb=relb, **bbd, **cst)
    nb = inp['x_prompt'].shape[0]
    in_maps = []
    for c in range(8):
        m = dict(shared)
        m['xp'] = np.ascontiguousarray(g('x_prompt')[c % nb, :L])
        m['xs'] = np.ascontiguousarray(g('x_sample')[c])
        m['ck'] = np.ascontiguousarray(g('cache_attn_k')[:D, c].reshape(D, 2048, 1024))
        m['cv'] = np.ascontiguousarray(g('cache_attn_v')[:D, c].reshape(D, 2048, 1024))
        m['sre'] = np.stack([_colT(g('state_ssm_re')[l, c].reshape(-1), 16) for l in range(D)])
        m['sim'] = np.stack([_colT(g('state_ssm_im')[l, c].reshape(-1), 16) for l in range(D)])
        in_maps.append(m)
    res = run_bass_kernel_spmd(kb.nc, in_maps, core_ids=list(range(8)))
    R = res.results
    NK = min(2048, L)
    uncol = lambda a: np.ascontiguousarray(a.T).reshape(32, 64)
    yp = np.stack([R[b]['yp'] for b in range(nb)])
    ys = np.stack([R[c]['ys'] for c in range(8)])
    nk = np.stack([np.stack([R[b]['nk'][l].reshape(NK, 8, 128) for b in range(nb)]) for l in range(D)])
    nv = np.stack([np.stack([R[b]['nv'][l].reshape(NK, 8, 128) for b in range(nb)]) for l in range(D)])
    nre = np.stack([np.stack([uncol(R[b]['nsre'][l]) for b in range(nb)]) for l in range(D)])
    nim = np.stack([np.stack([uncol(R[b]['nsim'][l]) for b in range(nb)]) for l in range(D)])
    nks = np.stack([np.stack([R[c]['nks'][l].reshape(8, 8, 128) for c in range(8)]) for l in range(D)])
    nvs = np.stack([np.stack([R[c]['nvs'][l].reshape(8, 8, 128) for c in range(8)]) for l in range(D)])
    nsre = np.stack([np.stack([uncol(R[c]['nssre'][l]) for c in range(8)]) for l in range(D)])
    nsim = np.stack([np.stack([uncol(R[c]['nssim'][l]) for c in range(8)]) for l in range(D)])
    nsgu = np.stack([np.stack([R[c]['nsgu'][l] for c in range(8)]) for l in range(D)])
    if KDBG:
        global DBG_OUT
        DBG_OUT = [R[c]['dbg'].reshape(128, NKC, TW) for c in range(8)]
    return (yp, ys, nk, nv, nre, nim, nks, nvs, nsre, nsim, nsgu)


def kernel(**inputs):
    return run_model(inputs, 4, 4)
```

```python
import contextlib, math
import numpy as np
import concourse.bass as bass
import concourse.mybir as mybir
from concourse.bass_utils import run_bass_kernel_spmd

F32 = mybir.dt.float32
BF16 = mybir.dt.bfloat16
I32 = mybir.dt.int32
ALU = mybir.AluOpType
AF = mybir.ActivationFunctionType
AX = mybir.AxisListType

ENGS = ['tensor', 'vector', 'scalar', 'gpsimd', 'sync']
DMA_R = 16
DEBUG_NAMES = None
import os as _os
KSTOP = int(_os.environ.get('KSTOP', '9'))
KDBG = int(_os.environ.get('KDBG', '0'))
KSUB = int(_os.environ.get('KSUB', '99'))
KVAR = int(_os.environ.get('KVAR', '0'))


class _Stop(Exception):
    pass


class Buf:
    __slots__ = ('name', 'w', 'r', 'excl')

    def __init__(self, name='', excl=False):
        self.name = name
        self.w = None
        self.r = []
        self.excl = excl


class WB:
    def __init__(self, n):
        self.sub = [Buf() for _ in range(n)]


class Ins:
    __slots__ = ('eng', 'fn', 'deps', 'kind', 'dma_no', 'marked', 'cnt')

    def __init__(self, eng, fn, kind):
        self.eng = eng
        self.fn = fn
        self.kind = kind
        self.deps = []
        self.marked = False
        self.cnt = 0
        self.dma_no = -1


class Prog:
    def __init__(self, nc):
        self.nc = nc
        self.q = {e: [] for e in ENGS}
        self.ndma = {e: 0 for e in ENGS}
        self.same_engine_sync = {'vector', 'scalar', 'gpsimd'}
        self.fence_deps = {e: [] for e in ENGS}
        self.since_fence_dma = []

    def emit(self, eng, fn, reads=(), writes=(), kind='c'):
        ins = Ins(eng, fn, kind)
        if kind == 'd':
            ins.dma_no = self.ndma[eng]
            self.ndma[eng] += 1
            self.since_fence_dma.append(ins)
        deps = list(self.fence_deps[eng])
        self.fence_deps[eng] = []
        for b in reads:
            if b.w is not None:
                deps.append(b.w)
            if b.excl:
                deps.extend(x for x in b.r if x.eng != eng)
        for b in writes:
            if b.w is not None:
                deps.append(b.w)
            deps.extend(b.r)
        for d in deps:
            if d is ins:
                continue
            if d.eng == eng and d.kind == 'c' and kind == 'c' and eng not in self.same_engine_sync:
                continue
            ins.deps.append(d)
        for b in reads:
            b.r.append(ins)
        for b in writes:
            b.w = ins
            b.r = []
        self.q[eng].append(ins)
        return ins

    def fence(self):
        tails = []
        for e in ENGS:
            for ins in reversed(self.q[e]):
                if ins.kind == 'c':
                    tails.append(ins)
                    break
        tails.extend(self.since_fence_dma)
        self.since_fence_dma = []
        for e in ENGS:
            self.fence_deps[e] = self.fence_deps[e] + tails

    def dma(self, eng, out, in_, reads=(), writes=(), **kw):
        return self.emit(eng, lambda e: e.dma_start(out=out, in_=in_, **kw), reads, writes, kind='d')

    def finalize(self):
        nc = self.nc
        for e in ENGS:
            for ins in self.q[e]:
                for d in ins.deps:
                    d.marked = True
        for e in ENGS:
            c = 0
            for ins in self.q[e]:
                if ins.kind == 'c' and ins.marked:
                    c += 1
                ins.cnt = c
        with contextlib.ExitStack() as es:
            esem = {e: es.enter_context(nc.semaphore('s_' + e)) for e in ENGS}
            dsem = {e: [es.enter_context(nc.semaphore('d_%s_%d' % (e, i))) for i in range(DMA_R)]
                    for e in ENGS if self.ndma[e] > 0}
            block = es.enter_context(nc.Block())
            prog = self

            def run_engine(ename, eng):
                seen_e = {e2: 0 for e2 in ENGS}
                seen_d = {}
                for ins in prog.q[ename]:
                    need_e = {}
                    need_d = {}
                    for d in ins.deps:
                        if d.kind == 'c':
                            if d.cnt > seen_e[d.eng] and d.cnt > need_e.get(d.eng, 0):
                                need_e[d.eng] = d.cnt
                        else:
                            key = (d.eng, d.dma_no % DMA_R)
                            val = 16 * (d.dma_no // DMA_R + 1)
                            if val > seen_d.get(key, 0) and val > need_d.get(key, 0):
                                need_d[key] = val
                    if ins.kind == 'd' and ins.dma_no >= DMA_R:
                        key = (ename, ins.dma_no % DMA_R)
                        val = 16 * (ins.dma_no // DMA_R)
                        if val > seen_d.get(key, 0) and val > need_d.get(key, 0):
                            need_d[key] = val
                    for e2, v in need_e.items():
                        eng.wait_ge(esem[e2], v)
                        seen_e[e2] = v
                    for key, v in need_d.items():
                        eng.wait_ge(dsem[key[0]][key[1]], v)
                        seen_d[key] = v
                    r = ins.fn(eng)
                    if DEBUG_NAMES is not None:
                        try:
                            DEBUG_NAMES.append((r.ins.name if hasattr(r, 'ins') else getattr(r, 'name', '?'), r.concise()[:400]))
                        except Exception as ex:
                            DEBUG_NAMES.append(str(ex))
                    if ins.kind == 'c':
                        if ins.marked:
                            r.then_inc(esem[ename], 1)
                    else:
                        r.then_inc(dsem[ename][ins.dma_no % DMA_R], 16)
                if prog.ndma[ename] > 0:
                    n = prog.ndma[ename]
                    for k in range(DMA_R):
                        cntk = len(range(k, n, DMA_R))
                        if cntk > 0 and 16 * cntk > seen_d.get((ename, k), 0):
                            eng.wait_ge(dsem[ename][k], 16 * cntk)

            @block.tensor
            def _(eng):
                run_engine('tensor', eng)

            @block.vector
            def _(eng):
                run_engine('vector', eng)

            @block.scalar
            def _(eng):
                run_engine('scalar', eng)

            @block.gpsimd
            def _(eng):
                run_engine('gpsimd', eng)

            @block.sync
            def _(eng):
                run_engine('sync', eng)


D_MODEL = 2048
NKC = 16
ATT_H = 8
IN_COLS = 4608
FFN_H = 5632
NHC = 44
EPS = 1e-6
ATT_SCALE = 128 ** -0.5
TOK = 1024
TW = 1032
ETW = 17 * 128
TVL = ETW + 127
TWO_PI = 2.0 * math.pi
class KB:
    def __init__(self, DEPTH, NCH):
        self.DEPTH = DEPTH
        self.NCH = NCH
        self.L = NCH * TOK
        self.NKEEP = min(2048, self.L)
        nc = bass.Bass("TRN2", target_bir_lowering=False)
        self.nc = nc
        self.P = Prog(nc)
        self.es = contextlib.ExitStack()
        self.rr = 0
        D = DEPTH
        L = self.L

        def din(name, shape, dt=F32):
            return nc.dram_tensor(name, list(shape), dt, kind="ExternalInput").ap()

        def dout(name, shape, dt=F32):
            return nc.dram_tensor(name, list(shape), dt, kind="ExternalOutput").ap()

        self.i = dict(
            xp=din('xp', [L, 2048]), xs=din('xs', [8, 2048]),
            ck=din('ck', [D, 2048, 1024]), cv=din('cv', [D, 2048, 1024]),
            sre=din('sre', [D, 128, 16]), sim=din('sim', [D, 128, 16]),
            w_in=din('w_in', [D, 2048, IN_COLS]), w_out=din('w_out', [D, 2048, 2048]),
            w_gate=din('w_gate', [D, 2048, FFN_H]), w_up=din('w_up', [D, 2048, FFN_H]),
            w_down=din('w_down', [D, FFN_H, 2048]), wglu=din('wglu', [D, 512, 512]),
            gcols=din('gcols', [D, 128, 5 * 16]), ssmcols=din('ssmcols', [D, 128, 3 * 16]),
            dcols=din('dcols', [D, 128, 8]),
            bbd_re=din('bbd_re', [D, 16, 128, 128]), bbd_im=din('bbd_im', [D, 16, 128, 128]),
            cbd_re=din('cbd_re', [D, 16, 128, 128]), cbd_im=din('cbd_im', [D, 16, 128, 128]),
            sgug=din('sgug', [D, 128, 512]), sguwT=din('sguwT', [D, 128, 512]), sgub=din('sgub', [D, 1, 512]),
            ident=din('ident', [128, 128]), jflip=din('jflip', [128, 128]), trilT=din('trilT', [128, 512]),
            iota=din('iota', [128, 1025]), c33=din('c33', [33, TVL]), relb=din('relb', [33, 8]),
        )
        self.o = dict(
            yp=dout('yp', [L, 2048]), ys=dout('ys', [8, 2048]),
            nk=dout('nk', [D, self.NKEEP, 1024]), nv=dout('nv', [D, self.NKEEP, 1024]),
            nsre=dout('nsre', [D, 128, 16]), nsim=dout('nsim', [D, 128, 16]),
            nks=dout('nks', [D, 8, 1024]), nvs=dout('nvs', [D, 8, 1024]),
            nssre=dout('nssre', [D, 128, 16]), nssim=dout('nssim', [D, 128, 16]),
            nsgu=dout('nsgu', [D, 8, 512]),
        )
        if KDBG:
            self.o['dbg'] = dout('dbg', [128, NKC * TW])
        self.xscr = nc.dram_tensor('xscr', [2048, L], F32).ap()
        self.ktscr = nc.dram_tensor('ktscr', [1024, L], BF16).ap()
        self.vscr = nc.dram_tensor('vscr', [L, 1024], BF16).ap()
        self.etv = nc.dram_tensor('etv', [8, TVL], F32)
        self.ett = nc.dram_tensor('ett', [8, 128, ETW], BF16).ap()
        self.b_xscr = [Buf() for _ in range(NCH)]
        self.b_kt = [Buf() for _ in range(NCH)]
        self.b_v = [Buf() for _ in range(NCH)]
        self.b_out = Buf()

    def sb(self, name, shape, dt, es=None):
        self.uid = getattr(self, 'uid', 0) + 1
        return (es or self.es).enter_context(self.nc.sbuf_tensor('%s_u%d' % (name, self.uid), list(shape), dt))

    def op(self, eng, name, reads=(), writes=(), **kw):
        return self.P.emit(eng, lambda e: getattr(e, name)(**kw), reads, writes)

    def mm(self, out, lhsT, rhs, start, stop, reads=(), writes=(), skip=False):
        if skip:
            return self.P.emit('tensor', lambda e: e.matmul(out, lhsT=lhsT, rhs=rhs, start=start, stop=stop, skip_group_check=True), reads, writes)
        return self.P.emit('tensor', lambda e: e.matmul(out, lhsT=lhsT, rhs=rhs, start=start, stop=stop), reads, writes)

    def act(self, out, in_, func, reads=(), writes=(), **kw):
        return self.P.emit('scalar', lambda e: e.activation(out=out, in_=in_, func=func, **kw), reads, writes)

    def ps(self):
        k = self.rr % getattr(self, 'ps_n', 8)
        self.rr += 1
        return self.psb[k]

    def alt(self):
        self.altc = getattr(self, 'altc', 0) + 1
        return 'vector' if self.altc % 2 else 'gpsimd'

    @contextlib.contextmanager
    def phase(self):
        es = contextlib.ExitStack()
        with es:
            yield es
        self.P.fence()

    def build(self):
        nc, P = self.nc, self.P
        with self.es:
            self.psb = []
            for k in range(8):
                t = self.es.enter_context(nc.psum_tensor('ps%d' % k, [128, 512], F32))
                self.psb.append((t, Buf('ps%d' % k, excl=True)))
            self.xT = self.sb('xT', [128, NKC, TW], F32)
            self.bx = [Buf('x0'), Buf('x1'), Buf('xs')]
            self.xS = self.sb('xS', [128, NKC, 8], F32)
            self.ident = self.sb('ident', [128, 128], F32)
            self.ones_bf = self.sb('ones_bf', [128, 128], BF16)
            self.gcol = self.sb('gcol', [128, 80], F32)
            self.b_gcol = Buf('gcol')
            self.Hst = self.sb('Hst', [128, 4, 16], F32)
            self.b_H = Buf('H')
            self.wstg = self.sb('wstg', [128, 4, 512], F32)
            self.bstg = [Buf() for _ in range(4)]
            self.stg_i = 0
            self.cur_stg = (self.wstg, self.bstg, 4)
            self.setup()
            try:
                for l in range(self.DEPTH):
                    P.dma('sync', self.gcol[:], self.i['gcols'][l], writes=[self.b_gcol])
                    for j in range(self.NCH):
                        self.chunk_layer(l, j)
            except _Stop:
                pass
            P.finalize()
        return nc

    def groups(self, j):
        g = [(0, 512, 0), (512, 1024, 1)]
        if j == 0:
            g.append((1024, 1032, 2))
        return g

    def setup(self):
        nc, P = self.nc, self.P
        b = Buf()
        P.dma('sync', self.ident[:], self.i['ident'], writes=[b])
        self.op('vector', 'memset', writes=[b], ap=self.ones_bf[:], constant=1.0)
        with self.phase() as es:
            c33 = self.sb('c33', [33, TVL], F32, es)
            relb = self.sb('relb', [33, 8], F32, es)
            jf = self.sb('jf', [128, 128], F32, es)
            tv = self.sb('tv', [8, TVL], F32, es)
            hk = self.sb('hk', [128, ETW], F32, es)
            tt = self.sb('tt', [128, ETW], BF16, es)
            b1, b2, b3, b4, b5, b6, b7 = [Buf() for _ in range(7)]
            P.dma('sync', c33[:], self.i['c33'], writes=[b1])
            P.dma('sync', relb[:], self.i['relb'], writes=[b2])
            P.dma('sync', jf[:], self.i['jflip'], writes=[b3])
            c0 = 0
            while c0 < TVL:
                n = min(512, TVL - c0)
                pt, pb = self.ps()
                self.mm(pt[0:8, 0:n], relb[:, :], c33[:, c0:c0 + n], True, True, reads=[b1, b2], writes=[pb])
                self.act(tv[:, c0:c0 + n], pt[0:8, 0:n], AF.Exp, reads=[pb], writes=[b4])
                c0 += n
            P.dma('sync', self.etv.ap(), tv[:], reads=[b4], writes=[b5])
            for h in range(8):
                src = bass.AP(self.etv, h * TVL, [[1, 128], [1, ETW]])
                P.dma('sync', hk[:], src, reads=[b5], writes=[b6])
                c0 = 0
                while c0 < ETW:
                    n = min(512, ETW - c0)
                    pt, pb = self.ps()
                    self.mm(pt[:, 0:n], jf[:, :], hk[:, c0:c0 + n], True, True, reads=[b3, b6], writes=[pb])
                    self.op('vector', 'tensor_copy', reads=[pb], writes=[b7], out=tt[:, c0:c0 + n], in_=pt[:, 0:n])
                    c0 += n
                P.dma('sync', self.ett[h], tt[:], reads=[b7], writes=[Buf()])

    def rmsnorm(self, es, src_fn, dst_fn, nch, gcol_fn, groups, rbufs, wbufs, tag, in_dt=F32):
        sq = [self.sb('sq%s%d' % (tag, k), [128, 512], BF16, es) for k in range(2)]
        bsq = [Buf(), Buf()]
        rstd = self.sb('rstd' + tag, [128, 512], F32, es)
        brs = Buf()
        for (c0, c1, gi) in groups:
            n = c1 - c0
            pt, pb = self.ps()
            for ch in range(nch):
                k = ch % 2
                self.act(sq[k][:, 0:n], src_fn(ch, c0, c1), AF.Square, reads=rbufs[gi], writes=[bsq[k]])
                self.mm(pt[:, 0:n], self.ones_bf[:, :], sq[k][:, 0:n], ch == 0, ch == nch - 1, reads=[bsq[k]], writes=[pb])
            self.op('vector', 'tensor_scalar', reads=[pb], writes=[brs], out=rstd[:, 0:n], in0=pt[:, 0:n],
                    scalar1=1.0 / (nch * 128), scalar2=EPS, op0=ALU.mult, op1=ALU.add)
            self.act(rstd[:, 0:n], rstd[:, 0:n], AF.Sqrt, reads=[brs], writes=[brs])
            self.op('vector', 'reciprocal', reads=[brs], writes=[brs], out=rstd[:, 0:n], in_=rstd[:, 0:n])
            for ch in range(nch):
                self.op('vector', 'scalar_tensor_tensor', reads=[brs, self.b_gcol] + list(rbufs[gi]), writes=wbufs[gi],
                        out=dst_fn(ch, c0, c1), in0=src_fn(ch, c0, c1), scalar=gcol_fn(ch), in1=rstd[:, 0:n],
                        op0=ALU.mult, op1=ALU.mult)
    def wload(self, wt, wb, src):
        nk, ncol = wt.shape[1], wt.shape[2]
        g = max(1, 512 // ncol)
        for k0 in range(0, nk, g):
            k1 = min(nk, k0 + g)
            stg_t, stg_b, stg_n = self.cur_stg
            i = self.stg_i % stg_n
            self.stg_i += 1
            st = stg_t[:, i, 0:(k1 - k0) * ncol].rearrange("p (a b) -> p a b", b=ncol)
            self.P.dma('sync', st, src[:, k0:k1, :], writes=[stg_b[i]])
            eng = ('gpsimd', 'scalar', 'vector', 'scalar', 'vector')[self.stg_i % 5]
            subs = [wb.sub[kc] for kc in range(k0, k1)]
            if eng == 'scalar':
                self.act(wt[:, k0:k1, :], st, AF.Copy, reads=[stg_b[i]], writes=subs)
            else:
                self.op(eng, 'tensor_copy', reads=[stg_b[i]], writes=subs, out=wt[:, k0:k1, :], in_=st)

    def gelu_from_psum(self, es_bufs, pt, pb, rows, n, out_ap, wb_out):
        t1, t2, b1, b2 = es_bufs
        x = pt[0:rows, 0:n]
        self.act(t1[0:rows, 0:n], x, AF.Square, reads=[pb], writes=[b1])
        self.op('vector', 'tensor_scalar', reads=[b1], writes=[b1], out=t1[0:rows, 0:n], in0=t1[0:rows, 0:n],
                scalar1=0.044715, scalar2=1.0, op0=ALU.mult, op1=ALU.add)
        self.op('vector', 'tensor_tensor', reads=[b1, pb], writes=[b2], out=t2[0:rows, 0:n], in0=t1[0:rows, 0:n], in1=x, op=ALU.mult)
        self.act(t1[0:rows, 0:n], t2[0:rows, 0:n], AF.Sigmoid, reads=[b2], writes=[b1], scale=1.5957691216057308)
        self.op('vector', 'tensor_tensor', reads=[b1, pb], writes=wb_out, out=out_ap, in0=t1[0:rows, 0:n], in1=x, op=ALU.mult)

    def chunk_layer(self, l, j):
        nc, P = self.nc, self.P
        base = j * TOK
        groups = self.groups(j)
        samp = (j == 0)
        xT = self.xT
        last = (l == self.DEPTH - 1)
        self.cur_stg = (self.wstg, self.bstg, 4)
        with self.phase() as es:
            if l == 0:
                xin = [self.sb('xin%d' % k, [128, 2048], F32, es) for k in range(2)]
                bxin = [Buf(), Buf()]
                for tt in range(8):
                    k = tt % 2
                    P.dma('sync', xin[k][:], self.i['xp'][base + tt * 128: base + (tt + 1) * 128, :], writes=[bxin[k]])
                    for c4 in range(4):
                        pt, pb = self.ps()
                        for q in range(4):
                            ch = c4 * 4 + q
                            self.mm(pt[:, q * 128:(q + 1) * 128], xin[k][:, ch * 128:(ch + 1) * 128], self.ident[:, :], True, True, reads=[bxin[k]], writes=[pb])
                        self.op('vector', 'tensor_copy', reads=[pb], writes=[self.bx[tt // 4]],
                                out=xT[:, c4 * 4:(c4 + 1) * 4, tt * 128:(tt + 1) * 128],
                                in_=pt[:, :].rearrange("p (q t) -> p q t", q=4))
                if samp:
                    P.dma('sync', xin[0][0:8, :], self.i['xs'], writes=[bxin[0]])
                    for c4 in range(4):
                        pt, pb = self.ps()
                        for q in range(4):
                            ch = c4 * 4 + q
                            self.mm(pt[:, q * 8:(q + 1) * 8], xin[0][0:8, ch * 128:(ch + 1) * 128], self.ident[0:8, 0:8], True, True, reads=[bxin[0]], writes=[pb])
                        self.op('vector', 'tensor_copy', reads=[pb], writes=[self.bx[2]],
                                out=xT[:, c4 * 4:(c4 + 1) * 4, 1024:1032], in_=pt[:, 0:32].rearrange("p (q t) -> p q t", q=4))
            else:
                for ch in range(NKC):
                    P.dma('sync', xT[:, ch, 0:1024], self.xscr[ch * 128:(ch + 1) * 128, base:base + 1024],
                          reads=[self.b_xscr[j]], writes=[self.bx[0], self.bx[1]])
                if samp:
                    self.op('vector', 'tensor_copy', writes=[self.bx[2]], out=xT[:, :, 1024:1032], in_=self.xS[:, :, :])
        with self.phase() as es_mix:
            hm = self.sb('hm', [128, NKC, TW], BF16, es_mix)
            mixedT = hm
            bmix = {gi: [Buf()] for gi in range(3)}
            with self.phase() as es_u:
                uT = self.sb('uT', [128, 4, TW], BF16, es_u)
                bu = Buf()
                with self.phase() as es_g:
                    guT = self.sb('guT', [128, 4, TW], BF16, es_g)
                    gv = self.sb('gv', [128, 8, 512], BF16, es_g)
                    gvs = self.sb('gvs', [8, 512], BF16, es_g)
                    bgu, bgv = Buf(), Buf()
                    with self.phase() as es_q:
                        qT = self.sb('qT', [128, 8, TW], BF16, es_q)
                        KTs = self.sb('KTs', [128, 8, 8], BF16, es_q)
                        Vs = self.sb('Vs', [8, 1024], BF16, es_q)
                        bq, bkts, bvs = Buf(), Buf(), Buf()
                        if KSTOP >= 1:
                            self.phase_A(l, j, groups, samp, hm, qT, uT, guT, gv, gvs, KTs, Vs, bq, bu, bgu, bgv, bkts, bvs)
                        if KSTOP >= 2:
                            self.phase_att(l, j, samp, qT, KTs, Vs, mixedT, bmix, bq, bkts, bvs)
                    if KSTOP >= 3:
                        self.phase_sgu(l, j, samp, guT, gv, gvs, bgu, bgv, mixedT, bmix)
                if KSTOP >= 4:
                    self.phase_ssm(l, j, groups, samp, uT, bu, mixedT, bmix)
            if KDBG and l == 0 and j == 0:
                self.P.fence()
                self.P.dma('gpsimd', self.o['dbg'], hm[:].rearrange("p a b -> p (a b)"), writes=[self.b_out])
                self.P.fence()
            if KSTOP >= 5:
                self.phase_out(l, j, groups, mixedT, bmix)
            self.cur_stg = (self.wstg, self.bstg, 4)
        if KSTOP >= 6:
            self.phase_ffn(l, j, groups)
        with self.phase() as es:
            if samp:
                self.op('vector', 'tensor_copy', reads=[self.bx[2]], writes=[Buf()], out=self.xS[:, :, :], in_=xT[:, :, 1024:1032])
            if not last:
                for ch in range(NKC):
                    P.dma('sync', self.xscr[ch * 128:(ch + 1) * 128, base:base + 1024], xT[:, ch, 0:1024],
                          reads=[self.bx[0], self.bx[1]], writes=[self.b_xscr[j]])
            else:
                yo = [self.sb('yo%d' % k, [128, 2048], F32, es) for k in range(2)]
                byo = [Buf(), Buf()]
                for tt in range(8):
                    k = tt % 2
                    for c4 in range(4):
                        pt, pb = self.ps()
                        for q in range(4):
                            ch = c4 * 4 + q
                            self.mm(pt[:, q * 128:(q + 1) * 128], xT[:, ch, tt * 128:(tt + 1) * 128], self.ident[:, :], True, True,
                                    reads=[self.bx[tt // 4]], writes=[pb])
                        self.op('vector', 'tensor_copy', reads=[pb], writes=[byo[k]], out=yo[k][:, c4 * 512:(c4 + 1) * 512], in_=pt[:, :])
                    P.dma('sync', self.o['yp'][base + tt * 128: base + (tt + 1) * 128, :], yo[k][:], reads=[byo[k]], writes=[self.b_out])
                if samp:
                    for c4 in range(4):
                        pt, pb = self.ps()
                        for q in range(4):
                            ch = c4 * 4 + q
                            self.mm(pt[0:8, q * 128:(q + 1) * 128], xT[:, ch, 1024:1032], self.ident[:, :], True, True, reads=[self.bx[2]], writes=[pb])
                        self.op('vector', 'tensor_copy', reads=[pb], writes=[byo[0]], out=yo[0][0:8, c4 * 512:(c4 + 1) * 512], in_=pt[0:8, :])
                    P.dma('sync', self.o['ys'], yo[0][0:8, :], reads=[byo[0]], writes=[self.b_out])

    def phase_A(self, l, j, groups, samp, hT, qT, uT, guT, gv, gvs, KTs, Vs, bq, bu, bgu, bgv, bkts, bvs):
        nc, P = self.nc, self.P
        base = j * TOK
        xT = self.xT
        keep_lo = self.L - self.NKEEP
        with self.phase() as es:
            bh = {gi: [Buf()] for gi in range(3)}
            with self.phase() as es2:
                self.rmsnorm(es2, lambda ch, c0, c1: xT[:, ch, c0:c1], lambda ch, c0, c1: hT[:, ch, c0:c1], NKC,
                             lambda ch: self.gcol[:, ch:ch + 1], groups, {gi: [self.bx[gi]] for gi in range(3)}, bh, 'a')
            allh = [bh[g[2]][0] for g in groups]
            wbuf = [self.sb('wA%d' % k, [128, NKC, 512], BF16, es) for k in range(2)]
            bw = [WB(NKC), WB(NKC)]
            kst = [self.sb('kst%d' % k, [128, 1024], BF16, es) for k in range(2)]
            bkst = [Buf(), Buf()]
            vst = [self.sb('vst%d' % k, [128, 512], BF16, es) for k in range(2)]
            bvst = [Buf(), Buf()]
            fst = [self.sb('fst%d' % k, [128, 512], F32, es) for k in range(2)]
            bfst = [Buf(), Buf()]
            t1 = self.sb('gt1', [128, 512], F32, es)
            t2 = self.sb('gt2', [128, 512], F32, es)
            gb = (t1, t2, Buf(), Buf())
            lnx = self.sb('lnx', [128, 512], F32, es)
            blnx = Buf()
            lnj = self.sb('lnj', [128, 512], F32, es)
            sm = self.sb('lnsm', [128, 8], F32, es)
            bsm = Buf()
            sgug = self.sb('sgug', [128, 512], F32, es)
            bsg = Buf()
            P.dma('sync', sgug[:], self.i['sgug'][l], writes=[bsg])
            win = self.i['w_in'][l]
            cnt = 0
            def _wlA(b_):
                self.wload(wbuf[b_ % 2][:], bw[b_ % 2], win[:, b_ * 512:(b_ + 1) * 512].rearrange("(kc p) n -> p kc n", p=128))
            _wlA(0)
            for blk in range(9):
                if blk >= KSUB:
                    break
                k = blk % 2
                if blk + 1 < 9:
                    _wlA(blk + 1)
                fm = blk in (0, 1, 2, 3, 6, 7)
                if fm:
                    for cc in range(4):
                        head = (blk % 2) * 4 + cc
                        for (c0, c1, gi) in groups:
                            n = c1 - c0
                            pt, pb = self.ps()
                            for kc in range(NKC):
                                self.mm(pt[:, 0:n], wbuf[k][:, kc, cc * 128:(cc + 1) * 128], hT[:, kc, c0:c1], kc == 0, kc == NKC - 1,
                                        reads=[bw[k].sub[kc], bh[gi][0]], writes=[pb])
                            if blk in (0, 1):
                                self.act(qT[:, head, c0:c1], pt[:, 0:n], AF.Copy, reads=[pb], writes=[bq])
                            elif blk in (2, 3):
                                if gi < 2:
                                    kk = head % 2
                                    self.act(kst[kk][:, c0:c1], pt[:, 0:n], AF.Copy, reads=[pb], writes=[bkst[kk]])
                                    if gi == 1:
                                        P.dma('sync', self.ktscr[head * 128:(head + 1) * 128, base:base + 1024], kst[kk][:, :],
                                              reads=[bkst[kk]], writes=[self.b_kt[j]])
                                else:
                                    self.act(KTs[:, head, :], pt[:, 0:n], AF.Copy, reads=[pb], writes=[bkts])
                            elif blk == 6:
                                self.act(uT[:, cc, c0:c1], pt[:, 0:n], AF.Copy, reads=[pb], writes=[bu])
                            else:
                                self.gelu_from_psum(gb, pt, pb, 128, n, guT[:, cc, c0:c1], [bgu])
                tokmaj = blk in (4, 5, 8) or (blk in (2, 3))
                if tokmaj:
                    tts = list(range(8)) + ([8] if samp else [])
                    for tt in tts:
                        rows = 128 if tt < 8 else 8
                        tok0 = base + tt * 128
                        if blk in (2, 3):
                            if tt < 8 and tok0 < keep_lo:
                                continue
                        c0 = tt * 128 if tt < 8 else 1024
                        gi = (tt // 4) if tt < 8 else 2
                        pt, pb = self.ps()
                        for kc in range(NKC):
                            self.mm(pt[0:rows, :], hT[:, kc, c0:c0 + rows], wbuf[k][:, kc, :], kc == 0, kc == NKC - 1,
                                    reads=[bw[k].sub[kc], bh[gi][0]], writes=[pb])
                        half = blk % 2
                        if blk in (2, 3, 4, 5):
                            isk = blk in (2, 3)
                            if tt < 8:
                                if not isk and KVAR != 1:
                                    kk = cnt % 2
                                    self.op('vector', 'tensor_copy', reads=[pb], writes=[bvst[kk]], out=vst[kk][:, :], in_=pt[:, :])
                                    P.dma('sync', self.vscr[tok0:tok0 + 128, half * 512:(half + 1) * 512], vst[kk][:, :],
                                          reads=[bvst[kk]], writes=[self.b_v[j]])
                                if tok0 >= keep_lo:
                                    kk = cnt % 2
                                    self.act(fst[kk][:, :], pt[:, :], AF.Copy, reads=[pb], writes=[bfst[kk]])
                                    dst = self.o['nk' if isk else 'nv'][l, tok0 - keep_lo: tok0 - keep_lo + 128, half * 512:(half + 1) * 512]
                                    P.dma('sync', dst, fst[kk][:, :], reads=[bfst[kk]], writes=[self.b_out])
                                cnt += 1
                            else:
                                kk = cnt % 2
                                cnt += 1
                                if not isk and KVAR != 2:
                                    self.op('vector', 'tensor_copy', reads=[pb], writes=[bvs], out=Vs[:, half * 512:(half + 1) * 512], in_=pt[0:8, :])
                                self.act(fst[kk][0:8, :], pt[0:8, :], AF.Copy, reads=[pb], writes=[bfst[kk]])
                                dst = self.o['nks' if isk else 'nvs'][l, :, half * 512:(half + 1) * 512]
                                P.dma('sync', dst, fst[kk][0:8, :], reads=[bfst[kk]], writes=[self.b_out])
                        else:
                            self.gelu_from_psum(gb, pt, pb, rows, 512, lnx[0:rows, :], [blnx])
                            self.op('vector', 'tensor_reduce', reads=[blnx], writes=[bsm], out=sm[0:rows, 0:1], in_=lnx[0:rows, :], axis=AX.X, op=ALU.add)
                            self.act(lnj[0:rows, :], lnx[0:rows, :], AF.Square, reads=[blnx], writes=[gb[2]])
                            self.op('vector', 'tensor_reduce', reads=[gb[2]], writes=[bsm], out=sm[0:rows, 1:2], in_=lnj[0:rows, :], axis=AX.X, op=ALU.add)
                            self.op('vector', 'tensor_scalar', reads=[bsm], writes=[bsm], out=sm[0:rows, 2:3], in0=sm[0:rows, 0:1], scalar1=1.0 / 512, scalar2=None, op0=ALU.mult)
                            self.op('vector', 'tensor_tensor', reads=[bsm], writes=[bsm], out=sm[0:rows, 3:4], in0=sm[0:rows, 2:3], in1=sm[0:rows, 2:3], op=ALU.mult)
                            self.op('vector', 'scalar_tensor_tensor', reads=[bsm], writes=[bsm], out=sm[0:rows, 4:5], in0=sm[0:rows, 1:2], scalar=1.0 / 512, in1=sm[0:rows, 3:4], op0=ALU.mult, op1=ALU.subtract)
                            self.op('vector', 'tensor_scalar', reads=[bsm], writes=[bsm], out=sm[0:rows, 5:6], in0=sm[0:rows, 4:5], scalar1=EPS, scalar2=None, op0=ALU.add)
                            self.act(sm[0:rows, 5:6], sm[0:rows, 5:6], AF.Sqrt, reads=[bsm], writes=[bsm])
                            self.op('vector', 'reciprocal', reads=[bsm], writes=[bsm], out=sm[0:rows, 5:6], in_=sm[0:rows, 5:6])
                            self.op('vector', 'tensor_scalar', reads=[blnx, bsm], writes=[blnx], out=lnx[0:rows, :], in0=lnx[0:rows, :],
                                    scalar1=sm[0:rows, 2:3], scalar2=sm[0:rows, 5:6], op0=ALU.subtract, op1=ALU.mult)
                            if tt < 8:
                                self.op('vector', 'tensor_tensor', reads=[blnx, bsg], writes=[bgv], out=gv[:, tt, :], in0=lnx[:, :], in1=sgug[:, :], op=ALU.mult)
                            else:
                                kk = 0
                                self.op('vector', 'tensor_tensor', reads=[blnx, bsg], writes=[bfst[kk]], out=fst[kk][0:8, :], in0=lnx[0:8, :], in1=sgug[0:8, :], op=ALU.mult)
                                self.op('vector', 'tensor_copy', reads=[bfst[kk]], writes=[bgv], out=gvs[:, :], in_=fst[kk][0:8, :])
                                P.dma('sync', self.o['nsgu'][l], fst[kk][0:8, :], reads=[bfst[kk]], writes=[self.b_out])
    def phase_att(self, l, j, samp, qT, KTs, Vs, mixedT, bmix, bq, bkts, bvs):
        nc, P = self.nc, self.P
        base = j * TOK
        lo_tile = max(0, 16 - 8 * j)
        lo_tok = base - 2048 + lo_tile * 128
        ntile = 24 - lo_tile
        with self.phase() as es:
            KTh = [self.sb('KTh%d' % k, [128, 3072], BF16, es) for k in range(2)]
            Vh = [self.sb('Vh%d' % k, [128, 24, 128], BF16, es) for k in range(2)]
            Eh = [self.sb('Eh%d' % k, [128, ETW], BF16, es) for k in range(2)]
            bKT, bV, bE = [Buf(), Buf()], [Buf(), Buf()], [Buf(), Buf()]
            Pt = [self.sb('Pt%d' % k, [128, 512], BF16, es) for k in range(3)]
            Pm = [self.sb('Pm%d' % k, [128, 512], BF16, es) for k in range(3)]
            bPt, bPm = [Buf() for _ in range(3)], [Buf() for _ in range(3)]
            rz = self.sb('rz', [128, 512], F32, es)
            brz = Buf()
            if samp:
                ckt = self.sb('ckt', [128, 16, 128], F32, es)
                bck = Buf()
                KTc = self.sb('KTc', [128, 2048], BF16, es)
                bKTc = Buf()
                Vc = self.sb('Vc', [128, 16, 128], BF16, es)
                bVc = Buf()
            pc = 0
            self.ps_n = 4
            accn = 0
            for h in range(8):
                k = h % 2
                P.dma('sync', KTh[k][:, lo_tile * 128:3072], self.ktscr[h * 128:(h + 1) * 128, lo_tok:base + 1024],
                      reads=[self.b_kt[jj] for jj in range(max(0, j - 2), j + 1)], writes=[bKT[k]])
                vsrc = self.vscr[lo_tok:base + 1024, h * 128:(h + 1) * 128].rearrange("(t p) e -> p t e", p=128)
                P.dma('sync', Vh[k][:, lo_tile:24, :], vsrc,
                      reads=[self.b_v[jj] for jj in range(max(0, j - 2), j + 1)], writes=[bV[k]])
                P.dma('sync', Eh[k][:, :], self.ett[h], writes=[bE[k]])
                for qg in range(2):
                    G = 16 + 4 * qg
                    q0 = qg * 512
                    po, pbo = self.psb[4 + 2 * (accn % 2)]
                    pz, pbz = self.psb[5 + 2 * (accn % 2)]
                    accn += 1
                    kts = [kt for kt in range(G - 16, G + 4) if kt >= lo_tile]
                    kts = [G] + [kt for kt in kts if kt != G]
                    for idx, kt in enumerate(kts):
                        i0 = max(0, kt - G)
                        i1 = min(3, kt - G + 16)
                        c0, c1 = i0 * 128, (i1 + 1) * 128
                        n = c1 - c0
                        e0 = (G + i0 - kt) * 128
                        pt, pb = self.ps()
                        m = pc % 3
                        pc += 1
                        self.mm(pt[:, 0:n], KTh[k][:, kt * 128:(kt + 1) * 128], qT[:, h, q0 + c0:q0 + c1], True, True,
                                reads=[bKT[k], bq], writes=[pb])
                        self.act(Pt[m][:, 0:n], pt[:, 0:n], AF.Exp, reads=[pb], writes=[bPt[m]], scale=ATT_SCALE)
                        self.op(self.alt(), 'tensor_tensor', reads=[bPt[m], bE[k]], writes=[bPm[m]], out=Pm[m][:, 0:n], in0=Pt[m][:, 0:n],
                                in1=Eh[k][:, e0:e0 + n], op=ALU.mult)
                        st, sp = idx == 0, idx == len(kts) - 1
                        self.mm(po[:, c0:c1], Vh[k][:, kt, :], Pm[m][:, 0:n], st, sp, reads=[bV[k], bPm[m]], writes=[pbo], skip=True)
                        self.mm(pz[:, c0:c1], self.ones_bf[:, :], Pm[m][:, 0:n], st, sp, reads=[bPm[m]], writes=[pbz], skip=True)
                    self.op('vector', 'reciprocal', reads=[pbz], writes=[brz], out=rz[:, :], in_=pz[:, :])
                    self.op('vector', 'tensor_tensor', reads=[pbo, brz], writes=bmix[qg], out=mixedT[:, h, q0:q0 + 512], in0=po[:, :], in1=rz[:, :], op=ALU.mult)
                if samp:
                    c = None
                    P.dma('sync', ckt[:], self.i['ck'][l][:, h * 128:(h + 1) * 128].rearrange("(t p) e -> p t e", p=128), writes=[bck])
                    P.dma('gpsimd', Vc[:], self.i['cv'][l][:, h * 128:(h + 1) * 128].rearrange("(t p) e -> p t e", p=128), writes=[bVc])
                    for c4 in range(4):
                        pt, pb = self.ps()
                        for q in range(4):
                            ct = c4 * 4 + q
                            self.mm(pt[:, q * 128:(q + 1) * 128], ckt[:, ct, :], self.ident[:, :], True, True, reads=[bck], writes=[pb])
                        self.act(KTc[:, c4 * 512:(c4 + 1) * 512], pt[:, :], AF.Copy, reads=[pb], writes=[bKTc])
                    ps_, pbs = self.ps()
                    qs = qT[:, h, 1024:1032]
                    self.op('vector', 'memset', writes=[pbs], ap=ps_[:, 0:8], constant=0.0)
                    self.mm(ps_[0:8, 0:8], KTs[:, h, :], qs, True, True, reads=[bkts, bq], writes=[pbs], skip=True)
                    for s in range(1, 17):
                        ct = 16 - s
                        self.mm(ps_[:, s * 8:(s + 1) * 8], KTc[:, ct * 128:(ct + 1) * 128], qs, True, True, reads=[bKTc, bq], writes=[pbs], skip=True)
                    m = pc % 3
                    pc += 1
                    self.act(Pt[m][:, 0:136], ps_[:, 0:136], AF.Exp, reads=[pbs], writes=[bPt[m]], scale=ATT_SCALE)
                    self.op('vector', 'tensor_tensor', reads=[bPt[m], bE[k]], writes=[bPm[m]],
                            out=Pm[m][:, 0:136].rearrange("p (s t) -> p s t", t=8), in0=Pt[m][:, 0:136].rearrange("p (s t) -> p s t", t=8),
                            in1=Eh[k][:, :].rearrange("p (s t) -> p s t", t=128)[:, :, 0:8], op=ALU.mult)
                    po, pbo = self.psb[4 + 2 * (accn % 2)]
                    pz, pbz = self.psb[5 + 2 * (accn % 2)]
                    accn += 1
                    self.mm(po[:, 0:8], Vs[0:8, h * 128:(h + 1) * 128], Pm[m][0:8, 0:8], True, False, reads=[bvs, bPm[m]], writes=[pbo], skip=True)
                    self.mm(pz[:, 0:8], self.ones_bf[0:8, :], Pm[m][0:8, 0:8], True, False, reads=[bPm[m]], writes=[pbz], skip=True)
                    for s in range(1, 17):
                        ct = 16 - s
                        self.mm(po[:, 0:8], Vc[:, ct, :], Pm[m][:, s * 8:(s + 1) * 8], False, s == 16, reads=[bVc, bPm[m]], writes=[pbo], skip=True)
                        self.mm(pz[:, 0:8], self.ones_bf[:, :], Pm[m][:, s * 8:(s + 1) * 8], False, s == 16, reads=[bPm[m]], writes=[pbz], skip=True)
                    self.op('vector', 'reciprocal', reads=[pbz], writes=[brz], out=rz[:, 0:8], in_=pz[:, 0:8])
                    self.op('vector', 'tensor_tensor', reads=[pbo, brz], writes=bmix[2], out=mixedT[:, h, 1024:1032], in0=po[:, 0:8], in1=rz[:, 0:8], op=ALU.mult)
        self.ps_n = 8
        groups = self.groups(j)
        with self.phase() as es2:
            self.rmsnorm(es2, lambda ch, c0, c1: mixedT[:, ch, c0:c1], lambda ch, c0, c1: mixedT[:, ch, c0:c1], 8,
                         lambda ch: self.gcol[:, 32 + ch:33 + ch], groups, bmix, bmix, 'b')

    def sin_reduced(self, dst, x, tf, ti, rb, wb):
        V = lambda name, **kw: self.op('vector', name, reads=rb + wb, writes=wb, **kw)
        V('tensor_scalar', out=tf, in0=x, scalar1=1.0 / TWO_PI, scalar2=0.5, op0=ALU.mult, op1=ALU.add)
        V('tensor_copy', out=ti, in_=tf)
        V('tensor_copy', out=tf, in_=ti)
        V('scalar_tensor_tensor', out=x, in0=tf, scalar=-TWO_PI, in1=x, op0=ALU.mult, op1=ALU.add)
        V('tensor_scalar', out=tf, in0=x, scalar1=-math.pi, scalar2=TWO_PI, op0=ALU.is_lt, op1=ALU.mult)
        V('tensor_tensor', out=x, in0=x, in1=tf, op=ALU.add)
        V('tensor_scalar', out=tf, in0=x, scalar1=math.pi, scalar2=-TWO_PI, op0=ALU.is_gt, op1=ALU.mult)
        V('tensor_tensor', out=x, in0=x, in1=tf, op=ALU.add)
        self.act(dst, x, AF.Sin, reads=rb + wb, writes=wb)

    def phase_ssm(self, l, j, groups, samp, uT, bu, mixedT, bmix):
        nc, P = self.nc, self.P
        with self.phase() as es:
            es_s = contextlib.ExitStack()
            cols = self.sb('ssmcols', [128, 48], F32, es)
            dcol = self.sb('dcol', [128, 8], F32, es)
            pr = self.sb('ssmpr', [128, 16, 16], F32, es)
            Bb = self.sb('Bb', [128, 16, 2, 128], BF16, es)
            Cb = self.sb('Cb', [128, 16, 2, 128], BF16, es)
            wg = self.sb('wglu', [128, 4, 512], BF16, es)
            bbr = self.sb('bbr', [128, 16, 128], F32, es_s)
            bbi = self.sb('bbi', [128, 16, 128], F32, es_s)
            b0, bpr, bbb, bBb, bCb, bwg = [Buf() for _ in range(6)]
            P.dma('sync', cols[:], self.i['ssmcols'][l], writes=[b0])
            P.dma('sync', dcol[:], self.i['dcols'][l], writes=[b0])
            P.dma('sync', bbr[:], self.i['bbd_re'][l].rearrange("s p n -> p s n"), writes=[bbb])
            P.dma('sync', bbi[:], self.i['bbd_im'][l].rearrange("s p n -> p s n"), writes=[bbb])
            P.dma('gpsimd', Cb[:, :, 0, :], self.i['cbd_re'][l].rearrange("s p n -> p s n"), writes=[bCb])
            P.dma('gpsimd', Cb[:, :, 1, :], self.i['cbd_im'][l].rearrange("s p n -> p s n"), writes=[bCb])
            P.dma('gpsimd', wg[:], self.i['wglu'][l].rearrange("(kc p) n -> p kc n", p=128), writes=[bwg])
            lre, lim, lst = cols[:, 0:16], cols[:, 16:32], cols[:, 32:48]
            K = lambda i: pr[:, i, :]
            V = lambda name, **kw: self.op('vector', name, reads=[b0, bpr], writes=[bpr], **kw)
            self.act(K(0), lst, AF.Exp, reads=[b0], writes=[bpr])
            V('tensor_tensor', out=K(1), in0=lre, in1=K(0), op=ALU.mult)
            V('tensor_tensor', out=K(2), in0=lim, in1=K(0), op=ALU.mult)
            self.act(K(3), K(1), AF.Exp, reads=[bpr], writes=[bpr])
            kti = self.sb('kti', [128, 16], I32, es_s)
            V('tensor_scalar', out=K(11), in0=K(2), scalar1=TWO_PI, scalar2=None, op0=ALU.add)
            self.sin_reduced(K(5), K(11), K(12), kti[:, :], [b0], [bpr])
            V('tensor_scalar', out=K(11), in0=K(2), scalar1=TWO_PI + 0.5 * math.pi, scalar2=None, op0=ALU.add)
            self.sin_reduced(K(4), K(11), K(12), kti[:, :], [b0], [bpr])
            V('tensor_tensor', out=K(6), in0=K(3), in1=K(4), op=ALU.mult)
            V('tensor_scalar', out=K(6), in0=K(6), scalar1=-1.0, scalar2=None, op0=ALU.add)
            V('tensor_tensor', out=K(7), in0=K(3), in1=K(5), op=ALU.mult)
            V('tensor_tensor', out=K(8), in0=lre, in1=lre, op=ALU.mult)
            V('tensor_tensor', out=K(11), in0=lim, in1=lim, op=ALU.mult)
            V('tensor_tensor', out=K(8), in0=K(8), in1=K(11), op=ALU.add)
            V('reciprocal', out=K(8), in_=K(8))
            V('tensor_tensor', out=K(9), in0=K(6), in1=lre, op=ALU.mult)
            V('tensor_tensor', out=K(11), in0=K(7), in1=lim, op=ALU.mult)
            V('tensor_tensor', out=K(9), in0=K(9), in1=K(11), op=ALU.add)
            V('tensor_tensor', out=K(9), in0=K(9), in1=K(8), op=ALU.mult)
            V('tensor_tensor', out=K(10), in0=K(7), in1=lre, op=ALU.mult)
            V('tensor_tensor', out=K(11), in0=K(6), in1=lim, op=ALU.mult)
            V('tensor_tensor', out=K(10), in0=K(10), in1=K(11), op=ALU.subtract)
            V('tensor_tensor', out=K(10), in0=K(10), in1=K(8), op=ALU.mult)
            ones_f = self.sb('ones_f', [128, 128], F32, es_s)
            crow = self.sb('crow', [128, 16, 2, 128], F32, es_s)
            bcr = Buf()
            self.op('vector', 'memset', writes=[bcr], ap=ones_f[:], constant=1.0)
            dg = self.sb('dg', [128, 128], F32, es_s)
            bdg = Buf()
            for sc in range(16):
                for ri in range(2):
                    self.op('vector', 'tensor_scalar', reads=[bpr], writes=[bdg], out=dg[:, :], in0=self.ident[:, :],
                            scalar1=pr[:, 9 + ri, sc:sc + 1], scalar2=None, op0=ALU.mult)
                    pt, pb = self.ps()
                    self.mm(pt[:, 0:128], ones_f[:, :], dg[:, :], True, True, reads=[bdg, bcr], writes=[pb])
                    self.op('vector', 'tensor_copy', reads=[pb], writes=[bcr], out=crow[:, sc, ri, :], in_=pt[:, 0:128])
            tb = self.sb('tb', [128, 16, 128], F32, es_s)
            tb2 = self.sb('tb2', [128, 16, 128], F32, es_s)
            btb = Buf()
            G = lambda name, **kw: self.op('gpsimd', name, reads=[bbb, bcr, btb], writes=[btb], **kw)
            G('tensor_tensor', out=tb[:], in0=bbr[:], in1=crow[:, :, 0, :], op=ALU.mult)
            G('tensor_tensor', out=tb2[:], in0=bbi[:], in1=crow[:, :, 1, :], op=ALU.mult)
            self.op('gpsimd', 'tensor_tensor', reads=[btb], writes=[bBb], out=Bb[:, :, 0, :], in0=tb[:], in1=tb2[:], op=ALU.subtract)
            G('tensor_tensor', out=tb[:], in0=bbr[:], in1=crow[:, :, 1, :], op=ALU.mult)
            G('tensor_tensor', out=tb2[:], in0=bbi[:], in1=crow[:, :, 0, :], op=ALU.mult)
            self.op('gpsimd', 'tensor_tensor', reads=[btb], writes=[bBb], out=Bb[:, :, 1, :], in0=tb[:], in1=tb2[:], op=ALU.add)
            self.op('gpsimd', 'tensor_scalar', reads=[bCb], writes=[bCb], out=Cb[:, :, 1, :], in0=Cb[:, :, 1, :], scalar1=-1.0, scalar2=None, op0=ALU.mult)
            self.P.fence()
            es_s.close()
            iot = self.sb('iot', [128, 1025], F32, es)
            P.dma('sync', iot[:], self.i['iota'], writes=[Buf()])
            cs = self.sb('cs', [128, 1025], F32, es)
            tfl = self.sb('tfl', [128, 1025], F32, es)
            tin = self.sb('tin', [128, 1025], I32, es)
            sn = self.sb('sn', [128, 1025], F32, es)
            wr = self.sb('wr', [128, 1024], F32, es)
            wi = self.sb('wi', [128, 1024], F32, es)
            ta = self.sb('ta', [128, 1024], F32, es)
            tc = self.sb('tc', [128, 1024], F32, es)
            hr = self.sb('hr', [128, 1024], BF16, es)
            hi = self.sb('hi', [128, 1024], BF16, es)
            ini = self.sb('ini', [128, 8], F32, es)
            yT = self.sb('ysT', [128, 4, TW], BF16, es)
            yf = self.sb('ysf', [128, 4, TW], F32, es)
            bcs, bsn, bwr, bwi, bta, btc, bhr, bhi, bini, byT = [Buf() for _ in range(10)]
            self.P.fence()
            if j == 0:
                self.op('vector', 'memset', reads=[self.b_H], writes=[self.b_H], ap=self.Hst[:, 0:2, :], constant=0.0)
                if samp:
                    P.dma('sync', self.Hst[:, 2, :], self.i['sre'][l], writes=[self.b_H])
                    P.dma('sync', self.Hst[:, 3, :], self.i['sim'][l], writes=[self.b_H])
            runs = [(0, 1024, 0)] + ([(1024, 8, 2)] if samp else [])
            self.ps_n = 5
            ypsum = {}
            for sc in range(16):
                uc, oc = sc // 4, sc // 4
                th = pr[:, 2, sc:sc + 1]
                for (tab, btab, off) in ((sn, bsn, TWO_PI), (cs, bcs, TWO_PI + 0.5 * math.pi)):
                    self.op('vector', 'tensor_scalar', reads=[bpr], writes=[btab], out=tab[:, :], in0=iot[:, :], scalar1=th, scalar2=off, op0=ALU.mult, op1=ALU.add)
                    self.sin_reduced(tab[:, :], tab[:, :], tfl[:, :], tin[:, :], [bpr], [btab])
                for (t0, T, hs) in runs:
                    xs = []
                    for ri in range(2):
                        for c0 in range(0, T, 512):
                            n = min(512, T - c0)
                            pt, pb = self.ps()
                            self.mm(pt[:, 0:n], Bb[:, sc, ri, :], uT[:, uc, t0 + c0:t0 + c0 + n], True, True, reads=[bBb, bu], writes=[pb])
                            xs.append((ri, c0, n, pt, pb))
                    for (ri, c0, n, pt, pb) in xs:
                        x = pt[:, 0:n]
                        if ri == 0:
                            self.op('vector', 'tensor_tensor', reads=[pb, bcs], writes=[bwr], out=wr[:, c0:c0 + n], in0=x, in1=cs[:, c0:c0 + n], op=ALU.mult)
                            self.op('vector', 'tensor_tensor', reads=[pb, bsn], writes=[bta], out=ta[:, c0:c0 + n], in0=x, in1=sn[:, c0:c0 + n], op=ALU.mult)
                        else:
                            self.op('vector', 'tensor_tensor', reads=[pb, bcs], writes=[bwi], out=wi[:, c0:c0 + n], in0=x, in1=cs[:, c0:c0 + n], op=ALU.mult)
                            self.op('vector', 'tensor_tensor', reads=[pb, bsn], writes=[btc], out=tc[:, c0:c0 + n], in0=x, in1=sn[:, c0:c0 + n], op=ALU.mult)
                    self.op('gpsimd', 'tensor_tensor', reads=[bwr, btc], writes=[bwr], out=wr[:, 0:T], in0=wr[:, 0:T], in1=tc[:, 0:T], op=ALU.add)
                    self.op('gpsimd', 'tensor_tensor', reads=[bwi, bta], writes=[bwi], out=wi[:, 0:T], in0=wi[:, 0:T], in1=ta[:, 0:T], op=ALU.subtract)
                    Hre, Him = self.Hst[:, hs, sc:sc + 1], self.Hst[:, hs + 1, sc:sc + 1]
                    c1_, s1_ = cs[:, 1:2], sn[:, 1:2]
                    I = lambda name, **kw: self.op('vector', name, reads=[self.b_H, bcs, bsn, bini], writes=[bini], **kw)
                    I('tensor_tensor', out=ini[:, 2:3], in0=Him, in1=s1_, op=ALU.mult)
                    I('scalar_tensor_tensor', out=ini[:, 0:1], in0=Hre, scalar=c1_, in1=ini[:, 2:3], op0=ALU.mult, op1=ALU.subtract)
                    I('tensor_tensor', out=ini[:, 3:4], in0=Him, in1=c1_, op=ALU.mult)
                    I('scalar_tensor_tensor', out=ini[:, 1:2], in0=Hre, scalar=s1_, in1=ini[:, 3:4], op0=ALU.mult, op1=ALU.add)
                    rho = pr[:, 3, sc:sc + 1]
                    self.op('vector', 'tensor_tensor_scan', reads=[bwr, bini, bpr], writes=[bwr], out=wr[:, 0:T], data0=rho.to_broadcast([128, T]), data1=wr[:, 0:T],
                            initial=ini[:, 0:1], op0=ALU.mult, op1=ALU.add)
                    self.op('vector', 'tensor_tensor_scan', reads=[bwi, bini, bpr], writes=[bwi], out=wi[:, 0:T], data0=rho.to_broadcast([128, T]), data1=wi[:, 0:T],
                            initial=ini[:, 1:2], op0=ALU.mult, op1=ALU.add)
                    Gp = lambda name, rd, wrb, **kw: self.op('gpsimd', name, reads=rd, writes=wrb, **kw)
                    Gp('tensor_tensor', [bwr, bcs], [bta], out=ta[:, 0:T], in0=wr[:, 0:T], in1=cs[:, 0:T], op=ALU.mult)
                    Gp('tensor_tensor', [bwi, bsn], [btc], out=tc[:, 0:T], in0=wi[:, 0:T], in1=sn[:, 0:T], op=ALU.mult)
                    self.op('vector', 'tensor_tensor', reads=[bta, btc], writes=[bhr], out=hr[:, 0:T], in0=ta[:, 0:T], in1=tc[:, 0:T], op=ALU.subtract)
                    self.op('vector', 'tensor_tensor', reads=[bta, btc, self.b_H], writes=[self.b_H], out=Hre, in0=ta[:, T - 1:T], in1=tc[:, T - 1:T], op=ALU.subtract)
                    Gp('tensor_tensor', [bwr, bsn, bhr], [bta], out=ta[:, 0:T], in0=wr[:, 0:T], in1=sn[:, 0:T], op=ALU.mult)
                    Gp('tensor_tensor', [bwi, bcs, bhr], [btc], out=tc[:, 0:T], in0=wi[:, 0:T], in1=cs[:, 0:T], op=ALU.mult)
                    self.op('vector', 'tensor_tensor', reads=[bta, btc], writes=[bhi], out=hi[:, 0:T], in0=ta[:, 0:T], in1=tc[:, 0:T], op=ALU.add)
                    self.op('vector', 'tensor_tensor', reads=[bta, btc, self.b_H], writes=[self.b_H], out=Him, in0=ta[:, T - 1:T], in1=tc[:, T - 1:T], op=ALU.add)
                    for c0 in range(0, T, 512):
                        n = min(512, T - c0)
                        key = (oc, t0 + c0)
                        pt, pb = self.psb[5 + (0 if t0 + c0 == 0 else (1 if t0 + c0 == 512 else 2))]
                        self.mm(pt[:, 0:n], Cb[:, sc, 0, :], hr[:, c0:c0 + n], sc % 4 == 0, False, reads=[bCb, bhr], writes=[pb], skip=True)
                        self.mm(pt[:, 0:n], Cb[:, sc, 1, :], hi[:, c0:c0 + n], False, sc % 4 == 3, reads=[bCb, bhi], writes=[pb], skip=True)
                        if sc % 4 == 3:
                            self.op('vector', 'scalar_tensor_tensor', reads=[pb, bu, b0], writes=[byT], out=yf[:, oc, t0 + c0:t0 + c0 + n], in0=uT[:, oc, t0 + c0:t0 + c0 + n],
                                    scalar=dcol[:, oc:oc + 1], in1=pt[:, 0:n], op0=ALU.mult, op1=ALU.add)
                            self.op('gpsimd', 'tensor_copy', reads=[byT], writes=[byT], out=yT[:, oc, t0 + c0:t0 + c0 + n], in_=yf[:, oc, t0 + c0:t0 + c0 + n])
            self.ps_n = 8
            if j == self.NCH - 1:
                P.dma('sync', self.o['nsre'][l], self.Hst[:, 0, :], reads=[self.b_H], writes=[self.b_out])
                P.dma('sync', self.o['nsim'][l], self.Hst[:, 1, :], reads=[self.b_H], writes=[self.b_out])
            if samp:
                P.dma('sync', self.o['nssre'][l], self.Hst[:, 2, :], reads=[self.b_H], writes=[self.b_out])
                P.dma('sync', self.o['nssim'][l], self.Hst[:, 3, :], reads=[self.b_H], writes=[self.b_out])
            sg = self.sb('sg', [128, 512], F32, es)
            bsg = Buf()
            for oc2 in range(4):
                for (c0, c1, gi) in groups:
                    n = c1 - c0
                    pt, pb = self.ps()
                    for kc in range(4):
                        self.mm(pt[:, 0:n], wg[:, kc, oc2 * 128:(oc2 + 1) * 128], yT[:, kc, c0:c1], kc == 0, kc == 3, reads=[bwg, byT], writes=[pb])
                    self.act(sg[:, 0:n], pt[:, 0:n], AF.Sigmoid, reads=[pb, b0], writes=[bsg], bias=dcol[:, 4 + oc2:5 + oc2])
                    self.op('vector', 'tensor_tensor', reads=[bsg, byT], writes=bmix[gi], out=mixedT[:, 8 + oc2, c0:c1], in0=sg[:, 0:n], in1=yf[:, oc2, c0:c1], op=ALU.mult)
            with self.phase() as es2:
                self.rmsnorm(es2, lambda ch, c0, c1: mixedT[:, 8 + ch, c0:c1], lambda ch, c0, c1: mixedT[:, 8 + ch, c0:c1], 4,
                             lambda ch: self.gcol[:, 40 + ch:41 + ch], groups, bmix, bmix, 'c')
    def phase_sgu(self, l, j, samp, guT, gv, gvs, bgu, bgv, mixedT, bmix):
        nc, P = self.nc, self.P
        with self.phase() as es:
            wT = self.sb('sguw', [128, 512], F32, es)
            tri = self.sb('tri', [128, 512], F32, es)
            wTb = self.sb('sguwb', [128, 512], BF16, es)
            brow = self.sb('sgubrow', [1, 512], F32, es)
            browb = self.sb('sgubrowb', [1, 512], BF16, es)
            b1, b2 = Buf(), Buf()
            P.dma('sync', wT[:], self.i['sguwT'][l], writes=[b1])
            P.dma('sync', tri[:], self.i['trilT'], writes=[b1])
            P.dma('sync', brow[:], self.i['sgub'][l], writes=[b1])
            self.op('vector', 'tensor_tensor', reads=[b1], writes=[b2], out=wTb[:], in0=wT[:], in1=tri[:], op=ALU.mult)
            self.op('vector', 'tensor_copy', reads=[b1], writes=[b2], out=browb[:], in_=brow[:])
            for h in range(4):
                for half in range(2):
                    pt, pb = self.ps()
                    for q in range(4):
                        tt = half * 4 + q
                        self.mm(pt[:, q * 128:(q + 1) * 128], gv[:, tt, h * 128:(h + 1) * 128], wTb[:, h * 128:(h + 1) * 128], True, False, reads=[bgv, b2], writes=[pb], skip=True)
                        self.mm(pt[:, q * 128:(q + 1) * 128], self.ones_bf[0:1, :], browb[0:1, h * 128:(h + 1) * 128], False, True, reads=[b2], writes=[pb], skip=True)
                    c0 = half * 512
                    self.op('vector', 'tensor_tensor', reads=[pb, bgu], writes=bmix[half], out=mixedT[:, 12 + h, c0:c0 + 512], in0=pt[:, :], in1=guT[:, h, c0:c0 + 512], op=ALU.mult)
                if samp:
                    pt, pb = self.ps()
                    self.mm(pt[:, 0:8], gvs[0:8, h * 128:(h + 1) * 128], wTb[0:8, h * 128:h * 128 + 8], True, False, reads=[bgv, b2], writes=[pb], skip=True)
                    self.mm(pt[:, 0:8], self.ones_bf[0:1, :], browb[0:1, h * 128:h * 128 + 8], False, True, reads=[b2], writes=[pb], skip=True)
                    self.op('vector', 'tensor_tensor', reads=[pb, bgu], writes=bmix[2], out=mixedT[:, 12 + h, 1024:1032], in0=pt[:, 0:8], in1=guT[:, h, 1024:1032], op=ALU.mult)
            with self.phase() as es2:
                self.rmsnorm(es2, lambda ch, c0, c1: mixedT[:, 12 + ch, c0:c1], lambda ch, c0, c1: mixedT[:, 12 + ch, c0:c1], 4,
                             lambda ch: self.gcol[:, 44 + ch:45 + ch], self.groups(j), bmix, bmix, 'd')

    def proj_post(self, es, groups, nkc, lhs_fn, rhs_fn, rbufs_fn, wld_fn, nblk, cpb, gcol_off, tag):
        nc, P = self.nc, self.P
        xT = self.xT
        ng = len(groups)
        wid = sum(c1 - c0 for (c0, c1, gi) in groups)
        yT = self.sb('yT' + tag, [128, NKC, wid], BF16, es)
        byT = Buf()
        sq = [self.sb('psq%s%d' % (tag, k), [128, 512], BF16, es) for k in range(3)]
        bsq = [Buf() for _ in range(3)]
        rstd = self.sb('prstd' + tag, [128, wid], F32, es)
        brs = Buf()
        tmp = [self.sb('ptmp%s%d' % (tag, k), [128, 512], F32, es) for k in range(2)]
        btmp = [Buf(), Buf()]
        self.ps_n = 8 - ng
        ssq = {gi: self.psb[8 - ng + i] for i, (c0, c1, gi) in enumerate(groups)}
        offs = {}
        o = 0
        for (c0, c1, gi) in groups:
            offs[gi] = o
            o += c1 - c0
        pend = []
        sqc = 0
        nxt_w = wld_fn(0)
        for blk in range(nblk):
            wbuf, bw = nxt_w
            if blk + 1 < nblk:
                nxt_w = wld_fn(blk + 1)
            for cc in range(cpb):
                ch = blk * cpb + cc
                for (c0, c1, gi) in groups:
                    n = c1 - c0
                    pt, pb = self.ps()
                    for kc in range(nkc):
                        self.mm(pt[:, 0:n], lhs_fn(wbuf, kc, cc), rhs_fn(kc, c0, c1), kc == 0, kc == nkc - 1, reads=[bw.sub[kc]] + rbufs_fn(gi), writes=[pb])
                    for f in pend:
                        f()
                    pend = []
                    k = sqc % 3
                    sqc += 1
                    self.act(yT[:, ch, offs[gi]:offs[gi] + n], pt[:, 0:n], AF.Copy, reads=[pb], writes=[byT])
                    self.act(sq[k][:, 0:n], pt[:, 0:n], AF.Square, reads=[pb], writes=[bsq[k]])
                    st, sp = (ch == 0), (ch == NKC - 1)

                    def f(k=k, n=n, gi=gi, st=st, sp=sp):
                        self.mm(ssq[gi][0][:, 0:n], self.ones_bf[:, :], sq[k][:, 0:n], st, sp, reads=[bsq[k]], writes=[ssq[gi][1]], skip=True)
                    pend.append(f)
        for f in pend:
            f()
        for (c0, c1, gi) in groups:
            n = c1 - c0
            r = rstd[:, offs[gi]:offs[gi] + n]
            self.op('vector', 'tensor_scalar', reads=[ssq[gi][1]], writes=[brs], out=r, in0=ssq[gi][0][:, 0:n], scalar1=1.0 / D_MODEL, scalar2=EPS, op0=ALU.mult, op1=ALU.add)
            self.act(r, r, AF.Sqrt, reads=[brs], writes=[brs])
            self.op('vector', 'reciprocal', reads=[brs], writes=[brs], out=r, in_=r)
            for ch in range(NKC):
                k = ch % 2
                eng = 'vector' if k else 'gpsimd'
                self.op('vector', 'scalar_tensor_tensor', reads=[brs, byT, self.b_gcol], writes=[btmp[k]], out=tmp[k][:, 0:n], in0=yT[:, ch, offs[gi]:offs[gi] + n],
                        scalar=self.gcol[:, gcol_off + ch:gcol_off + ch + 1], in1=r, op0=ALU.mult, op1=ALU.mult)
                self.op(eng, 'tensor_tensor', reads=[btmp[k], self.bx[gi]], writes=[self.bx[gi]], out=xT[:, ch, c0:c1], in0=xT[:, ch, c0:c1], in1=tmp[k][:, 0:n], op=ALU.add)
        self.ps_n = 8

    def phase_out(self, l, j, groups, mixedT, bmix):
        with self.phase() as es:
            self.cur_stg = (self.sb('stgO', [128, 10, 512], F32, es), [Buf() for _ in range(10)], 10)
            wbuf = [self.sb('wO%d' % k, [128, NKC, 512], BF16, es) for k in range(2)]
            bw = [WB(NKC), WB(NKC)]
            wsrc = self.i['w_out'][l]

            def wld(blk):
                k = blk % 2
                self.wload(wbuf[k][:], bw[k], wsrc[:, blk * 512:(blk + 1) * 512].rearrange("(kc p) n -> p kc n", p=128))
                return wbuf[k], bw[k]
            self.proj_post(es, groups, NKC, lambda w, kc, cc: w[:, kc, cc * 128:(cc + 1) * 128], lambda kc, c0, c1: mixedT[:, kc, c0:c1],
                           lambda gi: bmix[gi], wld, 4, 4, 16, 'o')

    def phase_ffn(self, l, j, groups):
        nc, P = self.nc, self.P
        xT = self.xT
        halves = [[g for g in groups if g[2] in (0, 2)], [g for g in groups if g[2] == 1]]
        for hg in halves:
            with self.phase() as es:
                wid = sum(c1 - c0 for (c0, c1, gi) in hg)
                offs = {}
                o = 0
                for (c0, c1, gi) in hg:
                    offs[gi] = o
                    o += c1 - c0
                hT = self.sb('hF', [128, NKC, wid], BF16, es)
                bh = {g[2]: [Buf()] for g in hg}
                with self.phase() as es2:
                    self.rmsnorm(es2, lambda ch, c0, c1: xT[:, ch, c0:c1],
                                 lambda ch, c0, c1: hT[:, ch, offs[0 if c0 == 0 else (2 if c0 == 1024 else 1)] :offs[0 if c0 == 0 else (2 if c0 == 1024 else 1)] + (c1 - c0)],
                                 NKC, lambda ch: self.gcol[:, 48 + ch:49 + ch], hg, {g[2]: [self.bx[g[2]]] for g in hg}, bh, 'f')
                actT = self.sb('actT', [128, NHC, wid], BF16, es)
                bact = Buf()
                with self.phase() as es3:
                    self.cur_stg = (self.sb('stgF', [128, 12, 512], F32, es3), [Buf() for _ in range(12)], 12)
                    wg = [self.sb('wG%d' % k, [128, NKC, 256], BF16, es3) for k in range(2)]
                    wu = [self.sb('wU%d' % k, [128, NKC, 256], BF16, es3) for k in range(2)]
                    bwg, bwu = [WB(NKC), WB(NKC)], [WB(NKC), WB(NKC)]
                    sl = [self.sb('sl%d' % k, [128, 512], F32, es3) for k in range(2)]
                    bsl = [Buf(), Buf()]
                    cnt = 0
                    def _wlG(b_):
                        self.wload(wg[b_ % 2][:], bwg[b_ % 2], self.i['w_gate'][l][:, b_ * 256:(b_ + 1) * 256].rearrange("(kc p) n -> p kc n", p=128))
                        self.wload(wu[b_ % 2][:], bwu[b_ % 2], self.i['w_up'][l][:, b_ * 256:(b_ + 1) * 256].rearrange("(kc p) n -> p kc n", p=128))
                    _wlG(0)
                    for blk in range(22):
                        k = blk % 2
                        if blk + 1 < 22:
                            _wlG(blk + 1)
                        for cc in range(2):
                            hc = blk * 2 + cc
                            for (c0, c1, gi) in hg:
                                n = c1 - c0
                                pg, pbg = self.ps()
                                pu, pbu = self.ps()
                                for kc in range(NKC):
                                    self.mm(pg[:, 0:n], wg[k][:, kc, cc * 128:(cc + 1) * 128], hT[:, kc, offs[gi]:offs[gi] + n], kc == 0, kc == NKC - 1, reads=[bwg[k].sub[kc], bh[gi][0]], writes=[pbg])
                                for kc in range(NKC):
                                    self.mm(pu[:, 0:n], wu[k][:, kc, cc * 128:(cc + 1) * 128], hT[:, kc, offs[gi]:offs[gi] + n], kc == 0, kc == NKC - 1, reads=[bwu[k].sub[kc], bh[gi][0]], writes=[pbu])
                                m = cnt % 2
                                cnt += 1
                                self.act(sl[m][:, 0:n], pg[:, 0:n], AF.Silu, reads=[pbg], writes=[bsl[m]])
                                self.op('vector', 'tensor_tensor', reads=[bsl[m], pbu], writes=[bact], out=actT[:, hc, offs[gi]:offs[gi] + n], in0=sl[m][:, 0:n], in1=pu[:, 0:n], op=ALU.mult)
                self.cur_stg = (self.wstg, self.bstg, 4)
                with self.phase() as es4:
                    self.cur_stg = (self.sb('stgD', [128, 10, 512], F32, es4), [Buf() for _ in range(10)], 10)
                    wd = [self.sb('wD%d' % k, [128, NHC, 128], BF16, es4) for k in range(2)]
                    bwd = [WB(NHC), WB(NHC)]

                    def wld(blk):
                        k = blk % 2
                        self.wload(wd[k][:], bwd[k], self.i['w_down'][l][:, blk * 128:(blk + 1) * 128].rearrange("(kc p) n -> p kc n", p=128))
                        return wd[k], bwd[k]
                    self.proj_post(es4, hg, NHC, lambda w, kc, cc: w[:, kc, :], lambda kc, c0, c1: actT[:, kc, offs[0 if c0 == 0 else (2 if c0 == 1024 else 1)]:offs[0 if c0 == 0 else (2 if c0 == 1024 else 1)] + (c1 - c0)],
                                   lambda gi: [bact], wld, 16, 1, 64, 'd')
                self.cur_stg = (self.wstg, self.bstg, 4)

def _t5_bucket_np(dist):
    max_exact = 16
    df = np.maximum(dist, 1).astype(np.float32)
    large = max_exact + (np.log(df / np.float32(max_exact)) / np.float32(math.log(2048 / max_exact)) * np.float32(16)).astype(np.int32)
    large = np.minimum(large, 31)
    return np.where(dist < max_exact, dist, large)


def _consts():
    c = {}
    c['ident'] = np.eye(128, dtype=np.float32)
    c['jflip'] = np.eye(128, dtype=np.float32)[::-1].copy()
    s = np.arange(128)[:, None]
    t = np.arange(128)[None, :]
    c['trilT'] = np.tile((s <= t).astype(np.float32), (1, 4))
    c['iota'] = np.tile(np.arange(1025, dtype=np.float32)[None, :], (128, 1))
    delta = np.arange(TVL) - 127
    m = ((delta >= 0) & (delta <= 128)).astype(np.int32) + ((delta >= 0) & (delta <= 512) & (delta % 4 == 0)) + \
        ((delta >= 0) & (delta <= 2048) & (delta % 16 == 0))
    c33 = np.zeros((33, TVL), np.float32)
    bk = _t5_bucket_np(np.maximum(delta, 0).astype(np.int32))
    valid = m > 0
    c33[bk[valid], np.nonzero(valid)[0]] = 1.0
    c33[32, :] = np.where(valid, np.log(np.maximum(m, 1)).astype(np.float32), np.float32(-30000.0))
    c['c33'] = c33
    return c


def _colT(a, n):
    return np.ascontiguousarray(a.reshape(n, 128).T)


_NC_CACHE = {}


def run_model(inp, DEPTH, NCH):
    f32 = np.float32
    key = (DEPTH, NCH)
    if key not in _NC_CACHE:
        _NC_CACHE[key] = KB(DEPTH, NCH)
        _NC_CACHE[key].build()
    kb = _NC_CACHE[key]
    L = NCH * TOK
    cst = _consts()
    D = DEPTH
    g = lambda k: np.asarray(inp[k], f32)
    gcols = np.stack([np.concatenate([_colT(g(k)[l], 16) for k in ('g_pre_mix', 'g_post_mix', 'g_mix_out', 'g_pre_ffn', 'g_post_ffn')], 1) for l in range(D)])
    ssmcols = np.stack([np.concatenate([_colT(g('ssm_lam_re')[l].reshape(-1), 16), _colT(g('ssm_lam_im')[l].reshape(-1), 16),
                                        _colT(np.repeat(g('ssm_log_step')[l], 64), 16)], 1) for l in range(D)])
    dcols = np.stack([np.concatenate([_colT(g('ssm_d')[l], 4), _colT(g('ssm_b_glu')[l], 4)], 1) for l in range(D)])
    bbd = {}
    for nm, src in (('bbd_re', g('ssm_b_re')), ('bbd_im', g('ssm_b_im'))):
        a = np.zeros((D, 16, 128, 128), f32)
        for sc in range(16):
            for gl in range(2):
                gg = 2 * sc + gl
                r0 = (gg % 8) * 16
                a[:, sc, r0:r0 + 16, gl * 64:(gl + 1) * 64] = np.transpose(src[:D, gg], (0, 2, 1))
        bbd[nm] = a
    for nm, src in (('cbd_re', g('ssm_c_re')), ('cbd_im', g('ssm_c_im'))):
        a = np.zeros((D, 16, 128, 128), f32)
        for sc in range(16):
            for gl in range(2):
                gg = 2 * sc + gl
                c0 = (gg % 8) * 16
                a[:, sc, gl * 64:(gl + 1) * 64, c0:c0 + 16] = np.transpose(src[:D, gg], (0, 2, 1))
        bbd[nm] = a
    sgug = np.stack([np.tile(g('sgu_g')[l][None, :], (128, 1)) for l in range(D)])
    sguwT = np.stack([np.concatenate([g('sgu_w')[l, h].T for h in range(4)], 1) for l in range(D)])
    sgub = g('sgu_b')[:D].reshape(D, 1, 512)
    relb = np.concatenate([g('rel_bias'), np.ones((1, 8), f32)], 0)
    shared = dict(w_in=g('w_in')[:D], w_out=g('w_out')[:D], w_gate=g('w_gate')[:D], w_up=g('w_up')[:D], w_down=g('w_down')[:D],
                  wglu=g('ssm_w_glu')[:D], gcols=gcols, ssmcols=ssmcols, dcols=dcols, sgug=sgug, sguwT=np.ascontiguousarray(sguwT),
                  sgub=np.ascontiguousarray(sgub), relb=relb, **bbd, **cst)
    nb = inp['x_prompt'].shape[0]
    in_maps = []
    for c in range(8):
        m = dict(shared)
        m['xp'] = np.ascontiguousarray(g('x_prompt')[c % nb, :L])
        m['xs'] = np.ascontiguousarray(g('x_sample')[c])
        m['ck'] = np.ascontiguousarray(g('cache_attn_k')[:D, c].reshape(D, 2048, 1024))
        m['cv'] = np.ascontiguousarray(g('cache_attn_v')[:D, c].reshape(D, 2048, 1024))
        m['sre'] = np.stack([_colT(g('state_ssm_re')[l, c].reshape(-1), 16) for l in range(D)])
        m['sim'] = np.stack([_colT(g('state_ssm_im')[l, c].reshape(-1), 16) for l in range(D)])
        in_maps.append(m)
    res = run_bass_kernel_spmd(kb.nc, in_maps, core_ids=list(range(8)))
    R = res.results
    NK = min(2048, L)
    uncol = lambda a: np.ascontiguousarray(a.T).reshape(32, 64)
    yp = np.stack([R[b]['yp'] for b in range(nb)])
    ys = np.stack([R[c]['ys'] for c in range(8)])
    nk = np.stack([np.stack([R[b]['nk'][l].reshape(NK, 8, 128) for b in range(nb)]) for l in range(D)])
    nv = np.stack([np.stack([R[b]['nv'][l].reshape(NK, 8, 128) for b in range(nb)]) for l in range(D)])
    nre = np.stack([np.stack([uncol(R[b]['nsre'][l]) for b in range(nb)]) for l in range(D)])
    nim = np.stack([np.stack([uncol(R[b]['nsim'][l]) for b in range(nb)]) for l in range(D)])
    nks = np.stack([np.stack([R[c]['nks'][l].reshape(8, 8, 128) for c in range(8)]) for l in range(D)])
    nvs = np.stack([np.stack([R[c]['nvs'][l].reshape(8, 8, 128) for c in range(8)]) for l in range(D)])
    nsre = np.stack([np.stack([uncol(R[c]['nssre'][l]) for c in range(8)]) for l in range(D)])
    nsim = np.stack([np.stack([uncol(R[c]['nssim'][l]) for c in range(8)]) for l in range(D)])
    nsgu = np.stack([np.stack([R[c]['nsgu'][l] for c in range(8)]) for l in range(D)])
    if KDBG:
        global DBG_OUT
        DBG_OUT = [R[c]['dbg'].reshape(128, NKC, TW) for c in range(8)]
    return (yp, ys, nk, nv, nre, nim, nks, nvs, nsre, nsim, nsgu)


def kernel(**inputs):
    return run_model(inputs, 4, 4)
```
